# Optimizing a Trainium2 kernel written in Bass

```python
import jax, jax.numpy as jnp
from jax import lax
import numpy as np

D_MODEL = 1024
BATCH = 16
SEQ = 256
DEPTH = 4
DEC_BATCH = 2
DEC_SEQ = 1024
PAST_LEN = 512

GRID_W = 64
HEAD_DIM = 64
H_RWKV = 4
H_NA = 4
H_GQA = 8
H_GQA_KV = 2
GQA_GROUP = H_GQA // H_GQA_KV
W_RWKV = H_RWKV * HEAD_DIM
W_NA = H_NA * HEAD_DIM
W_GQA = H_GQA * HEAD_DIM
W_GQA_KV = H_GQA_KV * HEAD_DIM
LORA_W = 64
LORA_A = 64
LORA_G = 128
SHORT_CONV = 3
RWKV_IN = 3 * W_RWKV + 2 * LORA_W + 2 * LORA_A + LORA_G
NA_IN = 3 * W_NA
GQA_IN = W_GQA + 2 * W_GQA_KV
D_IN = RWKV_IN + NA_IN + GQA_IN
NA_ROWS = 8
NA_COLS = 16
Q_BLOCK = 128
ROPE_BASE = 10000.0
ROPE_FREQ = HEAD_DIM // 4
D_FF = ((8 * D_MODEL + 3 * 256 - 1) // (3 * 256)) * 256
DEEPNORM_ALPHA = (2 * DEPTH) ** 0.25
DEEPNORM_BETA = (8 * DEPTH) ** -0.25
LN_EPS = 1e-5
RMS_EPS = 1e-6
GN_EPS = 64e-5
NEG_INF = -1e30

kernel_name = 'hybrid_rwkv7_natten_gqa_diffusion_step'


def _layer_norm(x, w, b):
    xf = x.astype(jnp.float32)
    mu = jnp.mean(xf, -1, keepdims=True)
    var = jnp.mean(jnp.square(xf - mu), -1, keepdims=True)
    return ((xf - mu) * lax.rsqrt(var + LN_EPS) * w + b).astype(x.dtype)


def _rms_norm(x, w):
    xf = x.astype(jnp.float32)
    return (xf * lax.rsqrt(jnp.mean(xf * xf, -1, keepdims=True) + RMS_EPS) * w).astype(x.dtype)


def _axial_rope(x):
    t = jnp.arange(x.shape[1])
    inv = ROPE_BASE ** (-jnp.arange(ROPE_FREQ, dtype=jnp.float32) / ROPE_FREQ)

    def rotate(xa, pos):
        ang = pos.astype(jnp.float32)[:, None] * inv
        cos, sin = jnp.cos(ang)[None, :, None, :], jnp.sin(ang)[None, :, None, :]
        x1 = xa[..., :ROPE_FREQ].astype(jnp.float32)
        x2 = xa[..., ROPE_FREQ:].astype(jnp.float32)
        return jnp.concatenate([x1 * cos - x2 * sin, x2 * cos + x1 * sin], -1)

    half = HEAD_DIM // 2
    out = jnp.concatenate([rotate(x[..., :half], t // GRID_W), rotate(x[..., half:], t % GRID_W)], -1)
    return out.astype(x.dtype)


def _blocked_attention(q, k, v):
    b, lq, hk, g, d = q.shape
    qb = jnp.moveaxis(q.reshape(b, lq // Q_BLOCK, Q_BLOCK, hk, g, d), 1, 0)

    def block(qi):
        s = jnp.einsum('bqhgd,bkhd->bhgqk', qi, k).astype(jnp.float32) * d ** -0.5
        p = jax.nn.softmax(s, axis=-1).astype(v.dtype)
        return jnp.einsum('bhgqk,bkhd->bqhgd', p, v)

    o = lax.map(block, qb)
    return jnp.moveaxis(o, 0, 1).reshape(b, lq, hk * g * d)


def _neighbourhood_attention(q, k, v, k_ctx, v_ctx, rpb):
    b, s, h, d = q.shape
    rows = s // GRID_W
    kh = min(NA_ROWS, rows)
    qg = q.reshape(b, rows, GRID_W, h, d)
    kg = k.reshape(b, rows, GRID_W, h, d)
    vg = v.reshape(b, rows, GRID_W, h, d)
    r = jnp.arange(rows)
    row_idx = jnp.clip(r - kh // 2, 0, rows - kh)[:, None] + jnp.arange(kh)[None, :]
    kb = jnp.take(kg, row_idx, axis=1)
    vb = jnp.take(vg, row_idx, axis=1)
    col = jnp.arange(GRID_W)
    c0 = jnp.clip(col - NA_COLS // 2, 0, GRID_W - NA_COLS)
    in_win = (col[None, :] >= c0[:, None]) & (col[None, :] < c0[:, None] + NA_COLS)
    dr = row_idx - r[:, None] + NA_ROWS - 1
    dc = jnp.clip(col[None, :] - col[:, None], 1 - NA_COLS, NA_COLS - 1) + NA_COLS - 1
    bias = rpb[:, dr[:, None, :, None], dc[None, :, None, :]].astype(jnp.float32)
    scale = d ** -0.5
    s_nb = jnp.einsum('brqhd,brjchd->bhrqjc', qg, kb).astype(jnp.float32) * scale + bias
    s_nb = jnp.where(in_win[:, None, :], s_nb, NEG_INF)
    s_ctx = jnp.einsum('brqhd,bkhd->bhrqk', qg, k_ctx).astype(jnp.float32) * scale
    n_nb = kh * GRID_W
    p = jax.nn.softmax(jnp.concatenate([s_nb.reshape(b, h, rows, GRID_W, n_nb), s_ctx], -1), axis=-1)
    p = p.astype(v.dtype)
    o = (jnp.einsum('bhrqjc,brjchd->brqhd', p[..., :n_nb].reshape(b, h, rows, GRID_W, kh, GRID_W), vb)
         + jnp.einsum('bhrqk,bkhd->brqhd', p[..., n_nb:], v_ctx))
    return o.reshape(b, s, h * d)


def _short_conv(x, w):
    n = x.shape[1]
    xp = jnp.pad(x, ((0, 0), (SHORT_CONV // 2, SHORT_CONV // 2), (0, 0)))
    return sum(xp[:, j:j + n] * w[j] for j in range(SHORT_CONV))


def _wkv_scan(s0, r, w, kk, a, k, v, reverse):
    def step(s, inp):
        r_t, w_t, kk_t, a_t, k_t, v_t = inp
        sk = jnp.einsum('bhvk,bhk->bhv', s, kk_t)
        s = (s * w_t[:, :, None, :] - sk[..., None] * (a_t * kk_t)[:, :, None, :]
             + v_t[..., None] * k_t[:, :, None, :])
        return s, jnp.einsum('bhvk,bhk->bhv', s, r_t)

    xs = tuple(jnp.moveaxis(t, 1, 0) for t in (r, w, kk, a, k, v))
    s_fin, o = lax.scan(step, s0, xs, reverse=reverse)
    return s_fin, jnp.moveaxis(o, 0, 1)


def _rwkv_mix(feat, s0_f, s0_b, p):
    conv, w0, w2, a0, a2, g2, k_k, k_a, r_k, lnx_w, lnx_b = p
    b, n, _ = feat.shape
    f = _short_conv(feat, conv).astype(jnp.float32)
    o1, o2, o3 = W_RWKV, 2 * W_RWKV, 3 * W_RWKV
    o4 = o3 + 2 * LORA_W
    o5 = o4 + 2 * LORA_A
    r, k, v = f[..., :o1], f[..., o1:o2], f[..., o2:o3]
    wd = f[..., o3:o4].reshape(b, n, 2, LORA_W)
    ad = f[..., o4:o5].reshape(b, n, 2, LORA_A)
    gd = f[..., o5:]
    log_w = -jax.nn.softplus(-(w0 + jnp.einsum('bndr,drc->bndc', jnp.tanh(wd), w2))) - 0.5
    decay = jnp.exp(-jnp.exp(log_w))
    a = jax.nn.sigmoid(a0 + jnp.einsum('bndr,drc->bndc', ad, a2))
    g = jax.nn.sigmoid(gd) @ g2
    heads = lambda t: t.reshape(t.shape[:-1] + (H_RWKV, HEAD_DIM))
    kk = heads(k * k_k)
    kk = kk / jnp.maximum(jnp.sqrt(jnp.sum(kk * kk, -1, keepdims=True)), 1e-12)
    k_dir = heads(k[:, :, None, :] * (1 + (a - 1) * k_a))
    decay, a = heads(decay), heads(a)
    r_h, v_h = heads(r), heads(v)
    s_f, o_f = _wkv_scan(s0_f.astype(jnp.float32), r_h, decay[:, :, 0], kk, a[:, :, 0], k_dir[:, :, 0], v_h, False)
    s_b, o_b = _wkv_scan(s0_b.astype(jnp.float32), r_h, decay[:, :, 1], kk, a[:, :, 1], k_dir[:, :, 1], v_h, True)
    o = o_f + o_b
    mu = jnp.mean(o, -1, keepdims=True)
    var = jnp.mean(jnp.square(o - mu), -1, keepdims=True)
    o = ((o - mu) * lax.rsqrt(var + GN_EPS)).reshape(b, n, W_RWKV) * lnx_w + lnx_b
    bonus = jnp.sum(r_h * (k_dir[:, :, 0] + k_dir[:, :, 1]) * r_k, -1, keepdims=True) * v_h
    y = (o + bonus.reshape(b, n, W_RWKV)) * g
    return y.astype(feat.dtype), s_f, s_b


def _layer(x, mod, shared, cached):
    (w_in, rwkv_p, rpb, q_norm, k_norm, w_out, ln1_w, ln1_b, w_ffn_in, w_ffn_out, ln2_w, ln2_b) = shared
    shift1, scale1, gate1, shift2, scale2, gate2 = jnp.split(mod, 6, axis=-1)
    b, n = x.shape[:2]
    proj = (x * (1 + scale1) + shift1) @ w_in
    f_rwkv = proj[..., :RWKV_IN]
    f_na = proj[..., RWKV_IN:RWKV_IN + NA_IN].reshape(b, n, 3, H_NA, HEAD_DIM)
    na_q, na_k, na_v = f_na[:, :, 0], f_na[:, :, 1], f_na[:, :, 2]
    f_g = proj[..., RWKV_IN + NA_IN:]
    g_q = _rms_norm(f_g[..., :W_GQA].reshape(b, n, H_GQA, HEAD_DIM), q_norm)
    g_k = _rms_norm(f_g[..., W_GQA:W_GQA + W_GQA_KV].reshape(b, n, H_GQA_KV, HEAD_DIM), k_norm)
    g_v = f_g[..., W_GQA + W_GQA_KV:].reshape(b, n, H_GQA_KV, HEAD_DIM)
    if cached is None:
        s0 = jnp.zeros((b, H_RWKV, HEAD_DIM, HEAD_DIM), jnp.float32)
        o_rwkv, s_f, s_b = _rwkv_mix(f_rwkv, s0, s0, rwkv_p)
        o_na = _blocked_attention(na_q[:, :, :, None], na_k, na_v)
        o_g = _blocked_attention(g_q.reshape(b, n, H_GQA_KV, GQA_GROUP, HEAD_DIM), g_k, g_v)
        ctx_tensors = (jnp.stack([s_f, s_b], 1).astype(x.dtype), na_k, na_v, g_k, g_v)
    else:
        s0_f, s0_b, na_k_ctx, na_v_ctx, g_k_ctx, g_v_ctx = cached
        o_rwkv, _, _ = _rwkv_mix(f_rwkv, s0_f, s0_b, rwkv_p)
        o_na = _neighbourhood_attention(na_q, na_k, na_v, na_k_ctx, na_v_ctx, rpb)
        g_q = _axial_rope(g_q)
        g_k = _axial_rope(g_k)
        keys = jnp.concatenate([g_k_ctx.astype(g_k.dtype), g_k], 1)
        vals = jnp.concatenate([g_v_ctx.astype(g_v.dtype), g_v], 1)
        o_g = _blocked_attention(g_q.reshape(b, n, H_GQA_KV, GQA_GROUP, HEAD_DIM), keys, vals)
        ctx_tensors = None
    mix = jnp.concatenate([o_rwkv, o_na, o_g], -1) @ w_out
    x = _layer_norm(DEEPNORM_ALPHA * x + gate1 * mix, ln1_w, ln1_b)
    gate, up = jnp.split((x * (1 + scale2) + shift2) @ w_ffn_in, 2, axis=-1)
    x = _layer_norm(DEEPNORM_ALPHA * x + gate2 * ((jax.nn.silu(gate) * up) @ w_ffn_out), ln2_w, ln2_b)
    return x, ctx_tensors


def setup_inputs(seed: int = 0) -> dict:
    key = jax.random.key(seed)
    ks = iter(jax.random.split(key, 40))
    nrm = lambda shape, s=1.0: s * jax.random.normal(next(ks), shape, jnp.float32)
    D = D_MODEL
    conv_base = jnp.asarray(np.array([0.25, 0.5, 0.25], np.float32))[None, :, None]
    return {
        'x_prompt': nrm((BATCH, SEQ, D)),
        'x_sample': nrm((DEC_BATCH, DEC_SEQ, D)),
        'state_rwkv': nrm((DEC_BATCH, DEPTH, 2, H_RWKV, HEAD_DIM, HEAD_DIM), 0.5),
        'cache_na_k': nrm((DEC_BATCH, DEPTH, PAST_LEN, H_NA, HEAD_DIM)),
        'cache_na_v': nrm((DEC_BATCH, DEPTH, PAST_LEN, H_NA, HEAD_DIM)),
        'cache_gqa_k': nrm((DEC_BATCH, DEPTH, PAST_LEN, H_GQA_KV, HEAD_DIM)),
        'cache_gqa_v': nrm((DEC_BATCH, DEPTH, PAST_LEN, H_GQA_KV, HEAD_DIM)),
        'c': nrm((DEC_BATCH, D)),
        'c_ctx': nrm((D,)),
        'w_mod': nrm((DEPTH, D, 6 * D), 0.5 * D ** -0.5),
        'b_mod': nrm((DEPTH, 6 * D), 0.02),
        'w_in': nrm((DEPTH, D, D_IN), D ** -0.5),
        'rwkv_conv': conv_base + nrm((DEPTH, SHORT_CONV, RWKV_IN), 0.1),
        'rwkv_w0': nrm((DEPTH, 2, W_RWKV), 0.5),
        'rwkv_w2': nrm((DEPTH, 2, LORA_W, W_RWKV), 0.5 * LORA_W ** -0.5),
        'rwkv_a0': nrm((DEPTH, 2, W_RWKV), 0.5),
        'rwkv_a2': nrm((DEPTH, 2, LORA_A, W_RWKV), 0.5 * LORA_A ** -0.5),
        'rwkv_g2': nrm((DEPTH, LORA_G, W_RWKV), LORA_G ** -0.5),
        'rwkv_k_k': 1.0 + nrm((DEPTH, W_RWKV), 0.1),
        'rwkv_k_a': 1.0 + nrm((DEPTH, W_RWKV), 0.1),
        'rwkv_r_k': nrm((DEPTH, H_RWKV, HEAD_DIM), 0.1),
        'rwkv_lnx_w': 1.0 + nrm((DEPTH, W_RWKV), 0.1),
        'rwkv_lnx_b': nrm((DEPTH, W_RWKV), 0.02),
        'na_rpb': nrm((DEPTH, H_NA, 2 * NA_ROWS - 1, 2 * NA_COLS - 1), 0.2),
        'gqa_q_norm': 1.0 + nrm((DEPTH, HEAD_DIM), 0.1),
        'gqa_k_norm': 1.0 + nrm((DEPTH, HEAD_DIM), 0.1),
        'w_out': nrm((DEPTH, D, D), DEEPNORM_BETA * D ** -0.5),
        'ln1_w': 1.0 + nrm((DEPTH, D), 0.1),
        'ln1_b': nrm((DEPTH, D), 0.02),
        'w_ffn_in': nrm((DEPTH, D, 2 * D_FF), D ** -0.5),
        'w_ffn_out': nrm((DEPTH, D_FF, D), DEEPNORM_BETA * D_FF ** -0.5),
        'ln2_w': 1.0 + nrm((DEPTH, D), 0.1),
        'ln2_b': nrm((DEPTH, D), 0.02),
    }


def reference(x_prompt, x_sample, state_rwkv, cache_na_k, cache_na_v, cache_gqa_k, cache_gqa_v, c, c_ctx,
              w_mod, b_mod, w_in, rwkv_conv, rwkv_w0, rwkv_w2, rwkv_a0, rwkv_a2, rwkv_g2, rwkv_k_k, rwkv_k_a,
              rwkv_r_k, rwkv_lnx_w, rwkv_lnx_b, na_rpb, gqa_q_norm, gqa_k_norm, w_out, ln1_w, ln1_b,
              w_ffn_in, w_ffn_out, ln2_w, ln2_b):
    y_prompt = x_prompt
    y_sample = x_sample
    st_rwkv, st_na_k, st_na_v, st_g_k, st_g_v = [], [], [], [], []
    for l in range(DEPTH):
        rwkv_p = (rwkv_conv[l], rwkv_w0[l], rwkv_w2[l], rwkv_a0[l], rwkv_a2[l], rwkv_g2[l],
                  rwkv_k_k[l], rwkv_k_a[l], rwkv_r_k[l], rwkv_lnx_w[l], rwkv_lnx_b[l])
        shared = (w_in[l], rwkv_p, na_rpb[l], gqa_q_norm[l], gqa_k_norm[l], w_out[l],
                  ln1_w[l], ln1_b[l], w_ffn_in[l], w_ffn_out[l], ln2_w[l], ln2_b[l])
        mod_ctx = jax.nn.silu(c_ctx) @ w_mod[l] + b_mod[l]
        y_prompt, ctx_t = _layer(y_prompt, mod_ctx, shared, None)
        st_rwkv.append(ctx_t[0])
        st_na_k.append(ctx_t[1])
        st_na_v.append(ctx_t[2])
        st_g_k.append(ctx_t[3])
        st_g_v.append(ctx_t[4])
        mod_lat = (jax.nn.silu(c) @ w_mod[l] + b_mod[l])[:, None, :]
        cached = (state_rwkv[:, l, 0], state_rwkv[:, l, 1], cache_na_k[:, l], cache_na_v[:, l],
                  cache_gqa_k[:, l], cache_gqa_v[:, l])
        y_sample, _ = _layer(y_sample, mod_lat, shared, cached)
    new_state_rwkv = jnp.stack(st_rwkv, 1)
    new_cache_na_k = jnp.stack(st_na_k, 1)
    new_cache_na_v = jnp.stack(st_na_v, 1)
    new_cache_gqa_k = jnp.stack(st_g_k, 1)
    new_cache_gqa_v = jnp.stack(st_g_v, 1)
    return (y_prompt, y_sample, new_state_rwkv, new_cache_na_k, new_cache_na_v, new_cache_gqa_k, new_cache_gqa_v)
```

```python
import numpy as np
import concourse.bass as bass
import concourse.mybir as mybir
from concourse.bass_utils import run_bass_kernel_spmd
from contextlib import ExitStack
import types
import os

F32 = mybir.dt.float32
BF16 = mybir.dt.bfloat16
AF = mybir.ActivationFunctionType
ALU = mybir.AluOpType

D = 1024
NL = 4
DIN = 2688
DFF = 2816
PAST = 512
ALPHA = (2 * NL) ** 0.25
LN_EPS = 1e-5
RMS_EPS = 1e-6
GN_EPS = 64e-5
NEG = -30000.0
PPL = 45

ENGS = ("pe", "act", "dve", "pool", "sp")


def _freeze(fn):
    if fn.__closure__ is None:
        return fn
    cells = []
    for c in fn.__closure__:
        try:
            cells.append(types.CellType(c.cell_contents))
        except ValueError:
            cells.append(c)
    return types.FunctionType(fn.__code__, fn.__globals__, fn.__name__, fn.__defaults__, tuple(cells))


class Prog:
    def __init__(self, nc, stack):
        self.nc = nc
        self.stack = stack
        self.q = {e: [] for e in ENGS}
        self.cnt = {e: 0 for e in ENGS}
        self.sem = {e: stack.enter_context(nc.semaphore("s_" + e)) for e in ENGS}
        self.known = {e: {} for e in ENGS}
        self.st = {}
        self.dsem = {}
        self.lane = None

    def sb(self, name, shape, dt=F32):
        return self.stack.enter_context(self.nc.sbuf_tensor(name, list(shape), dt))

    def ps(self, name, shape, dt=F32):
        return self.stack.enter_context(self.nc.psum_tensor(name, list(shape), dt))

    def dma_sem(self, name):
        if name not in self.dsem:
            self.dsem[name] = [self.stack.enter_context(self.nc.semaphore("d_" + name)), 0]
        return self.dsem[name]

    def _need(self, eng, ev, waits):
        if ev[0] == "eng":
            _, f, idx = ev
            if f == eng:
                if eng == "pe":
                    return
                if idx < self.cnt[eng] - 1:
                    return
            if self.known[eng].get(f, 0) >= idx:
                return
            self.known[eng][f] = idx
            waits.append((self.sem[f], idx))
        else:
            _, name, cnt = ev
            k = "d:" + name
            if self.known[eng].get(k, 0) >= cnt:
                return
            self.known[eng][k] = cnt
            waits.append((self.dsem[name][0], cnt))

    def _deps(self, eng, reads, writes):
        waits = []
        for k in reads:
            s = self.st.setdefault(k, [[], []])
            for ev in s[0]:
                self._need(eng, ev, waits)
            if isinstance(k, tuple) and k[0] == "ps":
                for ev in s[1]:
                    if not (ev[0] == "eng" and ev[1] == eng):
                        self._need(eng, ev, waits)
        for k in writes:
            s = self.st.setdefault(k, [[], []])
            for ev in s[0]:
                self._need(eng, ev, waits)
            for ev in s[1]:
                self._need(eng, ev, waits)
        return waits

    def _commit(self, ev, reads, writes):
        for k in reads:
            if k in writes:
                continue
            s = self.st[k]
            s[1] = [r for r in s[1] if not (r[0] == ev[0] and r[1] == ev[1])] + [ev]
        for k in writes:
            self.st[k] = [[ev], []]

    GLOBAL_KEYS = {"CST", "CST2", "PPK", "XMT", "OCT", "IDB", "A2T", "W2T", "G2", "PST", "SIL", "SILF", "QKN", "MODB", "LNB",
                   "TMPF", "TMPB", "ONESB"}

    def _k(self, k):
        if self.lane is None:
            return k
        if isinstance(k, tuple) and k[0] in ("ps", "ws", "X", "modrow", "L"):
            return k
        if isinstance(k, str) and k in self.GLOBAL_KEYS:
            return k
        return ("L", self.lane, k)

    def op(self, eng, fn, reads=(), writes=()):
        fn = _freeze(fn)
        reads = [self._k(k) for k in reads]
        writes = [self._k(k) for k in writes]
        waits = self._deps(eng, reads, writes)
        self.cnt[eng] += 1
        idx = self.cnt[eng]
        self._commit(("eng", eng, idx), reads, writes)
        sem = self.sem[eng]

        def run(e, waits=waits, fn=fn, sem=sem):
            for (s, v) in waits:
                e.wait_ge(s, v)
            fn(e).then_inc(sem, 1)

        self.q[eng].append(run)

    def dma(self, eng, out, in_, sname, reads=(), writes=()):
        reads = [self._k(k) for k in reads]
        writes = [self._k(k) for k in writes]
        k0 = None
        for k in list(writes) + list(reads):
            if not (isinstance(k, tuple) and k[0] == "modrow"):
                k0 = k
                break
        sname = "k_" + str(k0).replace("(", "").replace(")", "").replace(",", "_").replace(" ", "").replace("'", "")
        waits = self._deps(eng, reads, writes)
        d = self.dma_sem(sname)
        d[1] += 16
        self._commit(("dma", sname, d[1]), reads, writes)
        sem = d[0]

        def run(e, waits=waits, out=out, in_=in_, sem=sem):
            for (s, v) in waits:
                e.wait_ge(s, v)
            e.dma_start(out=out, in_=in_).then_inc(sem, 16)

        self.q[eng].append(run)

    def barrier(self):
        for e in ENGS:
            waits = []
            for f in ENGS:
                if f != e and self.cnt[f] > self.known[e].get(f, 0):
                    self.known[e][f] = self.cnt[f]
                    waits.append((self.sem[f], self.cnt[f]))
            for name, (sem, cnt) in self.dsem.items():
                k = "d:" + name
                if cnt > self.known[e].get(k, 0):
                    self.known[e][k] = cnt
                    waits.append((sem, cnt))

            def run(eh, waits=waits):
                for (s, v) in waits:
                    eh.wait_ge(s, v)

            self.q[e].append(run)

    def emit(self):
        nc = self.nc
        with nc.Block() as block:
            @block.tensor
            def _(e):
                for r in self.q["pe"]:
                    r(e)

            @block.scalar
            def _(e):
                for r in self.q["act"]:
                    r(e)

            @block.vector
            def _(e):
                for r in self.q["dve"]:
                    r(e)

            @block.gpsimd
            def _(e):
                for r in self.q["pool"]:
                    r(e)

            @block.sync
            def _(e):
                for r in self.q["sp"]:
                    r(e)


def build(depth=NL, stage=99):
    nc = bass.Bass("TRN2", target_bir_lowering=False)

    def din(name, shape):
        return nc.dram_tensor(name, list(shape), F32, kind="ExternalInput").ap()

    def dout(name, shape):
        return nc.dram_tensor(name, list(shape), F32, kind="ExternalOutput").ap()

    xin = din("xin", [1536, D])
    c2 = din("c2", [128, 16])
    stT = din("stT", [NL, 2, 2, 128, 64])
    cnak = din("cnak", [NL, PAST, 256])
    cnav = din("cnav", [NL, PAST, 256])
    cgk = din("cgk", [NL, PAST, 128])
    cgv = din("cgv", [NL, PAST, 128])
    w_mod = din("w_mod", [NL, D, 6 * D])
    b_mod = din("b_mod", [NL, 6 * D])
    w_in = din("w_in", [NL, D, DIN])
    w_out = din("w_out", [NL, D, D])
    w_fi = din("w_ffn_in", [NL, D, 2 * DFF])
    w_fo = din("w_ffn_out", [NL, DFF, D])
    pp = din("pp", [128, NL * PPL])
    w2 = din("rwkv_w2", [NL, 128, 256])
    a2 = din("rwkv_a2", [NL, 128, 256])
    g2 = din("rwkv_g2", [NL, 128, 256])
    ln1w = din("ln1_w", [NL, D]); ln1b = din("ln1_b", [NL, D])
    ln2w = din("ln2_w", [NL, D]); ln2b = din("ln2_b", [NL, D])
    qkn = din("qkn", [NL, 128])
    nab = din("nab", [NL, 4, 64, 15 * 64])
    cst = din("cst", [128, 128 * 12])
    cst2 = din("cst2", [128, 1603])

    y = dout("y", [1536, D])
    nst = dout("nst", [NL, 2, 2, 4, 64, 64])
    nak = dout("nak", [NL, 512, 256]); nav = dout("nav", [NL, 512, 256])
    ngk = dout("ngk", [NL, 512, 128]); ngv = dout("ngv", [NL, 512, 128])
    modrow = nc.dram_tensor("modrow", [NL, 2, 6 * D], F32, kind="Internal").ap()

    with ExitStack() as stack:
        P = Prog(nc, stack)
        X = P.sb("X", [128, 12, D])
        XMT = P.sb("XMT", [128, 8, 1024], BF16)
        OCT = P.sb("OCT", [128, 8, 1024], BF16)
        MODB = P.sb("MODB", [128, 2048])
        LNB = P.sb("LNB", [128, 2048])
        WS = P.sb("WS", [128, 4, 8 * 512], BF16)
        AR = P.sb("AR", [128, 14336])
        CST = P.sb("CST", [128, 128 * 12])
        CST2 = P.sb("CST2", [128, 1603])
        PPK = P.sb("PPK", [128, NL * PPL])
        W2T = P.sb("W2T", [128, 1, 256]); A2T = P.sb("A2T", [128, 1, 256]); G2 = P.sb("G2", [128, 1, 256])
        ONESB = P.sb("ONESB", [128, 64], BF16)
        SIL = P.sb("SIL", [128, 16], BF16)
        SILF = P.sb("SILF", [128, 16])
        QKN = P.sb("QKN", [128, 128])
        IDB = P.sb("IDB", [128, 128], BF16)
        TMPF = P.sb("TMPF", [128, 1024])
        TMPB = P.sb("TMPB", [128, 1024], BF16)
        STAT = P.sb("STAT", [128, 64])
        PSB = [P.ps("psb%d" % i, [128, 512]) for i in range(7)]
        PST = P.ps("pst", [128, 1024], BF16)

        IDENT = CST[:, 0:128]
        ONESBD = CST[:, 128:256]
        def MASK(d, j):
            return CST[:, 256 + (d * 5 + j) * 128: 256 + (d * 5 + j + 1) * 128]
        def MASK4(d):
            return CST[:, 256 + d * 5 * 128: 256 + (d * 5 + 4) * 128]
        RMASK = CST2[:, 0:1024]
        ROPE = CST2[:, 1024:1024 + 512].rearrange("p (t c) -> p t c", c=64)
        COLM = CST2[0:64, 1536:1600]

        psrr = [0]
        npsum = [5]
        def psum():
            i = psrr[0] % npsum[0]
            psrr[0] += 1
            return PSB[i], ("ps", i)
        def psacc(i):
            return PSB[5 + i], ("ps", 5 + i)

        P.dma("sp", CST[:], cst[:, :], "c0", writes=["CST"])
        P.dma("sp", CST2[:], cst2[:, :], "c0", writes=["CST2"])
        P.dma("sp", PPK[:], pp[:, :], "c0", writes=["PPK"])
        P.dma("sp", SILF[:], c2[:, :], "c0", writes=["SILF"])
        P.dma("pool", IDB[:], cst[:, 0:128], "c1", writes=["IDB"])
        P.op("pool", lambda e: e.memset(ONESB[:], 1.0), writes=["ONESB"])
        for t in range(12):
            P.dma("sp", X[:, t, :], xin[t * 128:(t + 1) * 128, :], "xin", writes=[("X", t)])
        P.op("act", lambda e: e.activation(out=SIL[:], in_=SILF[:], func=AF.Silu), reads=["SILF"], writes=["SIL"])

        wsrr = [0]
        def load_w(src2d, k0, nk, c0, ncols):
            s = wsrr[0] % 4
            wsrr[0] += 1
            view = WS[:, s, 0:nk * ncols].rearrange("p (k c) -> p k c", c=ncols)
            src = src2d.rearrange("(k p) c -> p k c", p=128)[:, k0:k0 + nk, c0:c0 + ncols]
            P.dma("pool", view, src, "ws%d" % s, writes=[("ws", s)])
            return view, ("ws", s)

        MR = AR[0:2, 0:6144]
        BM = AR[0:2, 6144:12288]
        for l in range(1):
            P.dma("sp", BM, b_mod[l:l + 1, :].partition_broadcast(2)[:, 0, :], "c0", writes=["BM"])
            for nt in range(12):
                wv, wk = load_w(w_mod[l], 0, 8, nt * 512, 512)
                pt, pk = psum()
                for k in range(8):
                    P.op("pe", lambda e, pt=pt, wv=wv, k=k: e.matmul(pt[0:2, :], lhsT=SIL[:, 2 * k:2 * k + 2], rhs=wv[:, k, :], start=(k == 0), stop=(k == 7)),
                         reads=["SIL", wk], writes=[pk])
                addone = 1.0 if nt in (2, 3, 8, 9) else 0.0
                P.op("dve", lambda e, pt=pt, nt=nt, addone=addone: e.scalar_tensor_tensor(out=MR[:, nt * 512:(nt + 1) * 512], in0=pt[0:2, :], scalar=addone, in1=BM[:, nt * 512:(nt + 1) * 512], op0=ALU.add, op1=ALU.add),
                     reads=[pk, "BM"], writes=["MR"])
            P.dma("sp", modrow[l], MR, "mr", reads=["MR"], writes=[("modrow", l)])
        P.barrier()

        PRE = {}

        def mods_gen(l1):
            MRS = MODB[0:2, 1536:2048]
            BMS = LNB[0:2, 1536:2048]
            wv = WS[:, 3, :].rearrange("p (k c) -> p k c", c=512)
            for nt in range(12):
                P.dma("sp", BMS, b_mod[l1:l1 + 1, nt * 512:(nt + 1) * 512].partition_broadcast(2)[:, 0, :], "bms", writes=["BMS"])
                P.dma("pool", wv, w_mod[l1].rearrange("(k p) c -> p k c", p=128)[:, :, nt * 512:(nt + 1) * 512], "ws3", writes=[("ws", 3)])
                yield
                pt, pk = psum()
                for k in range(8):
                    P.op("pe", lambda e, k=k: e.matmul(pt[0:2, :], lhsT=SIL[:, 2 * k:2 * k + 2], rhs=wv[:, k, :], start=(k == 0), stop=(k == 7)), reads=["SIL", ("ws", 3)], writes=[pk])
                yield
                addone = 1.0 if nt in (2, 3, 8, 9) else 0.0
                P.op("dve", lambda e: e.scalar_tensor_tensor(out=MRS, in0=pt[0:2, :], scalar=addone, in1=BMS, op0=ALU.add, op1=ALU.add), reads=[pk, "BMS"], writes=["MRS"])
                yield
                P.dma("sp", modrow[l1, :, nt * 512:(nt + 1) * 512], MRS, "mrs", reads=["MRS"], writes=[("modrow", l1)])
                yield

        def load_mod(l, ci, c0, n, dst):
            P.dma("sp", dst, modrow[l, ci:ci + 1, c0:c0 + n].partition_broadcast(128)[:, 0, :], "md", reads=[("modrow", l)], writes=["MODB"])

        def load_ln(l, wsrc, bsrc):
            P.dma("sp", LNB[:, 0:1024], wsrc[l:l + 1, :].partition_broadcast(128)[:, 0, :], "ln", writes=["LNB"])
            P.dma("sp", LNB[:, 1024:2048], bsrc[l:l + 1, :].partition_broadcast(128)[:, 0, :], "ln", writes=["LNB"])

        def ppc(l, j):
            return PPK[:, l * PPL + j: l * PPL + j + 1]

        def modulate_transpose(tiles, mod_sh, mod_sc):
            for i, tt in enumerate(tiles):
                P.op("dve", lambda e, tt=tt: e.tensor_tensor(out=TMPF[:], in0=X[:, tt, :], in1=mod_sc, op=ALU.mult), reads=[("X", tt), "MODB"], writes=["TMPF"])
                P.op("dve", lambda e: e.tensor_tensor(out=TMPB[:], in0=TMPF[:], in1=mod_sh, op=ALU.add), reads=["TMPF", "MODB"], writes=["TMPB"])
                for k in range(8):
                    P.op("pe", lambda e, k=k: e.transpose(out=PST[:, k * 128:(k + 1) * 128], in_=TMPB[:, k * 128:(k + 1) * 128], identity=IDB[:]), reads=["TMPB", "IDB"], writes=["PST"])
                P.op("act", lambda e, i=i: e.copy(out=XMT[:, :, i * 128:(i + 1) * 128], in_=PST[:].rearrange("p (k t) -> p k t", t=128)), reads=["PST"], writes=["XMT"])

        def layer_norm(tt, l):
            P.op("dve", lambda e: e.bn_stats(out=STAT[:, 0:6], in_=X[:, tt, 0:512]), reads=[("X", tt)], writes=["STAT"])
            P.op("dve", lambda e: e.bn_stats(out=STAT[:, 6:12], in_=X[:, tt, 512:1024]), reads=[("X", tt)], writes=["STAT"])
            P.op("dve", lambda e: e.bn_aggr(out=STAT[:, 12:14], in_=STAT[:, 0:12].rearrange("p (a b) -> p a b", b=6)), reads=["STAT"], writes=["STAT"])
            P.op("act", lambda e: e.activation(out=STAT[:, 14:15], in_=STAT[:, 13:14], func=AF.Sqrt, bias=CST2[:, 1602:1603], scale=1.0), reads=["STAT", "CST2"], writes=["STAT2"])
            P.op("dve", lambda e: e.reciprocal(out=STAT[:, 15:16], in_=STAT[:, 14:15]), reads=["STAT2"], writes=["STAT3"])
            P.op("dve", lambda e: e.tensor_scalar(out=X[:, tt, :], in0=X[:, tt, :], scalar1=STAT[:, 12:13], scalar2=STAT[:, 15:16], op0=ALU.subtract, op1=ALU.mult), reads=["STAT", "STAT3", ("X", tt)], writes=[("X", tt)])
            P.op("dve", lambda e: e.tensor_tensor(out=X[:, tt, :], in0=X[:, tt, :], in1=LNB[:, 0:1024], op=ALU.mult), reads=["LNB", ("X", tt)], writes=[("X", tt)])
            P.op("dve", lambda e: e.tensor_tensor(out=X[:, tt, :], in0=X[:, tt, :], in1=LNB[:, 1024:2048], op=ALU.add), reads=["LNB", ("X", tt)], writes=[("X", tt)])

        def rwkv_phase(l, NT, pass_seqs, bg=None):
            WR = WS[:].rearrange("p s c -> p (s c)")[:, 0:8 * 1152].rearrange("p (k c) -> p k c", c=1152)
            WRK = [("ws", 0), ("ws", 1), ("ws", 2)]
            o = [0]
            def arr(n):
                v = AR[:, o[0]:o[0] + n]
                o[0] += n
                return v
            def arrb16(n):
                v = AR[:, o[0]:o[0] + n // 2].bitcast(BF16)
                o[0] += n // 2
                return v
            canon = {"OB": "E2", "BON": "E1", "E3": "T2"}
            names = ["F6", "FAD", "FSG", "FR", "FK", "FV", "KKN", "A0", "A1", "LD", "CF", "KD", "BD", "E1", "E2", "T1", "T2"]

            def make_lane(li):
                A_ = {n: arr(256) for n in names}
                A = dict(A_)
                for a_, b_ in canon.items():
                    A[a_] = A_[b_]
                AK = lambda n: ("A", canon.get(n, n))
                STG = arr(258)
                OF = arr(1024)
                TOT = arr(4)
                GAM = arr(4)
                EXP = {n: arrb16(512).rearrange("p (c t) -> p c t", t=128) for n in ["QB", "RB", "KB", "BB", "VB"]}
                if li == 0:
                    ub = MODB[:].bitcast(BF16)
                    ub2 = LNB[:].bitcast(BF16)
                    KTBT = ub[:, 0:1024]
                    VT4 = ub[:, 1024:1536].rearrange("p (c t) -> p c t", t=128)
                    gr = [ub[:, 1536 + 512 * i:2048 + 512 * i].rearrange("p (c t) -> p c t", t=128) for i in range(3)]
                    XX = [ub2[:, 1024 * i:1024 * (i + 1)].rearrange("p (c x t) -> p c x t", x=2, t=128) for i in range(2)]
                    TTM = [ub2[:, 2048 + 512 * i:2560 + 512 * i].rearrange("p (c t) -> p c t", t=128) for i in range(2)]
                else:
                    uo = OCT[:, 2:8, :].rearrange("p c t -> p (c t)")
                    KTBT = uo[:, 0:1024]
                    VT4 = uo[:, 1024:1536].rearrange("p (c t) -> p c t", t=128)
                    gr = [uo[:, 1536 + 512 * i:2048 + 512 * i].rearrange("p (c t) -> p c t", t=128) for i in range(3)]
                    XX = [uo[:, 3072 + 1024 * i:3072 + 1024 * (i + 1)].rearrange("p (c x t) -> p c x t", x=2, t=128) for i in range(2)]
                    TTM = [uo[:, 5120 + 512 * i:5632 + 512 * i].rearrange("p (c t) -> p c t", t=128) for i in range(2)]
                GR = {"AKK": gr[0], "ARK": gr[1], "ARB": gr[2]}
                KT4 = KTBT[:, 0:512].rearrange("p (c t) -> p c t", t=128)
                BT4 = KTBT[:, 512:1024].rearrange("p (c t) -> p c t", t=128)
                CH = {n: TMPF[:, li * 384 + i * 128:li * 384 + (i + 1) * 128] for i, n in enumerate(["X1F", "M", "MT"])}
                CHB = {n: TMPB[:, li * 384 + i * 128:li * 384 + (i + 1) * 128] for i, n in enumerate(["X1B", "NU", "MB"])}
                P.lane = li
                for n in EXP:
                    P.op("pool", lambda e, n=n: e.memset(EXP[n], 0.0), writes=[("EXP", n)])
                P.lane = None
                v3 = lambda n: A[n].rearrange("p (c t) -> p c t", t=64)
                EK = [("EXP", n) for n in ("QB", "RB", "KB", "BB", "VB")]

                def proj_conv(fc, name, off, NTs, s0):
                    dst = A[name]
                    lo = max(s0 - 1, 0)
                    hi = min(s0 + 257, NTs)
                    sh = lo - (s0 - 1)
                    n = hi - lo
                    S = STG; SK = "STG"
                    pt, pk = psum()
                    for k in range(8):
                        P.op("pe", lambda e, k=k: e.matmul(pt[:, 0:n], lhsT=WR[:, k, fc * 128:(fc + 1) * 128], rhs=XMT[:, k, off + lo:off + hi], start=(k == 0), stop=(k == 7)),
                             reads=WRK + ["XMT"], writes=[pk])
                    if sh > 0:
                        P.op("pool", lambda e: e.memset(S[:, 0:1], 0.0), writes=[SK])
                    if sh + n < 258:
                        P.op("pool", lambda e: e.memset(S[:, 257:258], 0.0), writes=[SK])
                    P.op("act", lambda e: e.copy(out=S[:, sh:sh + n], in_=pt[:, 0:n]), reads=[pk], writes=[SK])
                    P.op("act", lambda e: e.activation(out=dst, in_=S[:, 1:257], func=AF.Copy, scale=ppc(l, 9 + fc)), reads=[SK, "PPK"], writes=[AK(name)])
                    P.op("dve", lambda e: e.scalar_tensor_tensor(out=dst, in0=S[:, 0:256], scalar=ppc(l, fc), in1=dst, op0=ALU.mult, op1=ALU.add), reads=[SK, "PPK", AK(name)], writes=[AK(name)])
                    P.op("dve", lambda e: e.scalar_tensor_tensor(out=dst, in0=S[:, 2:258], scalar=ppc(l, 18 + fc), in1=dst, op0=ALU.mult, op1=ALU.add), reads=[SK, "PPK", AK(name)], writes=[AK(name)])

                def prep_shared(hp, off, NTs, s0):
                    for fc, name in ((6, "F6"), (7, "FAD"), (8, "FSG"), (hp, "FR"), (2 + hp, "FK"), (4 + hp, "FV")):
                        proj_conv(fc, name, off, NTs, s0)
                        yield
                    P.op("act", lambda e: e.activation(out=A["F6"], in_=A["F6"], func=AF.Tanh), reads=[AK("F6")], writes=[AK("F6")])
                    P.op("act", lambda e: e.activation(out=A["FSG"], in_=A["FSG"], func=AF.Sigmoid), reads=[AK("FSG")], writes=[AK("FSG")])
                    for dd in (0, 1):
                        pt, pk = psum()
                        P.op("pe", lambda e: e.matmul(pt[:, 0:256], lhsT=A2T[64 * dd:64 * dd + 64, 0, hp * 128:(hp + 1) * 128], rhs=A["FAD"][64 * dd:64 * dd + 64, :], start=True, stop=True), reads=["A2T", AK("FAD")], writes=[pk])
                        P.op("act", lambda e: e.activation(out=A["A%d" % dd], in_=pt[:, 0:256], func=AF.Sigmoid, bias=ppc(l, 31 + dd * 2 + hp), scale=1.0), reads=[pk, "PPK"], writes=[AK("A%d" % dd)])
                    P.op("dve", lambda e: e.tensor_scalar(out=A["T1"], in0=A["FK"], scalar1=ppc(l, 35 + hp), scalar2=None, op0=ALU.mult), reads=[AK("FK"), "PPK"], writes=[AK("T1")])
                    P.op("pool", lambda e: e.tensor_tensor(out=A["T2"], in0=A["T1"], in1=A["T1"], op=ALU.mult), reads=[AK("T1")], writes=[AK("T2")])
                    yield
                    pt, pk = psum()
                    P.op("pe", lambda e: e.matmul(pt[:, 0:256], lhsT=ONESBD, rhs=A["T2"], start=True, stop=True), reads=["CST", AK("T2")], writes=[pk])
                    P.op("act", lambda e: e.activation(out=A["T2"], in_=pt[:, 0:256], func=AF.Sqrt, scale=64.0), reads=[pk], writes=[AK("T2")])
                    yield
                    P.op("dve", lambda e: e.tensor_scalar(out=A["T2"], in0=A["T2"], scalar1=1e-12, scalar2=None, op0=ALU.max), reads=[AK("T2")], writes=[AK("T2")])
                    P.op("dve", lambda e: e.reciprocal(out=A["T2"], in_=A["T2"]), reads=[AK("T2")], writes=[AK("T2")])
                    yield
                    P.op("dve", lambda e: e.tensor_tensor(out=A["KKN"], in0=A["T1"], in1=A["T2"], op=ALU.mult), reads=[AK("T1"), AK("T2")], writes=[AK("KKN")])
                    for h in (0, 1):
                        P.op("pool", lambda e, h=h: e.tensor_copy(out=EXP["VB"][64 * h:64 * h + 64, :, 64 * h:64 * h + 64], in_=v3("FV")[64 * h:64 * h + 64]), reads=[AK("FV"), ("EXP", "VB")], writes=[("EXP", "VB")])
                    yield

                def prep_dir(hp, d):
                    pt, pk = psum()
                    P.op("pe", lambda e: e.matmul(pt[:, 0:256], lhsT=W2T[64 * d:64 * d + 64, 0, hp * 128:(hp + 1) * 128], rhs=A["F6"][64 * d:64 * d + 64, :], start=True, stop=True), reads=["W2T", AK("F6")], writes=[pk])
                    P.op("act", lambda e: e.activation(out=A["LD"], in_=pt[:, 0:256], func=AF.Sigmoid, bias=ppc(l, 27 + d * 2 + hp), scale=1.0), reads=[pk, "PPK"], writes=[AK("LD")])
                    Ad = A["A%d" % d]; AdK = AK("A%d" % d)
                    P.op("pool", lambda e: e.tensor_tensor(out=A["BD"], in0=Ad, in1=A["KKN"], op=ALU.mult), reads=[AdK, AK("KKN")], writes=[AK("BD")])
                    yield
                    P.op("dve", lambda e: e.tensor_scalar(out=A["LD"], in0=A["LD"], scalar1=-0.6065306597126334, scalar2=None, op0=ALU.mult), reads=[AK("LD")], writes=[AK("LD")])
                    P.op("dve", lambda e: e.tensor_scalar(out=A["T1"], in0=Ad, scalar1=ppc(l, 37 + hp), scalar2=ppc(l, 37 + hp), op0=ALU.mult, op1=ALU.subtract), reads=[AdK, "PPK"], writes=[AK("T1")])
                    yield
                    P.op("dve", lambda e: e.tensor_tensor_scan(out=A["CF"], data0=RMASK[:, 0:256], data1=A["LD"], initial=0.0, op0=ALU.mult, op1=ALU.add), reads=[AK("LD"), "CST2"], writes=[AK("CF")])
                    P.op("dve", lambda e: e.scalar_tensor_tensor(out=A["KD"], in0=A["T1"], scalar=1.0, in1=A["FK"], op0=ALU.add, op1=ALU.mult), reads=[AK("T1"), AK("FK")], writes=[AK("KD")])
                    yield
                    CF3 = A["CF"].rearrange("p (c t) -> p c t", t=64)
                    P.op("dve", lambda e: e.tensor_copy(out=TOT.rearrange("p (c o) -> p c o", o=1), in_=CF3[:, :, 63:64]), reads=[AK("CF")], writes=["TOT"])
                    yield
                    P.op("act", lambda e: e.activation(out=GAM, in_=TOT, func=AF.Exp), reads=["TOT"], writes=["GAM"])
                    TOTB = TOT.rearrange("p (c o) -> p c o", o=1).to_broadcast([128, 4, 64])
                    if d == 0:
                        P.op("dve", lambda e: e.tensor_tensor(out=A["T2"], in0=A["CF"], in1=A["LD"], op=ALU.subtract), reads=[AK("CF"), AK("LD")], writes=[AK("T2")])
                        P.op("act", lambda e: e.activation(out=A["E2"], in_=A["CF"], func=AF.Exp), reads=[AK("CF")], writes=[AK("E2")])
                        yield
                        P.op("act", lambda e: e.activation(out=A["E1"], in_=A["T2"], func=AF.Exp), reads=[AK("T2")], writes=[AK("E1")])
                        yield
                        P.op("act", lambda e: e.activation(out=A["E3"], in_=A["CF"], func=AF.Exp, scale=-1.0), reads=[AK("CF"), AK("T2")], writes=[AK("E3")])
                    else:
                        P.op("dve", lambda e: e.tensor_tensor(out=v3("T2"), in0=TOTB, in1=v3("CF"), op=ALU.subtract), reads=["TOT", AK("CF")], writes=[AK("T2")])
                        yield
                        P.op("act", lambda e: e.activation(out=A["E1"], in_=A["T2"], func=AF.Exp), reads=[AK("T2")], writes=[AK("E1")])
                        P.op("dve", lambda e: e.tensor_tensor(out=A["CF"], in0=A["T2"], in1=A["LD"], op=ALU.add), reads=[AK("T2"), AK("LD")], writes=[AK("CF")])
                        yield
                        P.op("act", lambda e: e.activation(out=A["E2"], in_=A["CF"], func=AF.Exp), reads=[AK("CF")], writes=[AK("E2")])
                        P.op("act", lambda e: e.activation(out=A["E3"], in_=A["CF"], func=AF.Exp, scale=-1.0), reads=[AK("CF"), AK("T2"), AK("E1")], writes=[AK("E3")])
                    yield
                    i = 0
                    for (n, a, b) in (("QB", "KKN", "E1"), ("RB", "FR", "E2"), ("KB", "KD", "E3"), ("BB", "BD", "E3")):
                        for h in (0, 1):
                            eng = "dve" if i % 2 == 0 else "pool"
                            i += 1
                            P.op(eng, lambda e, n=n, a=a, b=b, h=h: e.tensor_tensor(out=EXP[n][64 * h:64 * h + 64, :, 64 * h:64 * h + 64], in0=v3(a)[64 * h:64 * h + 64], in1=v3(b)[64 * h:64 * h + 64], op=ALU.mult), reads=[AK(a), AK(b), ("EXP", n)], writes=[("EXP", n)])
                        yield

                def units_pre(d):
                    E = lambda n, c: EXP[n][:, c, :]
                    for c in range(4):
                        P.op("pe", lambda e: e.transpose(out=PST[:, c * 128:(c + 1) * 128], in_=E("KB", c), identity=IDB[:]), reads=EK + ["IDB"], writes=["PST"])
                        P.op("pe", lambda e: e.transpose(out=PST[:, (4 + c) * 128:(5 + c) * 128], in_=E("BB", c), identity=IDB[:]), reads=EK + ["IDB"], writes=["PST"])
                    P.op("act", lambda e: e.copy(out=KTBT, in_=PST[:, 0:1024]), reads=["PST"], writes=["KTBT"])
                    for c in range(4):
                        P.op("pe", lambda e: e.transpose(out=PST[:, c * 128:(c + 1) * 128], in_=E("VB", c), identity=IDB[:]), reads=EK + ["IDB"], writes=["PST"])
                    P.op("act", lambda e: e.copy(out=VT4, in_=PST[:, 0:512].rearrange("p (c t) -> p c t", t=128)), reads=["PST"], writes=["VT4"])
                    yield
                    mb = lambda j: MASK(d, j).unsqueeze(1).to_broadcast([128, 4, 128])
                    grams = ((("QB", "BB"), 0, XX[0][:, :, 0, :], ("XX", 0)), (("BB", "QB"), 1, XX[0][:, :, 1, :], ("XX", 0)),
                             (("KB", "QB"), 2, GR["AKK"], "GAKK"), (("KB", "RB"), 3, GR["ARK"], "GARK"), (("BB", "RB"), 3, GR["ARB"], "GARB"))
                    for gi, ((lh, rh), mj, dst, dk) in enumerate(grams):
                        pg, pgk = psum()
                        for c in range(4):
                            P.op("pe", lambda e: e.matmul(pg[:, c * 128:(c + 1) * 128], lhsT=E(lh, c), rhs=E(rh, c), start=True, stop=True), reads=EK, writes=[pgk])
                        pg3 = pg[:, 0:512].rearrange("p (c t) -> p c t", t=128)
                        P.op("dve", lambda e: e.tensor_tensor(out=dst, in0=pg3, in1=mb(mj), op=ALU.mult), reads=[pgk, "CST"], writes=[dk])
                        if gi == 1:
                            P.op("dve", lambda e: e.scalar_tensor_tensor(out=TTM[0], in0=pg3, scalar=-1.0, in1=mb(mj), op0=ALU.mult, op1=ALU.mult), reads=[pgk, "CST"], writes=[("TT", 0)])
                        yield
                    cur = 0
                    for k in range(5):
                        nx = 1 - cur
                        pxs = [psum(), psum()]
                        for c in range(4):
                            px, pxk = pxs[c // 2]
                            cc = c % 2
                            P.op("pe", lambda e: e.matmul(px[:, (2 * cc) * 128:(2 * cc + 1) * 128], lhsT=XX[cur][:, c, 1, :], rhs=XX[cur][:, c, 0, :], start=True, stop=True), reads=[("XX", cur)], writes=[pxk])
                            P.op("pe", lambda e: e.matmul(px[:, (2 * cc + 1) * 128:(2 * cc + 2) * 128], lhsT=XX[cur][:, c, 0, :], rhs=XX[cur][:, c, 1, :], start=True, stop=True), reads=[("XX", cur)], writes=[pxk])
                        for hlf in (0, 1):
                            px, pxk = pxs[hlf]
                            dstv = XX[nx][:, 2 * hlf:2 * hlf + 2, :, :]
                            srcv = px[:, 0:512].rearrange("p (c x t) -> p c x t", x=2, t=128)
                            if hlf == 0:
                                P.op("act", lambda e: e.copy(out=dstv, in_=srcv), reads=[pxk], writes=[("XX", nx, hlf)])
                            else:
                                P.op("act", lambda e: e.copy(out=dstv, in_=srcv), reads=[pxk], writes=[("XX", nx, hlf)])
                        yield
                        pT, pTk = psum()
                        for c in range(4):
                            xk = ("XX", nx, c // 2)
                            P.op("pe", lambda e: e.matmul(pT[:, c * 128:(c + 1) * 128], lhsT=XX[nx][:, c, 0, :], rhs=TTM[cur][:, c, :], start=True, stop=False), reads=[xk, ("TT", cur)], writes=[pTk])
                            P.op("pe", lambda e: e.matmul(pT[:, c * 128:(c + 1) * 128], lhsT=IDB[:], rhs=TTM[cur][:, c, :], start=False, stop=False), reads=["IDB", ("TT", cur)], writes=[pTk])
                            P.op("pe", lambda e: e.matmul(pT[:, c * 128:(c + 1) * 128], lhsT=IDB[:], rhs=XX[nx][:, c, 1, :], start=False, stop=True), reads=["IDB", xk], writes=[pTk])
                        P.op("act", lambda e: e.copy(out=TTM[nx], in_=pT[:, 0:512].rearrange("p (c t) -> p c t", t=128)), reads=[pTk], writes=[("TT", nx)])
                        P.st[P._k(("XX", nx))] = [list(P.st[P._k(("XX", nx, 0))][0]) + list(P.st[P._k(("XX", nx, 1))][0]), []]
                        cur = nx
                        yield
                    return cur

                def chain(d, c, tti, odst, okey, obase):
                    E = lambda n: EXP[n][:, c, :]
                    p3, p3k = psum()
                    P.op("pe", lambda e: e.matmul(p3[:, 0:128], lhsT=E("QB"), rhs=CHB["MB"], start=True, stop=False), reads=EK + ["CMB"], writes=[p3k])
                    P.op("pe", lambda e: e.matmul(p3[:, 0:128], lhsT=GR["AKK"][:, c, :], rhs=VT4[:, c, :], start=False, stop=True), reads=["GAKK", "VT4"], writes=[p3k])
                    P.op("act", lambda e: e.copy(out=CHB["X1B"], in_=p3[:, 0:128]), reads=[p3k], writes=["CX1B"])
                    yield
                    p4, p4k = psum()
                    P.op("pe", lambda e: e.matmul(p4[:, 0:128], lhsT=TTM[tti][:, c, :], rhs=CHB["X1B"], start=True, stop=False), reads=[("TT", tti), "CX1B"], writes=[p4k])
                    P.op("pe", lambda e: e.matmul(p4[:, 0:128], lhsT=IDB[:], rhs=CHB["X1B"], start=False, stop=True), reads=["IDB", "CX1B"], writes=[p4k])
                    P.op("dve", lambda e: e.tensor_scalar(out=CHB["NU"], in0=p4[:, 0:128], scalar1=-1.0, scalar2=None, op0=ALU.mult), reads=[p4k], writes=["CNU"])
                    yield
                    p6, p6k = psum()
                    P.op("pe", lambda e: e.matmul(p6[:, 0:128], lhsT=KT4[:, c, :], rhs=VT4[:, c, :], start=True, stop=False), reads=["KTBT", "VT4"], writes=[p6k])
                    P.op("pe", lambda e: e.matmul(p6[:, 0:128], lhsT=BT4[:, c, :], rhs=CHB["NU"], start=False, stop=True), reads=["KTBT", "CNU"], writes=[p6k])
                    p5, p5k = psum()
                    P.op("pe", lambda e: e.matmul(p5[:, 0:128], lhsT=CHB["MB"], rhs=E("RB"), start=True, stop=False), reads=EK + ["CMB"], writes=[p5k])
                    P.op("pe", lambda e: e.matmul(p5[:, 0:128], lhsT=VT4[:, c, :], rhs=GR["ARK"][:, c, :], start=False, stop=False), reads=["VT4", "GARK"], writes=[p5k])
                    P.op("pe", lambda e: e.matmul(p5[:, 0:128], lhsT=CHB["NU"], rhs=GR["ARB"][:, c, :], start=False, stop=True), reads=["CNU", "GARB"], writes=[p5k])
                    P.op("dve", lambda e: e.tensor_tensor(out=CH["MT"], in0=p6[:, 0:128], in1=CH["M"], op=ALU.add), reads=[p6k, "CM"], writes=["CMT"])
                    yield
                    P.op("act", lambda e: e.activation(out=CHB["MB"], in_=CH["MT"], func=AF.Copy, scale=GAM[:, c:c + 1]), reads=["CMT", "GAM"], writes=["CMB"])
                    P.op("dve", lambda e: e.tensor_scalar(out=CH["M"], in0=CH["MT"], scalar1=GAM[:, c:c + 1], scalar2=None, op0=ALU.mult), reads=["CMT", "GAM"], writes=["CM"])
                    for h in (0, 1):
                        P.op("act", lambda e, h=h: e.copy(out=odst[64 * h:64 * h + 64, obase:obase + 64], in_=p5[64 * h:64 * h + 64, 64 * h:64 * h + 64]), reads=[p5k], writes=[okey])
                    yield

                def finalize(hp, off, s0):
                    OBK = AK("OB")
                    P.op("dve", lambda e: e.tensor_tensor(out=A["OB"], in0=A["OB"], in1=OF[:, s0:s0 + 256], op=ALU.add), reads=[OBK, "OF"], writes=[OBK])
                    pt, pk = psum()
                    P.op("pe", lambda e: e.matmul(pt[:, 0:256], lhsT=ONESBD, rhs=A["OB"], start=True, stop=True), reads=["CST", OBK], writes=[pk])
                    yield
                    P.op("dve", lambda e: e.tensor_tensor(out=A["OB"], in0=A["OB"], in1=pt[:, 0:256], op=ALU.subtract), reads=[pk, OBK], writes=[OBK])
                    P.op("pool", lambda e: e.tensor_tensor(out=A["T1"], in0=A["OB"], in1=A["OB"], op=ALU.mult), reads=[OBK], writes=[AK("T1")])
                    yield
                    pt2, pk2 = psum()
                    P.op("pe", lambda e: e.matmul(pt2[:, 0:256], lhsT=ONESBD, rhs=A["T1"], start=True, stop=True), reads=["CST", AK("T1")], writes=[pk2])
                    P.op("act", lambda e: e.activation(out=A["T2"], in_=pt2[:, 0:256], func=AF.Sqrt, bias=CST2[:, 1601:1602], scale=1.0), reads=[pk2, "CST2"], writes=[AK("T2")])
                    yield
                    P.op("dve", lambda e: e.reciprocal(out=A["T2"], in_=A["T2"]), reads=[AK("T2")], writes=[AK("T2")])
                    P.op("pool", lambda e: e.tensor_tensor(out=A["T1"], in0=A["A0"], in1=A["A1"], op=ALU.add), reads=[AK("A0"), AK("A1"), AK("T1")], writes=[AK("T1")])
                    yield
                    P.op("dve", lambda e: e.tensor_tensor(out=A["OB"], in0=A["OB"], in1=A["T2"], op=ALU.mult), reads=[OBK, AK("T2")], writes=[OBK])
                    P.op("dve", lambda e: e.tensor_scalar(out=A["OB"], in0=A["OB"], scalar1=ppc(l, 41 + hp), scalar2=ppc(l, 43 + hp), op0=ALU.mult, op1=ALU.add), reads=[OBK, "PPK"], writes=[OBK])
                    P.op("dve", lambda e: e.tensor_scalar(out=A["T1"], in0=A["T1"], scalar1=-2.0, scalar2=ppc(l, 37 + hp), op0=ALU.add, op1=ALU.mult), reads=[AK("T1"), "PPK"], writes=[AK("T1")])
                    yield
                    P.op("dve", lambda e: e.scalar_tensor_tensor(out=A["T1"], in0=A["T1"], scalar=2.0, in1=A["FK"], op0=ALU.add, op1=ALU.mult), reads=[AK("T1"), AK("FK")], writes=[AK("T1")])
                    P.op("dve", lambda e: e.scalar_tensor_tensor(out=A["T1"], in0=A["T1"], scalar=ppc(l, 39 + hp), in1=A["FR"], op0=ALU.mult, op1=ALU.mult), reads=[AK("T1"), "PPK", AK("FR")], writes=[AK("T1")])
                    yield
                    pt3, pk3 = psum()
                    P.op("pe", lambda e: e.matmul(pt3[:, 0:256], lhsT=ONESBD, rhs=A["T1"], start=True, stop=True), reads=["CST", AK("T1")], writes=[pk3])
                    P.op("dve", lambda e: e.scalar_tensor_tensor(out=A["BON"], in0=pt3[:, 0:256], scalar=64.0, in1=A["FV"], op0=ALU.mult, op1=ALU.mult), reads=[pk3, AK("FV")], writes=[AK("BON")])
                    yield
                    pt4, pk4 = psum()
                    P.op("pe", lambda e: e.matmul(pt4[:, 0:256], lhsT=G2[:, 0, hp * 128:(hp + 1) * 128], rhs=A["FSG"], start=True, stop=True), reads=["G2", AK("FSG")], writes=[pk4])
                    P.op("dve", lambda e: e.tensor_tensor(out=A["OB"], in0=A["OB"], in1=A["BON"], op=ALU.add), reads=[OBK, AK("BON")], writes=[OBK])
                    yield
                    P.op("dve", lambda e: e.tensor_tensor(out=OCT[:, hp, off + s0:off + s0 + 256], in0=A["OB"], in1=pt4[:, 0:256], op=ALU.mult), reads=[OBK, pk4], writes=[("OCTW", hp)])
                    yield

                def init_state(is_sample, d, hp):
                    P.op("pool", lambda e: e.memset(CH["M"], 0.0), writes=["CM"])
                    if is_sample:
                        for h in (0, 1):
                            P.dma("sp", CH["M"][64 * h:64 * h + 64, 64 * h:64 * h + 64], stT[l, d, hp, 64 * h:64 * h + 64, :], "stin", writes=["CM"])
                    P.op("pool", lambda e: e.tensor_copy(out=CHB["MB"], in_=CH["M"]), reads=["CM"], writes=["CMB"])

                def emit_state(seq_idx, d, hp):
                    pt, pk = psum()
                    P.op("pe", lambda e: e.transpose(out=pt[:, 0:128], in_=CH["M"], identity=IDENT), reads=["CM", "CST"], writes=[pk])
                    P.op("act", lambda e: e.copy(out=CH["MT"], in_=pt[:, 0:128]), reads=[pk], writes=["CMT"])
                    for h in (0, 1):
                        P.dma("sp", nst[l, seq_idx, d, 2 * hp + h], CH["MT"][64 * h:64 * h + 64, 64 * h:64 * h + 64], "ost", reads=["CMT"])

                def job(off, NTs, is_sample, seq_idx, hp):
                    nseg = NTs // 256
                    sweeps = [((0, 1), [0])] if nseg == 1 else [((0,), list(range(nseg))), ((1,), list(reversed(range(nseg))))]
                    for (dirs, segs) in sweeps:
                        if nseg > 1:
                            init_state(is_sample, dirs[0], hp)
                        for sg in segs:
                            s0 = sg * 256
                            yield from prep_shared(hp, off, NTs, s0)
                            for d in dirs:
                                if nseg == 1:
                                    init_state(is_sample, d, hp)
                                yield from prep_dir(hp, d)
                                tti = yield from units_pre(d)
                                for c in ([0, 1, 2, 3] if d == 0 else [3, 2, 1, 0]):
                                    if d == 0:
                                        yield from chain(d, c, tti, OF, "OF", s0 + 64 * c)
                                    else:
                                        yield from chain(d, c, tti, A["OB"], AK("OB"), 64 * c)
                                if nseg == 1 and not is_sample:
                                    emit_state(seq_idx, d, hp)
                                    yield
                                if d == 1:
                                    yield from finalize(hp, off, s0)
                        if nseg > 1 and not is_sample:
                            emit_state(seq_idx, dirs[0], hp)
                            yield
                return job

            npsum[0] = 7
            jobs = [make_lane(0), make_lane(1)]
            assert o[0] <= 14336, o[0]

            def lane_stream(li):
                for (off, NTs, is_sample, seq_idx) in pass_seqs:
                    yield from jobs[li](off, NTs, is_sample, seq_idx, li)

            gens = [(0, lane_stream(0)), (1, lane_stream(1))]
            if bg is not None:
                gens.append((2, bg))
            while gens:
                for (li, g) in list(gens):
                    P.lane = li
                    try:
                        next(g)
                    except StopIteration:
                        gens.remove((li, g))
            P.lane = None
            npsum[0] = 5
            P.barrier()

        def attn_phase(l, tiles, NT, is_sample, pass_seqs):
            NTT = NT // 128
            nrow = NT // 64
            o = [0]
            def arrb(n):
                v = AR[:, o[0]:o[0] + n // 2].bitcast(BF16)
                o[0] += n // 2
                return v
            def arrf(n):
                v = AR[:, o[0]:o[0] + n]
                o[0] += n
                return v
            NKC = 512 if is_sample else 0
            QKT = arrb(4 * NT).rearrange("p (c t) -> p c t", t=NT)
            GQT = arrb(4 * NT).rearrange("p (c t) -> p c t", t=NT)
            GK2 = arrb(2 * (NT + NKC)).rearrange("p (c t) -> p c t", t=NT + NKC)
            VNA = arrb(nrow * 4 * 64).rearrange("p (r h c) -> p r h c", h=4, c=64)
            VG = arrb((NTT + NKC // 128) * 2 * 64).rearrange("p (t h c) -> p t h c", h=2, c=64)
            tok_off = o[0]
            TOK = arrf(1280)
            TK2 = arrf(768)
            RS = arrf(16)
            PT = [arrb(512) for _ in range(3)]
            SBs = [arrf(512), MODB[:, 0:512]]
            sbrr = [0]
            RC = MODB[:, 512:1024]
            if is_sample:
                KCT = arrb(2 * 512).rearrange("p (c t) -> p c t", t=512)
                VCN = arrb(4 * 4 * 64).rearrange("p (t h c) -> p t h c", h=4, c=64)
                BR = arrf(960)
            assert o[0] <= 14336, o[0]
            ptrr = [0]
            def ptile():
                i = ptrr[0] % 3
                ptrr[0] += 1
                return PT[i], ("PT", i)

            P.dma("sp", QKN[:], qkn[l:l + 1, :].partition_broadcast(128)[:, 0, :], "c0", writes=["QKN"])
            if is_sample:
                for t in range(4):
                    P.dma("sp", TOK[:, 0:256], cnak[l, t * 128:(t + 1) * 128, :], "ctx", writes=["TOK"])
                    P.op("pool", lambda e: e.tensor_copy(out=TMPB[:, 0:256], in_=TOK[:, 0:256]), reads=["TOK"], writes=["TMPB"])
                    for cc in range(2):
                        P.op("pe", lambda e, cc=cc: e.transpose(out=PST[:, cc * 128:(cc + 1) * 128], in_=TMPB[:, cc * 128:(cc + 1) * 128], identity=IDB[:]), reads=["TMPB", "IDB"], writes=["PST"])
                    P.op("act", lambda e, t=t: e.copy(out=KCT[:, :, t * 128:(t + 1) * 128], in_=PST[:, 0:256].rearrange("p (c t) -> p c t", t=128)), reads=["PST"], writes=["KCT"])
                    P.dma("sp", TOK[:, 256:512], cnav[l, t * 128:(t + 1) * 128, :], "ctx", writes=["TOK2"])
                    P.op("pool", lambda e, t=t: e.tensor_copy(out=VCN[:, t, :, :], in_=TOK[:, 256:512].rearrange("p (h c) -> p h c", c=64)), reads=["TOK2"], writes=["VCN"])
                    P.dma("sp", TOK[:, 512:640], cgk[l, t * 128:(t + 1) * 128, :], "ctx", writes=["TOK3"])
                    for kv in (0, 1):
                        P.op("pool", lambda e, kv=kv: e.tensor_copy(out=TMPB[:, 256 + 128 * kv:384 + 128 * kv].rearrange("p (r c) -> p r c", c=64), in_=TOK[:, 512 + 64 * kv:576 + 64 * kv].unsqueeze(1).to_broadcast([128, 2, 64])), reads=["TOK3"], writes=["TMPB2"])
                    for kv in (0, 1):
                        P.op("pe", lambda e, kv=kv: e.transpose(out=PST[:, 256 + 128 * kv:384 + 128 * kv], in_=TMPB[:, 256 + 128 * kv:384 + 128 * kv], identity=IDB[:]), reads=["TMPB2", "IDB"], writes=["PST2"])
                    P.op("act", lambda e, t=t: e.copy(out=GK2[:, :, t * 128:(t + 1) * 128], in_=PST[:, 256:512].rearrange("p (c t) -> p c t", t=128)), reads=["PST2"], writes=["GKT"])
                    P.dma("sp", TOK[:, 640:768], cgv[l, t * 128:(t + 1) * 128, :], "ctx", writes=["TOK4"])
                    P.op("pool", lambda e, t=t: e.tensor_copy(out=VG[:, t, :, :], in_=TOK[:, 640:768].rearrange("p (h c) -> p h c", c=64)), reads=["TOK4"], writes=["VG"])
                P.barrier()

            SUB = int(os.environ.get("ATT_SUB", "9"))
            if SUB < 1:
                P.barrier(); return
            wv, wk = load_w(w_in[l], 0, 8, 1152, 512)
            for cc in range(4):
                for g0 in range(0, NT, 512):
                    gn = min(512, NT - g0)
                    pt, pk = psum()
                    for k in range(8):
                        P.op("pe", lambda e, pt=pt, k=k, cc=cc, g0=g0, gn=gn: e.matmul(pt[:, 0:gn], lhsT=wv[:, k, cc * 128:(cc + 1) * 128], rhs=XMT[:, k, g0:g0 + gn], start=(k == 0), stop=(k == 7)), reads=[wk, "XMT"], writes=[pk])
                    P.op("act", lambda e, pt=pt, cc=cc, g0=g0, gn=gn: e.copy(out=QKT[:, cc, g0:g0 + gn], in_=pt[:, 0:gn]), reads=[pk], writes=["QKT"])
            if SUB < 2:
                P.barrier(); return
            wa, wak = load_w(w_in[l], 0, 8, 1408, 512)
            wb, wbk = load_w(w_in[l], 0, 8, 1920, 512)
            wc, wck = load_w(w_in[l], 0, 8, 2432, 256)
            LB = [dict(TOK=TOK, TK2=TK2, RS=RS, TMPB=TMPB),
                  dict(TOK=LNB[:, 0:1280], TK2=LNB[:, 1280:2048], RS=MODB[:, 1024:1040], TMPB=TMPF[:].bitcast(BF16)[:, 0:1024])]

            def tile_job(i, tt, li):
                TOK_ = LB[li]["TOK"]; TK2_ = LB[li]["TK2"]; RS_ = LB[li]["RS"]; TMPB_ = LB[li]["TMPB"]
                kx = lambda k: k + "_%d" % li
                pa, pak = psum()
                for k in range(8):
                    P.op("pe", lambda e, k=k: e.matmul(pa[:, 0:512], lhsT=XMT[:, k, i * 128:(i + 1) * 128], rhs=wa[:, k, :], start=(k == 0), stop=(k == 7)), reads=[wak, "XMT"], writes=[pak])
                P.op("act", lambda e: e.copy(out=TOK_[:, 0:512], in_=pa[:, 0:512]), reads=[pak], writes=[kx("TOK")])
                pb, pbk = psum()
                for k in range(8):
                    P.op("pe", lambda e, k=k: e.matmul(pb[:, 0:512], lhsT=XMT[:, k, i * 128:(i + 1) * 128], rhs=wb[:, k, :], start=(k == 0), stop=(k == 7)), reads=[wbk, "XMT"], writes=[pbk])
                P.op("act", lambda e: e.copy(out=TOK_[:, 512:1024], in_=pb[:, 0:512]), reads=[pbk], writes=[kx("TOK2")])
                pc, pck = psum()
                for k in range(8):
                    P.op("pe", lambda e, k=k: e.matmul(pc[:, 0:256], lhsT=XMT[:, k, i * 128:(i + 1) * 128], rhs=wc[:, k, :], start=(k == 0), stop=(k == 7)), reads=[wck, "XMT"], writes=[pck])
                P.op("act", lambda e: e.copy(out=TOK_[:, 1024:1280], in_=pc[:, 0:256]), reads=[pck], writes=[kx("TOK3")])
                for rr in (0, 1):
                    pv_, pvk = psum()
                    for k in range(8):
                        P.op("pe", lambda e, k=k: e.matmul(pv_[0:64, 0:256], lhsT=XMT[:, k, i * 128 + rr * 64:i * 128 + rr * 64 + 64], rhs=wa[:, k, 256:512], start=(k == 0), stop=(k == 7)), reads=[wak, "XMT"], writes=[pvk])
                    P.op("act", lambda e: e.copy(out=VNA[0:64, 2 * i + rr, :, :], in_=pv_[0:64, 0:256].rearrange("p (h c) -> p h c", c=64)), reads=[pvk], writes=[("VNA", 2 * i + rr)])
                yield
                QK = TOK_[:, 512:1152].rearrange("p (h c) -> p h c", c=64)
                T2v = TK2_[:, 0:640].rearrange("p (h c) -> p h c", c=64)
                TK = [kx("TOK2"), kx("TOK3")]
                P.op("dve", lambda e: e.tensor_tensor(out=T2v, in0=QK, in1=QK, op=ALU.mult), reads=TK, writes=[kx("TK2")])
                P.op("dve", lambda e: e.tensor_reduce(out=RS_[:, 0:10], in_=T2v, axis=mybir.AxisListType.X, op=ALU.add), reads=[kx("TK2")], writes=[kx("RS")])
                yield
                P.op("act", lambda e: e.activation(out=RS_[:, 0:10], in_=RS_[:, 0:10], func=AF.Sqrt, bias=CST2[:, 1600:1601], scale=1.0 / 64.0), reads=[kx("RS"), "CST2"], writes=[kx("RS")])
                yield
                P.op("dve", lambda e: e.reciprocal(out=RS_[:, 0:10], in_=RS_[:, 0:10]), reads=[kx("RS")], writes=[kx("RS")])
                yield
                P.op("dve", lambda e: e.tensor_tensor(out=QK, in0=QK, in1=RS_[:, 0:10].unsqueeze(2).to_broadcast([128, 10, 64]), op=ALU.mult), reads=[kx("RS")] + TK, writes=TK)
                P.op("dve", lambda e: e.tensor_tensor(out=QK[:, 0:8, :], in0=QK[:, 0:8, :], in1=QKN[:, 0:64].unsqueeze(1).to_broadcast([128, 8, 64]), op=ALU.mult), reads=["QKN"] + TK, writes=TK)
                P.op("dve", lambda e: e.tensor_tensor(out=QK[:, 8:10, :], in0=QK[:, 8:10, :], in1=QKN[:, 64:128].unsqueeze(1).to_broadcast([128, 2, 64]), op=ALU.mult), reads=["QKN"] + TK, writes=TK)
                yield
                if not is_sample:
                    r0 = i * 128
                    P.dma("sp", nak[l, r0:r0 + 128, :], TOK_[:, 0:256], "oc", reads=[kx("TOK")])
                    P.dma("sp", nav[l, r0:r0 + 128, :], TOK_[:, 256:512], "oc", reads=[kx("TOK")])
                    P.dma("sp", ngk[l, r0:r0 + 128, :], TOK_[:, 1024:1152], "oc", reads=[kx("TOK3")])
                    P.dma("sp", ngv[l, r0:r0 + 128, :], TOK_[:, 1152:1280], "oc", reads=[kx("TOK3")])
                else:
                    Q4 = TOK_[:, 512:1152].rearrange("p (h a c) -> p h a c", a=4, c=16)
                    T4 = TK2_[:, 0:640].rearrange("p (h a c) -> p h a c", a=4, c=16)
                    rp = ROPE[:, i, :]
                    for ax in (0, 1):
                        cosb = rp[:, 16 * ax:16 * ax + 16].unsqueeze(1).to_broadcast([128, 10, 16])
                        sinb = rp[:, 32 + 16 * ax:48 + 16 * ax].unsqueeze(1).to_broadcast([128, 10, 16])
                        x1 = Q4[:, :, 2 * ax, :]; x2 = Q4[:, :, 2 * ax + 1, :]
                        t1 = T4[:, :, 0, :]; t2 = T4[:, :, 1, :]
                        P.op("dve", lambda e: e.tensor_tensor(out=t1, in0=x1, in1=sinb, op=ALU.mult), reads=TK + ["CST2"], writes=[kx("TK2")])
                        P.op("dve", lambda e: e.tensor_tensor(out=t2, in0=x2, in1=sinb, op=ALU.mult), reads=TK + ["CST2"], writes=[kx("TK2")])
                        yield
                        P.op("dve", lambda e: e.tensor_tensor(out=x1, in0=x1, in1=cosb, op=ALU.mult), reads=TK + ["CST2", kx("TK2")], writes=TK)
                        P.op("dve", lambda e: e.tensor_tensor(out=x2, in0=x2, in1=cosb, op=ALU.mult), reads=TK + ["CST2", kx("TK2")], writes=TK)
                        yield
                        P.op("dve", lambda e: e.tensor_tensor(out=x1, in0=x1, in1=t2, op=ALU.subtract), reads=TK + [kx("TK2")], writes=TK)
                        P.op("dve", lambda e: e.tensor_tensor(out=x2, in0=x2, in1=t1, op=ALU.add), reads=TK + [kx("TK2")], writes=TK)
                        yield
                P.op("pool", lambda e: e.tensor_copy(out=TMPB_[:, 0:512], in_=TOK_[:, 512:1024]), reads=TK, writes=[kx("TMPB")])
                for kv in (0, 1):
                    P.op("pool", lambda e, kv=kv: e.tensor_copy(out=TMPB_[:, 512 + 128 * kv:640 + 128 * kv].rearrange("p (r c) -> p r c", c=64), in_=TOK_[:, 1024 + 64 * kv:1088 + 64 * kv].unsqueeze(1).to_broadcast([128, 2, 64])), reads=TK, writes=[kx("TMPB")])
                P.op("pool", lambda e: e.tensor_copy(out=VG[:, NKC // 128 + i, :, :], in_=TOK_[:, 1152:1280].rearrange("p (h c) -> p h c", c=64)), reads=[kx("TOK3")], writes=[("VG", i)])
                yield
                for cc in range(6):
                    P.op("pe", lambda e, cc=cc: e.transpose(out=PST[:, cc * 128:(cc + 1) * 128], in_=TMPB_[:, cc * 128:(cc + 1) * 128], identity=IDB[:]), reads=[kx("TMPB"), "IDB"], writes=["PST"])
                P.op("act", lambda e: e.copy(out=GQT[:, :, i * 128:(i + 1) * 128], in_=PST[:, 0:512].rearrange("p (c t) -> p c t", t=128)), reads=["PST"], writes=[("GQT", i)])
                P.op("act", lambda e: e.copy(out=GK2[:, :, NKC + i * 128:NKC + (i + 1) * 128], in_=PST[:, 512:768].rearrange("p (c t) -> p c t", t=128)), reads=["PST"], writes=[("GKT", i)])
                yield

            def tl_stream(li):
                for i, tt in enumerate(tiles):
                    if i % 2 == li:
                        yield from tile_job(i, tt, li)
            gens = [tl_stream(0), tl_stream(1)]
            while gens:
                for g in list(gens):
                    try:
                        next(g)
                    except StopIteration:
                        gens.remove(g)
            for base, cnt_ in (("GQT", NTT), ("GKT", NTT), ("VG", NTT), ("VNA", 2 * NTT)):
                evs = list(P.st.get(base, [[], []])[0])
                for i_ in range(cnt_):
                    evs += list(P.st.get((base, i_), [[], []])[0])
                P.st[base] = [evs, []]
            if SUB < 3:
                P.barrier(); return
            PRE["wout"] = (load_w(w_out[l], 0, 8, 0, 512), load_w(w_out[l], 0, 8, 512, 512))
            nvt = NTT + NKC // 128
            VGA = AR[:, tok_off:tok_off + nvt * 128].bitcast(BF16).rearrange("p (t h c) -> p t h c", h=2, c=128)
            tl0 = ["TOK_0", "TOK2_0", "TOK3_0", "TK2_0"]
            P.op("pool", lambda e: e.memset(VGA[:, :, :, 64:128], 1.0), writes=["VGA1"] + tl0)
            P.op("act", lambda e: e.copy(out=VGA[:, :, :, 0:64], in_=VG), reads=["VG"], writes=["VGA"] + tl0)
            def finish_head(acc, acck, acs, acsk, chunk, half, q0, qn):
                P.op("dve", lambda e: e.reciprocal(out=RC[0:64, 0:qn], in_=acs[0:64, 0:qn]), reads=[acsk], writes=["RC"])
                P.op("dve", lambda e: e.tensor_tensor(out=OCT[64 * half:64 * half + 64, chunk, q0:q0 + qn], in0=acc[0:64, 0:qn], in1=RC[0:64, 0:qn], op=ALU.mult), reads=[acck, "RC"], writes=["OCT"])

            def score_block(qT, kT, nk, qn, bias=None):
                ps_, psk = psum()
                P.op("pe", lambda e: e.matmul(ps_[0:nk, 0:qn], lhsT=kT, rhs=qT, start=True, stop=True), reads=["QKT", "GQT", "GKT", "KCT"], writes=[psk])
                pt_, ptk = ptile()
                if bias is None:
                    P.op("act", lambda e: e.activation(out=pt_[0:nk, 0:qn], in_=ps_[0:nk, 0:qn], func=AF.Exp, scale=0.125), reads=[psk], writes=[ptk])
                else:
                    si = sbrr[0] % 2
                    sbrr[0] += 1
                    SB = SBs[si]
                    P.op("dve", lambda e: e.scalar_tensor_tensor(out=SB[0:nk, 0:qn], in0=ps_[0:nk, 0:qn], scalar=0.125, in1=bias, op0=ALU.mult, op1=ALU.add), reads=[psk, "BR"], writes=[("SB", si)])
                    P.op("act", lambda e: e.activation(out=pt_[0:nk, 0:qn], in_=SB[0:nk, 0:qn], func=AF.Exp), reads=[("SB", si)], writes=[ptk])
                return pt_, ptk

            VK = ["VNA", "VG", "VCN", "ONESB"]

            def pv(acc, acck, acs, acsk, vT, nk, pt_, ptk, c0, n, first):
                P.op("pe", lambda e: e.matmul(acc[0:64, c0:c0 + n], lhsT=vT, rhs=pt_[0:nk, 0:n], start=first, stop=False), reads=VK + [ptk], writes=[acck])
                P.op("pe", lambda e: e.matmul(acs[0:64, c0:c0 + n], lhsT=ONESB[0:nk, :], rhs=pt_[0:nk, 0:n], start=first, stop=False), reads=VK + [ptk], writes=[acsk])

            if is_sample:
                qblocks = [(0, 512, 0, 16), (512, 512, 0, 16)]
            else:
                qblocks = [(off, NTs, off // 64, (off + NTs) // 64) for (off, NTs, _s, _i) in pass_seqs]
            def pv_aug(acc, acck, vT, nk, pt_, ptk, c0, n, first):
                P.op("pe", lambda e: e.matmul(acc[:, c0:c0 + n], lhsT=vT, rhs=pt_[0:nk, 0:n], start=first, stop=False), reads=["VGA", "VGA1", ptk], writes=[acck])

            def finish_head_aug(acc, acck, chunk, half, q0, qn):
                P.op("dve", lambda e: e.reciprocal(out=RC[64:128, 0:qn], in_=acc[64:128, 0:qn]), reads=[acck], writes=["RC"])
                P.op("dve", lambda e: e.tensor_tensor(out=OCT[64 * half:64 * half + 64, chunk, q0:q0 + qn], in0=acc[0:64, 0:qn], in1=RC[64:128, 0:qn], op=ALU.mult), reads=[acck, "RC"], writes=["OCT"])

            def run_blocks(blocks, acc, acck, acs, acsk, aug=False):
                sc = {}
                if blocks:
                    b = blocks[0]
                    sc[0] = score_block(b[0], b[1], b[2], b[3], bias=b[4])
                for i, b in enumerate(blocks):
                    if i + 1 < len(blocks):
                        nb = blocks[i + 1]
                        sc[i + 1] = score_block(nb[0], nb[1], nb[2], nb[3], bias=nb[4])
                    pt_, ptk = sc.pop(i)
                    if aug:
                        pv_aug(acc, acck, b[5], b[2], pt_, ptk, b[6], b[3], i == 0)
                    else:
                        pv(acc, acck, acs, acsk, b[5], b[2], pt_, ptk, b[6], b[3], i == 0)

            for h in range(4):
                hb = 64 * (h % 2)
                qch = h // 2
                kch = 2 + h // 2
                if is_sample:
                    P.dma("sp", BR[0:64, :], nab[l, h], "ctx", writes=["BR"])
                    P.op("dve", lambda e: e.tensor_tensor(out=BR[0:64, :].rearrange("p (a c) -> p a c", c=64), in0=BR[0:64, :].rearrange("p (a c) -> p a c", c=64), in1=COLM.unsqueeze(1).to_broadcast([64, 15, 64]), op=ALU.add), reads=["BR", "CST2"], writes=["BR"])
                for (q0, qn, kr0, kr1) in qblocks:
                    acc, acck = psacc(0)
                    acs, acsk = psacc(1)
                    qT = QKT[hb:hb + 64, qch, q0:q0 + qn]
                    blocks = []
                    if is_sample:
                        for t in range(4):
                            blocks.append((qT, KCT[hb:hb + 64, h // 2, t * 128:(t + 1) * 128], 128, qn, None, VCN[:, t, h, :], 0))
                        rows = range(q0 // 64, (q0 + qn) // 64)
                        for j in range(16):
                            att = [r for r in rows if min(max(r - 4, 0), 8) <= j <= min(max(r - 4, 0), 8) + 7]
                            if not att:
                                continue
                            rlo, rhi = att[0], att[-1]
                            nr = rhi - rlo + 1
                            e0 = rlo - j + 7
                            c0 = (rlo - q0 // 64) * 64
                            blocks.append((QKT[hb:hb + 64, qch, rlo * 64:(rhi + 1) * 64], QKT[hb:hb + 64, kch, j * 64:(j + 1) * 64], 64, nr * 64,
                                           BR[0:64, e0 * 64:(e0 + nr) * 64], VNA[0:64, j, h, :], c0))
                    else:
                        for j in range(kr0, kr1):
                            blocks.append((qT, QKT[hb:hb + 64, kch, j * 64:(j + 1) * 64], 64, qn, None, VNA[0:64, j, h, :], 0))
                    run_blocks(blocks, acc, acck, acs, acsk)
                    finish_head(acc, acck, acs, acsk, 2 + h // 2, h % 2, q0, qn)
            for h in range(8):
                hb = 64 * (h % 2)
                kvh = h // 4
                for (q0, qn, kr0, kr1) in qblocks:
                    acc, acck = psacc(0)
                    acs, acsk = psacc(1)
                    qT = GQT[hb:hb + 64, h // 2, q0:q0 + qn]
                    kts = list(range(NKC // 128)) + [NKC // 128 + t for t in range(kr0 // 2, kr1 // 2)]
                    blocks = [(qT, GK2[hb:hb + 64, kvh, t * 128:(t + 1) * 128], 128, qn, None, VGA[:, t, kvh, :], 0) for t in kts]
                    run_blocks(blocks, acc, acck, acs, acsk, aug=True)
                    finish_head_aug(acc, acck, 4 + h // 2, h % 2, q0, qn)
            P.barrier()

        def dense_out(l, tiles, NT, ci):
            load_mod(l, ci, 2048, 1024, MODB[:, 0:1024])
            load_ln(l, ln1w, ln1b)
            if "wout" in PRE:
                (wv0, wk0), (wv1, wk1) = PRE.pop("wout")
            else:
                wv0, wk0 = load_w(w_out[l], 0, 8, 0, 512)
                wv1, wk1 = load_w(w_out[l], 0, 8, 512, 512)
            for i, tt in enumerate(tiles):
                for (wv, wk, n0) in ((wv0, wk0, 0), (wv1, wk1, 512)):
                    pt, pk = psum()
                    for k in range(8):
                        P.op("pe", lambda e, k=k, i=i, pt=pt, wv=wv: e.matmul(pt[:, 0:512], lhsT=OCT[:, k, i * 128:(i + 1) * 128], rhs=wv[:, k, :], start=(k == 0), stop=(k == 7)), reads=[wk, "OCT"], writes=[pk])
                    P.op("dve", lambda e, pt=pt, n0=n0: e.tensor_tensor(out=TMPF[:, n0:n0 + 512], in0=pt[:, 0:512], in1=MODB[:, n0:n0 + 512], op=ALU.mult), reads=[pk, "MODB"], writes=["TMPF"])
                    P.op("dve", lambda e, tt=tt, n0=n0: e.scalar_tensor_tensor(out=X[:, tt, n0:n0 + 512], in0=X[:, tt, n0:n0 + 512], scalar=ALPHA, in1=TMPF[:, n0:n0 + 512], op0=ALU.mult, op1=ALU.add), reads=["TMPF", ("X", tt)], writes=[("X", tt)])
                layer_norm(tt, l)

        def ffn(l, tiles, NT, ci):
            load_mod(l, ci, 3072, 2048, MODB[:, 0:2048])
            pre_g = load_w(w_fi[l], 0, 8, 0, 512)
            pre_u = load_w(w_fi[l], 0, 8, DFF, 512)
            modulate_transpose(tiles, MODB[:, 0:1024], MODB[:, 1024:2048])
            load_mod(l, ci, 5120, 1024, MODB[:, 0:1024])
            load_ln(l, ln2w, ln2b)
            HT = AR[:, 0:11 * NT // 2].bitcast(BF16).rearrange("p (j t) -> p j t", t=NT)
            GS = AR[:, 6000:6512]
            for half in range(2):
                for jb in range(3):
                    j0 = half * 11 + jb * 4
                    nj = min(4, half * 11 + 11 - j0)
                    if half == 0 and jb == 0:
                        (wg, wgk), (wu, wuk) = pre_g, pre_u
                    else:
                        wg, wgk = load_w(w_fi[l], 0, 8, j0 * 128, nj * 128)
                        wu, wuk = load_w(w_fi[l], 0, 8, DFF + j0 * 128, nj * 128)
                    for jj in range(nj):
                        for g0 in range(0, NT, 512):
                            gn = min(512, NT - g0)
                            pg, pgk = psum()
                            for k in range(8):
                                P.op("pe", lambda e, k=k, pg=pg, jj=jj, g0=g0, gn=gn: e.matmul(pg[:, 0:gn], lhsT=wg[:, k, jj * 128:(jj + 1) * 128], rhs=XMT[:, k, g0:g0 + gn], start=(k == 0), stop=(k == 7)), reads=[wgk, "XMT"], writes=[pgk])
                            pu, puk = psum()
                            for k in range(8):
                                P.op("pe", lambda e, k=k, pu=pu, jj=jj, g0=g0, gn=gn: e.matmul(pu[:, 0:gn], lhsT=wu[:, k, jj * 128:(jj + 1) * 128], rhs=XMT[:, k, g0:g0 + gn], start=(k == 0), stop=(k == 7)), reads=[wuk, "XMT"], writes=[puk])
                            P.op("act", lambda e, pg=pg, gn=gn: e.activation(out=GS[:, 0:gn], in_=pg[:, 0:gn], func=AF.Silu), reads=[pgk], writes=["GS"])
                            P.op("dve", lambda e, pu=pu, gn=gn, g0=g0, jj=jj, jb=jb: e.tensor_tensor(out=HT[:, jb * 4 + jj, g0:g0 + gn], in0=GS[:, 0:gn], in1=pu[:, 0:gn], op=ALU.mult), reads=[puk, "GS"], writes=["HT"])
                for n0 in (0, 512):
                    wa_, wak_ = load_w(w_fo[l], half * 11, 8, n0, 512)
                    wb_, wbk_ = load_w(w_fo[l], half * 11 + 8, 3, n0, 512)
                    for i, tt in enumerate(tiles):
                        pt, pk = psum()
                        for j in range(11):
                            wv, wk, jj = (wa_, wak_, j) if j < 8 else (wb_, wbk_, j - 8)
                            P.op("pe", lambda e, j=j, jj=jj, wv=wv, pt=pt, i=i: e.matmul(pt[:, 0:512], lhsT=HT[:, j, i * 128:(i + 1) * 128], rhs=wv[:, jj, :], start=(j == 0), stop=(j == 10)), reads=[wk, "HT"], writes=[pk])
                        P.op("dve", lambda e, pt=pt, n0=n0: e.tensor_tensor(out=TMPF[:, n0:n0 + 512], in0=pt[:, 0:512], in1=MODB[:, n0:n0 + 512], op=ALU.mult), reads=[pk, "MODB"], writes=["TMPF"])
                        if half == 0:
                            P.op("dve", lambda e, tt=tt, n0=n0: e.scalar_tensor_tensor(out=X[:, tt, n0:n0 + 512], in0=X[:, tt, n0:n0 + 512], scalar=ALPHA, in1=TMPF[:, n0:n0 + 512], op0=ALU.mult, op1=ALU.add), reads=["TMPF", ("X", tt)], writes=[("X", tt)])
                        else:
                            P.op("dve", lambda e, tt=tt, n0=n0: e.tensor_tensor(out=X[:, tt, n0:n0 + 512], in0=X[:, tt, n0:n0 + 512], in1=TMPF[:, n0:n0 + 512], op=ALU.add), reads=["TMPF", ("X", tt)], writes=[("X", tt)])
            for tt in tiles:
                layer_norm(tt, l)

        seqs = [([0, 1, 2, 3], 0, False, [(0, 256, False, 0), (256, 256, False, 1)]), (list(range(4, 12)), 1, True, [(0, 1024, True, 0)])]
        if os.environ.get("SEQS"):
            seqs = [seqs[int(c)] for c in os.environ["SEQS"]]
        for l in range(depth):
            P.barrier()
            P.dma("sp", W2T[:, 0, :], w2[l], "c0", writes=["W2T"])
            P.dma("sp", A2T[:, 0, :], a2[l], "c0", writes=["A2T"])
            P.dma("sp", G2[:, 0, :], g2[l], "c0", writes=["G2"])
            for (tiles, ci, is_sample, pass_seqs) in seqs:
                NT = 128 * len(tiles)
                if stage < 1:
                    continue
                load_mod(l, ci, 0, 2048, MODB[:, 0:2048])
                if stage >= 2:
                    WR_ = WS[:].rearrange("p s c -> p (s c)")[:, 0:8 * 1152].rearrange("p (k c) -> p k c", c=1152)
                    P.dma("pool", WR_, w_in[l].rearrange("(k p) c -> p k c", p=128)[:, :, 0:1152], "ws0", writes=[("ws", 0), ("ws", 1), ("ws", 2)])
                modulate_transpose(tiles, MODB[:, 0:1024], MODB[:, 1024:2048])
                P.barrier()
                if stage >= 2:
                    rwkv_phase(l, NT, pass_seqs, bg=(mods_gen(l + 1) if (is_sample and l + 1 < depth) else None))
                if stage >= 3:
                    attn_phase(l, tiles, NT, is_sample, pass_seqs)
                if stage >= 4:
                    dense_out(l, tiles, NT, ci)
                if stage >= 5:
                    ffn(l, tiles, NT, ci)
        for t in range(12):
            P.dma("sp", y[t * 128:(t + 1) * 128, :], X[:, t, :], "yo", reads=[("X", t)])
        P.barrier()
        P.emit()
    return nc


def _consts():
    cst = np.zeros((128, 128 * 12), np.float32)
    cst[:, 0:128] = np.eye(128, dtype=np.float32)
    bd = np.zeros((128, 128), np.float32)
    bd[0:64, 0:64] = 1.0; bd[64:128, 64:128] = 1.0
    cst[:, 128:256] = bd / 64.0
    i = np.arange(64)[:, None]; j = np.arange(64)[None, :]
    SL = (i > j).astype(np.float32); SU = (i < j).astype(np.float32)
    LI = (i >= j).astype(np.float32); UI = (i <= j).astype(np.float32)
    def blk(m):
        o = np.zeros((128, 128), np.float32); o[0:64, 0:64] = m; o[64:, 64:] = m; return o
    per = {0: [SL, SU, SU, UI, UI], 1: [SU, SL, SL, LI, LI]}
    for d in (0, 1):
        for k, m in enumerate(per[d]):
            cst[:, 256 + (d * 5 + k) * 128: 256 + (d * 5 + k + 1) * 128] = blk(m)
    cst2 = np.zeros((128, 1603), np.float32)
    rm = np.ones(1024, np.float32); rm[::64] = 0.0
    cst2[:, 0:1024] = rm[None, :]
    t = np.arange(1024)
    inv = 10000.0 ** (-np.arange(16, dtype=np.float32) / 16.0)
    angr = (t // 64).astype(np.float32)[:, None] * inv[None, :]
    angc = (t % 64).astype(np.float32)[:, None] * inv[None, :]
    tab = np.concatenate([np.cos(angr), np.cos(angc), np.sin(angr), np.sin(angc)], 1).astype(np.float32)
    cst2[:, 1024:1536] = tab.reshape(8, 128, 64).transpose(1, 0, 2).reshape(128, 512)
    cq = np.arange(64)[None, :]; ck = np.arange(64)[:, None]
    c0 = np.clip(cq - 8, 0, 48)
    inwin = (ck >= c0) & (ck < c0 + 16)
    cst2[0:64, 1536:1600] = np.where(inwin, 0.0, NEG).astype(np.float32)
    cst2[:, 1600] = RMS_EPS; cst2[:, 1601] = GN_EPS; cst2[:, 1602] = LN_EPS
    return cst, cst2


_NC_CACHE = {}


def _prep(inp):
    f = lambda k: np.ascontiguousarray(np.asarray(inp[k], dtype=np.float32))
    cst, cst2 = _consts()
    x_prompt = f("x_prompt"); x_sample = f("x_sample")
    conv = f("rwkv_conv"); w0 = f("rwkv_w0"); a0 = f("rwkv_a0")
    kk_ = f("rwkv_k_k"); ka_ = f("rwkv_k_a"); rk_ = f("rwkv_r_k").reshape(NL, 256)
    lw_ = f("rwkv_lnx_w"); lb_ = f("rwkv_lnx_b")
    pp = np.zeros((128, NL * PPL), np.float32)
    for l in range(NL):
        b = l * PPL
        pp[:, b:b + 27] = conv[l].reshape(3, 9, 128).transpose(2, 0, 1).reshape(128, 27)
        pp[:, b + 27:b + 31] = w0[l].reshape(2, 2, 128).transpose(2, 0, 1).reshape(128, 4)
        pp[:, b + 31:b + 35] = a0[l].reshape(2, 2, 128).transpose(2, 0, 1).reshape(128, 4)
        for j, arr in enumerate((kk_, ka_, rk_, lw_, lb_)):
            pp[:, b + 35 + 2 * j:b + 37 + 2 * j] = arr[l].reshape(2, 128).T
    rpb = f("na_rpb")
    ck = np.arange(64)[:, None]; cq = np.arange(64)[None, :]
    dc = np.clip(ck - cq, -15, 15) + 15
    e = np.arange(15)
    nab = rpb[:, :, 14 - e][:, :, :, dc]
    nab = np.ascontiguousarray(nab.transpose(0, 1, 3, 2, 4)).reshape(NL, 4, 64, 15 * 64)
    qkn = np.concatenate([f("gqa_q_norm"), f("gqa_k_norm")], 1)
    c = f("c"); c_ctx = f("c_ctx")
    st = f("state_rwkv")
    shared = {
        "w_mod": f("w_mod"), "b_mod": f("b_mod"), "w_in": f("w_in"), "w_out": f("w_out"),
        "w_ffn_in": f("w_ffn_in"), "w_ffn_out": f("w_ffn_out"), "pp": pp,
        "rwkv_w2": f("rwkv_w2").reshape(NL, 128, 256), "rwkv_a2": f("rwkv_a2").reshape(NL, 128, 256), "rwkv_g2": f("rwkv_g2"),
        "ln1_w": f("ln1_w"), "ln1_b": f("ln1_b"), "ln2_w": f("ln2_w"), "ln2_b": f("ln2_b"),
        "qkn": qkn, "nab": nab, "cst": cst, "cst2": cst2,
    }
    in_maps = []
    for i in range(8):
        b = i // 4
        m = dict(shared)
        m["xin"] = np.concatenate([x_prompt[2 * i], x_prompt[2 * i + 1], x_sample[b]], 0)
        cc = np.stack([c_ctx, c[b]], 1)
        m["c2"] = np.ascontiguousarray(cc.reshape(8, 128, 2).transpose(1, 0, 2).reshape(128, 16))
        s = st[b].transpose(0, 1, 2, 4, 3)
        m["stT"] = np.ascontiguousarray(s.reshape(NL, 2, 2, 128, 64))
        m["cnak"] = np.ascontiguousarray(f("cache_na_k")[b].reshape(NL, PAST, 256))
        m["cnav"] = np.ascontiguousarray(f("cache_na_v")[b].reshape(NL, PAST, 256))
        m["cgk"] = np.ascontiguousarray(f("cache_gqa_k")[b].reshape(NL, PAST, 128))
        m["cgv"] = np.ascontiguousarray(f("cache_gqa_v")[b].reshape(NL, PAST, 128))
        in_maps.append(m)
    return in_maps


def kernel(**inp):
    depth = NL
    if depth not in _NC_CACHE:
        _NC_CACHE[depth] = build(depth)
    nc = _NC_CACHE[depth]
    in_maps = _prep(inp)
    res = run_bass_kernel_spmd(nc, in_maps, core_ids=list(range(8)))
    R = res.results
    y_prompt = np.stack([R[i // 2]["y"][(i % 2) * 256:(i % 2) * 256 + 256] for i in range(16)], 0)
    y_sample = np.stack([R[0]["y"][512:], R[4]["y"][512:]], 0)
    nst = np.stack([R[i // 2]["nst"][:, i % 2] for i in range(16)], 0)
    def cache(name, hh):
        return np.stack([R[i // 2][name][:, (i % 2) * 256:(i % 2) * 256 + 256].reshape(NL, 256, hh, 64) for i in range(16)], 0)
    return (y_prompt.astype(np.float32), y_sample.astype(np.float32), nst.astype(np.float32),
            cache("nak", 4), cache("nav", 4), cache("ngk", 2), cache("ngv", 2))
```

```python
import numpy as np
import concourse.bass as bass
import concourse.mybir as mybir
from concourse.bass_utils import run_bass_kernel_spmd
from contextlib import ExitStack
import types
import os

F32 = mybir.dt.float32
BF16 = mybir.dt.bfloat16
AF = mybir.ActivationFunctionType
ALU = mybir.AluOpType

D = 1024
NL = 4
DIN = 2688
DFF = 2816
PAST = 512
ALPHA = (2 * NL) ** 0.25
LN_EPS = 1e-5
RMS_EPS = 1e-6
GN_EPS = 64e-5
NEG = -30000.0
PPL = 45

ENGS = ("pe", "act", "dve", "pool", "sp")


def _freeze(fn):
    if fn.__closure__ is None:
        return fn
    cells = []
    for c in fn.__closure__:
        try:
            cells.append(types.CellType(c.cell_contents))
        except ValueError:
            cells.append(c)
    return types.FunctionType(fn.__code__, fn.__globals__, fn.__name__, fn.__defaults__, tuple(cells))


class Prog:
    def __init__(self, nc, stack):
        self.nc = nc
        self.stack = stack
        self.q = {e: [] for e in ENGS}
        self.cnt = {e: 0 for e in ENGS}
        self.sem = {e: stack.enter_context(nc.semaphore("s_" + e)) for e in ENGS}
        self.known = {e: {} for e in ENGS}
        self.st = {}
        self.dsem = {}
        self.lane = None

    def sb(self, name, shape, dt=F32):
        return self.stack.enter_context(self.nc.sbuf_tensor(name, list(shape), dt))

    def ps(self, name, shape, dt=F32):
        return self.stack.enter_context(self.nc.psum_tensor(name, list(shape), dt))

    def dma_sem(self, name):
        if name not in self.dsem:
            self.dsem[name] = [self.stack.enter_context(self.nc.semaphore("d_" + name)), 0]
        return self.dsem[name]

    def _need(self, eng, ev, waits):
        if ev[0] == "eng":
            _, f, idx = ev
            if f == eng:
                if eng == "pe":
                    return
                if idx < self.cnt[eng] - 1:
                    return
            if self.known[eng].get(f, 0) >= idx:
                return
            self.known[eng][f] = idx
            waits.append((self.sem[f], idx))
        else:
            _, name, cnt = ev
            k = "d:" + name
            if self.known[eng].get(k, 0) >= cnt:
                return
            self.known[eng][k] = cnt
            waits.append((self.dsem[name][0], cnt))

    def _deps(self, eng, reads, writes):
        waits = []
        for k in reads:
            s = self.st.setdefault(k, [[], []])
            for ev in s[0]:
                self._need(eng, ev, waits)
            if isinstance(k, tuple) and k[0] == "ps":
                for ev in s[1]:
                    if not (ev[0] == "eng" and ev[1] == eng):
                        self._need(eng, ev, waits)
        for k in writes:
            s = self.st.setdefault(k, [[], []])
            for ev in s[0]:
                self._need(eng, ev, waits)
            for ev in s[1]:
                self._need(eng, ev, waits)
        return waits

    def _commit(self, ev, reads, writes):
        for k in reads:
            if k in writes:
                continue
            s = self.st[k]
            s[1] = [r for r in s[1] if not (r[0] == ev[0] and r[1] == ev[1])] + [ev]
        for k in writes:
            self.st[k] = [[ev], []]

    GLOBAL_KEYS = {"CST", "CST2", "PPK", "XMT", "OCT", "IDB", "A2T", "W2T", "G2", "PST", "SIL", "SILF", "QKN", "MODB", "LNB",
                   "TMPF", "TMPB", "ONESB"}

    def _k(self, k):
        if self.lane is None:
            return k
        if isinstance(k, tuple) and k[0] in ("ps", "ws", "X", "modrow", "L"):
            return k
        if isinstance(k, str) and k in self.GLOBAL_KEYS:
            return k
        return ("L", self.lane, k)

    def op(self, eng, fn, reads=(), writes=()):
        fn = _freeze(fn)
        reads = [self._k(k) for k in reads]
        writes = [self._k(k) for k in writes]
        waits = self._deps(eng, reads, writes)
        self.cnt[eng] += 1
        idx = self.cnt[eng]
        self._commit(("eng", eng, idx), reads, writes)
        sem = self.sem[eng]

        def run(e, waits=waits, fn=fn, sem=sem):
            for (s, v) in waits:
                e.wait_ge(s, v)
            fn(e).then_inc(sem, 1)

        self.q[eng].append(run)

    def dma(self, eng, out, in_, sname, reads=(), writes=()):
        reads = [self._k(k) for k in reads]
        writes = [self._k(k) for k in writes]
        k0 = None
        for k in list(writes) + list(reads):
            if not (isinstance(k, tuple) and k[0] == "modrow"):
                k0 = k
                break
        sname = "k_" + str(k0).replace("(", "").replace(")", "").replace(",", "_").replace(" ", "").replace("'", "")
        waits = self._deps(eng, reads, writes)
        d = self.dma_sem(sname)
        d[1] += 16
        self._commit(("dma", sname, d[1]), reads, writes)
        sem = d[0]

        def run(e, waits=waits, out=out, in_=in_, sem=sem):
            for (s, v) in waits:
                e.wait_ge(s, v)
            e.dma_start(out=out, in_=in_).then_inc(sem, 16)

        self.q[eng].append(run)

    def barrier(self):
        for e in ENGS:
            waits = []
            for f in ENGS:
                if f != e and self.cnt[f] > self.known[e].get(f, 0):
                    self.known[e][f] = self.cnt[f]
                    waits.append((self.sem[f], self.cnt[f]))
            for name, (sem, cnt) in self.dsem.items():
                k = "d:" + name
                if cnt > self.known[e].get(k, 0):
                    self.known[e][k] = cnt
                    waits.append((sem, cnt))

            def run(eh, waits=waits):
                for (s, v) in waits:
                    eh.wait_ge(s, v)

            self.q[e].append(run)

    def emit(self):
        nc = self.nc
        with nc.Block() as block:
            @block.tensor
            def _(e):
                for r in self.q["pe"]:
                    r(e)

            @block.scalar
            def _(e):
                for r in self.q["act"]:
                    r(e)

            @block.vector
            def _(e):
                for r in self.q["dve"]:
                    r(e)

            @block.gpsimd
            def _(e):
                for r in self.q["pool"]:
                    r(e)

            @block.sync
            def _(e):
                for r in self.q["sp"]:
                    r(e)


def build(depth=NL, stage=99):
    nc = bass.Bass("TRN2", target_bir_lowering=False)

    def din(name, shape):
        return nc.dram_tensor(name, list(shape), F32, kind="ExternalInput").ap()

    def dout(name, shape):
        return nc.dram_tensor(name, list(shape), F32, kind="ExternalOutput").ap()

    xin = din("xin", [1536, D])
    c2 = din("c2", [128, 16])
    stT = din("stT", [NL, 2, 2, 128, 64])
    cnak = din("cnak", [NL, PAST, 256])
    cnav = din("cnav", [NL, PAST, 256])
    cgk = din("cgk", [NL, PAST, 128])
    cgv = din("cgv", [NL, PAST, 128])
    w_mod = din("w_mod", [NL, D, 6 * D])
    b_mod = din("b_mod", [NL, 6 * D])
    w_in = din("w_in", [NL, D, DIN])
    w_out = din("w_out", [NL, D, D])
    w_fi = din("w_ffn_in", [NL, D, 2 * DFF])
    w_fo = din("w_ffn_out", [NL, DFF, D])
    pp = din("pp", [128, NL * PPL])
    w2 = din("rwkv_w2", [NL, 128, 256])
    a2 = din("rwkv_a2", [NL, 128, 256])
    g2 = din("rwkv_g2", [NL, 128, 256])
    ln1w = din("ln1_w", [NL, D]); ln1b = din("ln1_b", [NL, D])
    ln2w = din("ln2_w", [NL, D]); ln2b = din("ln2_b", [NL, D])
    qkn = din("qkn", [NL, 128])
    nab = din("nab", [NL, 4, 64, 15 * 64])
    cst = din("cst", [128, 128 * 12])
    cst2 = din("cst2", [128, 1603])

    y = dout("y", [1536, D])
    nst = dout("nst", [NL, 2, 2, 4, 64, 64])
    nak = dout("nak", [NL, 512, 256]); nav = dout("nav", [NL, 512, 256])
    ngk = dout("ngk", [NL, 512, 128]); ngv = dout("ngv", [NL, 512, 128])
    modrow = nc.dram_tensor("modrow", [NL, 2, 6 * D], F32, kind="Internal").ap()

    with ExitStack() as stack:
        P = Prog(nc, stack)
        X = P.sb("X", [128, 12, D])
        XMT = P.sb("XMT", [128, 8, 1024], BF16)
        OCT = P.sb("OCT", [128, 8, 1024], BF16)
        MODB = P.sb("MODB", [128, 2048])
        LNB = P.sb("LNB", [128, 2048])
        WS = P.sb("WS", [128, 4, 8 * 512], BF16)
        AR = P.sb("AR", [128, 14336])
        CST = P.sb("CST", [128, 128 * 12])
        CST2 = P.sb("CST2", [128, 1603])
        PPK = P.sb("PPK", [128, NL * PPL])
        W2T = P.sb("W2T", [128, 1, 256]); A2T = P.sb("A2T", [128, 1, 256]); G2 = P.sb("G2", [128, 1, 256])
        ONESB = P.sb("ONESB", [128, 64], BF16)
        SIL = P.sb("SIL", [128, 16], BF16)
        SILF = P.sb("SILF", [128, 16])
        QKN = P.sb("QKN", [128, 128])
        IDB = P.sb("IDB", [128, 128], BF16)
        TMPF = P.sb("TMPF", [128, 1024])
        TMPB = P.sb("TMPB", [128, 1024], BF16)
        STAT = P.sb("STAT", [128, 64])
        PSB = [P.ps("psb%d" % i, [128, 512]) for i in range(7)]
        PST = P.ps("pst", [128, 1024], BF16)

        IDENT = CST[:, 0:128]
        ONESBD = CST[:, 128:256]
        def MASK(d, j):
            return CST[:, 256 + (d * 5 + j) * 128: 256 + (d * 5 + j + 1) * 128]
        def MASK4(d):
            return CST[:, 256 + d * 5 * 128: 256 + (d * 5 + 4) * 128]
        RMASK = CST2[:, 0:1024]
        ROPE = CST2[:, 1024:1024 + 512].rearrange("p (t c) -> p t c", c=64)
        COLM = CST2[0:64, 1536:1600]

        psrr = [0]
        npsum = [5]
        def psum():
            i = psrr[0] % npsum[0]
            psrr[0] += 1
            return PSB[i], ("ps", i)
        def psacc(i):
            return PSB[5 + i], ("ps", 5 + i)

        P.dma("sp", CST[:], cst[:, :], "c0", writes=["CST"])
        P.dma("sp", CST2[:], cst2[:, :], "c0", writes=["CST2"])
        P.dma("sp", PPK[:], pp[:, :], "c0", writes=["PPK"])
        P.dma("sp", SILF[:], c2[:, :], "c0", writes=["SILF"])
        P.dma("pool", IDB[:], cst[:, 0:128], "c1", writes=["IDB"])
        P.op("pool", lambda e: e.memset(ONESB[:], 1.0), writes=["ONESB"])
        for t in range(12):
            P.dma("sp", X[:, t, :], xin[t * 128:(t + 1) * 128, :], "xin", writes=[("X", t)])
        P.op("act", lambda e: e.activation(out=SIL[:], in_=SILF[:], func=AF.Silu), reads=["SILF"], writes=["SIL"])

        wsrr = [0]
        def load_w(src2d, k0, nk, c0, ncols):
            s = wsrr[0] % 4
            wsrr[0] += 1
            view = WS[:, s, 0:nk * ncols].rearrange("p (k c) -> p k c", c=ncols)
            src = src2d.rearrange("(k p) c -> p k c", p=128)[:, k0:k0 + nk, c0:c0 + ncols]
            P.dma("pool", view, src, "ws%d" % s, writes=[("ws", s)])
            return view, ("ws", s)

        MR = AR[0:2, 0:6144]
        BM = AR[0:2, 6144:12288]
        for l in range(1):
            P.dma("sp", BM, b_mod[l:l + 1, :].partition_broadcast(2)[:, 0, :], "c0", writes=["BM"])
            for nt in range(12):
                wv, wk = load_w(w_mod[l], 0, 8, nt * 512, 512)
                pt, pk = psum()
                for k in range(8):
                    P.op("pe", lambda e, pt=pt, wv=wv, k=k: e.matmul(pt[0:2, :], lhsT=SIL[:, 2 * k:2 * k + 2], rhs=wv[:, k, :], start=(k == 0), stop=(k == 7)),
                         reads=["SIL", wk], writes=[pk])
                addone = 1.0 if nt in (2, 3, 8, 9) else 0.0
                P.op("dve", lambda e, pt=pt, nt=nt, addone=addone: e.scalar_tensor_tensor(out=MR[:, nt * 512:(nt + 1) * 512], in0=pt[0:2, :], scalar=addone, in1=BM[:, nt * 512:(nt + 1) * 512], op0=ALU.add, op1=ALU.add),
                     reads=[pk, "BM"], writes=["MR"])
            P.dma("sp", modrow[l], MR, "mr", reads=["MR"], writes=[("modrow", l)])
        P.barrier()

        PRE = {}

        def mods_gen(l1):
            wvs = [WS[:, 3, 0:2048].rearrange("p (k c) -> p k c", c=256), WS[:, 3, 2048:4096].rearrange("p (k c) -> p k c", c=256)]
            MRS = [MODB[0:2, 1536:1792], MODB[0:2, 1792:2048]]
            BMS = [LNB[0:2, 1536:1792], LNB[0:2, 1792:2048]]
            src = w_mod[l1].rearrange("(k p) c -> p k c", p=128)

            def issue(j):
                b = j % 2
                P.dma("sp", BMS[b], b_mod[l1:l1 + 1, j * 256:(j + 1) * 256].partition_broadcast(2)[:, 0, :], "bms", writes=[("BMS", b)])
                P.dma("pool", wvs[b], src[:, :, j * 256:(j + 1) * 256], "ws3", writes=[("wm", b)])

            issue(0)
            for _ in range(6):
                yield
            for j in range(24):
                b = j % 2
                if j + 1 < 24:
                    issue(j + 1)
                for _ in range(4):
                    yield
                pt, pk = psum()
                for k in range(8):
                    P.op("pe", lambda e, k=k: e.matmul(pt[0:2, 0:256], lhsT=SIL[:, 2 * k:2 * k + 2], rhs=wvs[b][:, k, :], start=(k == 0), stop=(k == 7)), reads=["SIL", ("wm", b)], writes=[pk])
                yield
                addone = 1.0 if (j // 2) in (2, 3, 8, 9) else 0.0
                P.op("dve", lambda e: e.scalar_tensor_tensor(out=MRS[b], in0=pt[0:2, 0:256], scalar=addone, in1=BMS[b], op0=ALU.add, op1=ALU.add), reads=[pk, ("BMS", b)], writes=[("MRS", b)])
                yield
                P.dma("sp", modrow[l1, :, j * 256:(j + 1) * 256], MRS[b], "mrs", reads=[("MRS", b)], writes=[("modrow", l1)])
                yield

        def load_mod(l, ci, c0, n, dst):
            P.dma("sp", dst, modrow[l, ci:ci + 1, c0:c0 + n].partition_broadcast(128)[:, 0, :], "md", reads=[("modrow", l)], writes=["MODB"])

        def load_ln(l, wsrc, bsrc):
            P.dma("sp", LNB[:, 0:1024], wsrc[l:l + 1, :].partition_broadcast(128)[:, 0, :], "ln", writes=["LNB"])
            P.dma("sp", LNB[:, 1024:2048], bsrc[l:l + 1, :].partition_broadcast(128)[:, 0, :], "ln", writes=["LNB"])

        def ppc(l, j):
            return PPK[:, l * PPL + j: l * PPL + j + 1]

        def modulate_transpose(tiles, mod_sh, mod_sc):
            for i, tt in enumerate(tiles):
                P.op("dve", lambda e, tt=tt: e.tensor_tensor(out=TMPF[:], in0=X[:, tt, :], in1=mod_sc, op=ALU.mult), reads=[("X", tt), "MODB"], writes=["TMPF"])
                P.op("dve", lambda e: e.tensor_tensor(out=TMPB[:], in0=TMPF[:], in1=mod_sh, op=ALU.add), reads=["TMPF", "MODB"], writes=["TMPB"])
                for k in range(8):
                    P.op("pe", lambda e, k=k: e.transpose(out=PST[:, k * 128:(k + 1) * 128], in_=TMPB[:, k * 128:(k + 1) * 128], identity=IDB[:]), reads=["TMPB", "IDB"], writes=["PST"])
                P.op("act", lambda e, i=i: e.copy(out=XMT[:, :, i * 128:(i + 1) * 128], in_=PST[:].rearrange("p (k t) -> p k t", t=128)), reads=["PST"], writes=["XMT"])

        def layer_norm(tt, l):
            P.op("dve", lambda e: e.bn_stats(out=STAT[:, 0:6], in_=X[:, tt, 0:512]), reads=[("X", tt)], writes=["STAT"])
            P.op("dve", lambda e: e.bn_stats(out=STAT[:, 6:12], in_=X[:, tt, 512:1024]), reads=[("X", tt)], writes=["STAT"])
            P.op("dve", lambda e: e.bn_aggr(out=STAT[:, 12:14], in_=STAT[:, 0:12].rearrange("p (a b) -> p a b", b=6)), reads=["STAT"], writes=["STAT"])
            P.op("act", lambda e: e.activation(out=STAT[:, 14:15], in_=STAT[:, 13:14], func=AF.Sqrt, bias=CST2[:, 1602:1603], scale=1.0), reads=["STAT", "CST2"], writes=["STAT2"])
            P.op("dve", lambda e: e.reciprocal(out=STAT[:, 15:16], in_=STAT[:, 14:15]), reads=["STAT2"], writes=["STAT3"])
            P.op("dve", lambda e: e.tensor_scalar(out=X[:, tt, :], in0=X[:, tt, :], scalar1=STAT[:, 12:13], scalar2=STAT[:, 15:16], op0=ALU.subtract, op1=ALU.mult), reads=["STAT", "STAT3", ("X", tt)], writes=[("X", tt)])
            P.op("dve", lambda e: e.tensor_tensor(out=X[:, tt, :], in0=X[:, tt, :], in1=LNB[:, 0:1024], op=ALU.mult), reads=["LNB", ("X", tt)], writes=[("X", tt)])
            P.op("dve", lambda e: e.tensor_tensor(out=X[:, tt, :], in0=X[:, tt, :], in1=LNB[:, 1024:2048], op=ALU.add), reads=["LNB", ("X", tt)], writes=[("X", tt)])

        def rwkv_phase(l, NT, pass_seqs, bg=None):
            WR = WS[:].rearrange("p s c -> p (s c)")[:, 0:8 * 1152].rearrange("p (k c) -> p k c", c=1152)
            WRK = [("ws", 0), ("ws", 1), ("ws", 2)]
            o = [0]
            def arr(n):
                v = AR[:, o[0]:o[0] + n]
                o[0] += n
                return v
            def arrb16(n):
                v = AR[:, o[0]:o[0] + n // 2].bitcast(BF16)
                o[0] += n // 2
                return v
            canon = {"OB": "E2", "BON": "E1", "E3": "T2"}
            names = ["F6", "FAD", "FSG", "FR", "FK", "FV", "KKN", "A0", "A1", "LD", "CF", "KD", "BD", "E1", "E2", "T1", "T2"]

            def make_lane(li):
                A_ = {n: arr(256) for n in names}
                A = dict(A_)
                for a_, b_ in canon.items():
                    A[a_] = A_[b_]
                AK = lambda n: ("A", canon.get(n, n))
                STG = arr(258)
                OF = arr(1024)
                TOT = arr(4)
                GAM = arr(4)
                EXP = {n: arrb16(512).rearrange("p (c t) -> p c t", t=128) for n in ["QB", "RB", "KB", "BB", "VB"]}
                if li == 0:
                    ub = MODB[:].bitcast(BF16)
                    ub2 = LNB[:].bitcast(BF16)
                    KTBT = ub[:, 0:1024]
                    VT4 = ub[:, 1024:1536].rearrange("p (c t) -> p c t", t=128)
                    gr = [ub[:, 1536 + 512 * i:2048 + 512 * i].rearrange("p (c t) -> p c t", t=128) for i in range(3)]
                    XX = [ub2[:, 1024 * i:1024 * (i + 1)].rearrange("p (c x t) -> p c x t", x=2, t=128) for i in range(2)]
                    TTM = [ub2[:, 2048 + 512 * i:2560 + 512 * i].rearrange("p (c t) -> p c t", t=128) for i in range(2)]
                else:
                    uo = OCT[:, 2:8, :].rearrange("p c t -> p (c t)")
                    KTBT = uo[:, 0:1024]
                    VT4 = uo[:, 1024:1536].rearrange("p (c t) -> p c t", t=128)
                    gr = [uo[:, 1536 + 512 * i:2048 + 512 * i].rearrange("p (c t) -> p c t", t=128) for i in range(3)]
                    XX = [uo[:, 3072 + 1024 * i:3072 + 1024 * (i + 1)].rearrange("p (c x t) -> p c x t", x=2, t=128) for i in range(2)]
                    TTM = [uo[:, 5120 + 512 * i:5632 + 512 * i].rearrange("p (c t) -> p c t", t=128) for i in range(2)]
                GR = {"AKK": gr[0], "ARK": gr[1], "ARB": gr[2]}
                KT4 = KTBT[:, 0:512].rearrange("p (c t) -> p c t", t=128)
                BT4 = KTBT[:, 512:1024].rearrange("p (c t) -> p c t", t=128)
                CH = {n: TMPF[:, li * 384 + i * 128:li * 384 + (i + 1) * 128] for i, n in enumerate(["X1F", "M", "MT"])}
                CHB = {n: TMPB[:, li * 384 + i * 128:li * 384 + (i + 1) * 128] for i, n in enumerate(["X1B", "NU", "MB"])}
                P.lane = li
                for n in EXP:
                    P.op("pool", lambda e, n=n: e.memset(EXP[n], 0.0), writes=[("EXP", n)])
                P.lane = None
                v3 = lambda n: A[n].rearrange("p (c t) -> p c t", t=64)
                EK = [("EXP", n) for n in ("QB", "RB", "KB", "BB", "VB")]

                def proj_conv(fc, name, off, NTs, s0):
                    dst = A[name]
                    lo = max(s0 - 1, 0)
                    hi = min(s0 + 257, NTs)
                    sh = lo - (s0 - 1)
                    n = hi - lo
                    S = STG; SK = "STG"
                    pt, pk = psum()
                    for k in range(8):
                        P.op("pe", lambda e, k=k: e.matmul(pt[:, 0:n], lhsT=WR[:, k, fc * 128:(fc + 1) * 128], rhs=XMT[:, k, off + lo:off + hi], start=(k == 0), stop=(k == 7)),
                             reads=WRK + ["XMT"], writes=[pk])
                    if sh > 0:
                        P.op("pool", lambda e: e.memset(S[:, 0:1], 0.0), writes=[SK])
                    if sh + n < 258:
                        P.op("pool", lambda e: e.memset(S[:, 257:258], 0.0), writes=[SK])
                    P.op("act", lambda e: e.copy(out=S[:, sh:sh + n], in_=pt[:, 0:n]), reads=[pk], writes=[SK])
                    P.op("act", lambda e: e.activation(out=dst, in_=S[:, 1:257], func=AF.Copy, scale=ppc(l, 9 + fc)), reads=[SK, "PPK"], writes=[AK(name)])
                    P.op("dve", lambda e: e.scalar_tensor_tensor(out=dst, in0=S[:, 0:256], scalar=ppc(l, fc), in1=dst, op0=ALU.mult, op1=ALU.add), reads=[SK, "PPK", AK(name)], writes=[AK(name)])
                    P.op("dve", lambda e: e.scalar_tensor_tensor(out=dst, in0=S[:, 2:258], scalar=ppc(l, 18 + fc), in1=dst, op0=ALU.mult, op1=ALU.add), reads=[SK, "PPK", AK(name)], writes=[AK(name)])

                def prep_shared(hp, off, NTs, s0):
                    for fc, name in ((6, "F6"), (7, "FAD"), (8, "FSG"), (hp, "FR"), (2 + hp, "FK"), (4 + hp, "FV")):
                        proj_conv(fc, name, off, NTs, s0)
                        yield
                    P.op("act", lambda e: e.activation(out=A["F6"], in_=A["F6"], func=AF.Tanh), reads=[AK("F6")], writes=[AK("F6")])
                    P.op("act", lambda e: e.activation(out=A["FSG"], in_=A["FSG"], func=AF.Sigmoid), reads=[AK("FSG")], writes=[AK("FSG")])
                    for dd in (0, 1):
                        pt, pk = psum()
                        P.op("pe", lambda e: e.matmul(pt[:, 0:256], lhsT=A2T[64 * dd:64 * dd + 64, 0, hp * 128:(hp + 1) * 128], rhs=A["FAD"][64 * dd:64 * dd + 64, :], start=True, stop=True), reads=["A2T", AK("FAD")], writes=[pk])
                        P.op("act", lambda e: e.activation(out=A["A%d" % dd], in_=pt[:, 0:256], func=AF.Sigmoid, bias=ppc(l, 31 + dd * 2 + hp), scale=1.0), reads=[pk, "PPK"], writes=[AK("A%d" % dd)])
                    P.op("dve", lambda e: e.tensor_scalar(out=A["T1"], in0=A["FK"], scalar1=ppc(l, 35 + hp), scalar2=None, op0=ALU.mult), reads=[AK("FK"), "PPK"], writes=[AK("T1")])
                    P.op("pool", lambda e: e.tensor_tensor(out=A["T2"], in0=A["T1"], in1=A["T1"], op=ALU.mult), reads=[AK("T1")], writes=[AK("T2")])
                    yield
                    pt, pk = psum()
                    P.op("pe", lambda e: e.matmul(pt[:, 0:256], lhsT=ONESBD, rhs=A["T2"], start=True, stop=True), reads=["CST", AK("T2")], writes=[pk])
                    P.op("act", lambda e: e.activation(out=A["T2"], in_=pt[:, 0:256], func=AF.Sqrt, scale=64.0), reads=[pk], writes=[AK("T2")])
                    yield
                    P.op("dve", lambda e: e.tensor_scalar(out=A["T2"], in0=A["T2"], scalar1=1e-12, scalar2=None, op0=ALU.max), reads=[AK("T2")], writes=[AK("T2")])
                    P.op("dve", lambda e: e.reciprocal(out=A["T2"], in_=A["T2"]), reads=[AK("T2")], writes=[AK("T2")])
                    yield
                    P.op("dve", lambda e: e.tensor_tensor(out=A["KKN"], in0=A["T1"], in1=A["T2"], op=ALU.mult), reads=[AK("T1"), AK("T2")], writes=[AK("KKN")])
                    for h in (0, 1):
                        P.op("pool", lambda e, h=h: e.tensor_copy(out=EXP["VB"][64 * h:64 * h + 64, :, 64 * h:64 * h + 64], in_=v3("FV")[64 * h:64 * h + 64]), reads=[AK("FV"), ("EXP", "VB")], writes=[("EXP", "VB")])
                    yield

                def prep_dir(hp, d):
                    pt, pk = psum()
                    P.op("pe", lambda e: e.matmul(pt[:, 0:256], lhsT=W2T[64 * d:64 * d + 64, 0, hp * 128:(hp + 1) * 128], rhs=A["F6"][64 * d:64 * d + 64, :], start=True, stop=True), reads=["W2T", AK("F6")], writes=[pk])
                    P.op("act", lambda e: e.activation(out=A["LD"], in_=pt[:, 0:256], func=AF.Sigmoid, bias=ppc(l, 27 + d * 2 + hp), scale=1.0), reads=[pk, "PPK"], writes=[AK("LD")])
                    Ad = A["A%d" % d]; AdK = AK("A%d" % d)
                    P.op("pool", lambda e: e.tensor_tensor(out=A["BD"], in0=Ad, in1=A["KKN"], op=ALU.mult), reads=[AdK, AK("KKN")], writes=[AK("BD")])
                    yield
                    P.op("dve", lambda e: e.tensor_scalar(out=A["LD"], in0=A["LD"], scalar1=-0.6065306597126334, scalar2=None, op0=ALU.mult), reads=[AK("LD")], writes=[AK("LD")])
                    P.op("dve", lambda e: e.tensor_scalar(out=A["T1"], in0=Ad, scalar1=ppc(l, 37 + hp), scalar2=ppc(l, 37 + hp), op0=ALU.mult, op1=ALU.subtract), reads=[AdK, "PPK"], writes=[AK("T1")])
                    yield
                    P.op("dve", lambda e: e.tensor_tensor_scan(out=A["CF"], data0=RMASK[:, 0:256], data1=A["LD"], initial=0.0, op0=ALU.mult, op1=ALU.add), reads=[AK("LD"), "CST2"], writes=[AK("CF")])
                    P.op("dve", lambda e: e.scalar_tensor_tensor(out=A["KD"], in0=A["T1"], scalar=1.0, in1=A["FK"], op0=ALU.add, op1=ALU.mult), reads=[AK("T1"), AK("FK")], writes=[AK("KD")])
                    yield
                    CF3 = A["CF"].rearrange("p (c t) -> p c t", t=64)
                    P.op("dve", lambda e: e.tensor_copy(out=TOT.rearrange("p (c o) -> p c o", o=1), in_=CF3[:, :, 63:64]), reads=[AK("CF")], writes=["TOT"])
                    yield
                    P.op("act", lambda e: e.activation(out=GAM, in_=TOT, func=AF.Exp), reads=["TOT"], writes=["GAM"])
                    TOTB = TOT.rearrange("p (c o) -> p c o", o=1).to_broadcast([128, 4, 64])
                    if d == 0:
                        P.op("dve", lambda e: e.tensor_tensor(out=A["T2"], in0=A["CF"], in1=A["LD"], op=ALU.subtract), reads=[AK("CF"), AK("LD")], writes=[AK("T2")])
                        P.op("act", lambda e: e.activation(out=A["E2"], in_=A["CF"], func=AF.Exp), reads=[AK("CF")], writes=[AK("E2")])
                        yield
                        P.op("act", lambda e: e.activation(out=A["E1"], in_=A["T2"], func=AF.Exp), reads=[AK("T2")], writes=[AK("E1")])
                        yield
                        P.op("act", lambda e: e.activation(out=A["E3"], in_=A["CF"], func=AF.Exp, scale=-1.0), reads=[AK("CF"), AK("T2")], writes=[AK("E3")])
                    else:
                        P.op("dve", lambda e: e.tensor_tensor(out=v3("T2"), in0=TOTB, in1=v3("CF"), op=ALU.subtract), reads=["TOT", AK("CF")], writes=[AK("T2")])
                        yield
                        P.op("act", lambda e: e.activation(out=A["E1"], in_=A["T2"], func=AF.Exp), reads=[AK("T2")], writes=[AK("E1")])
                        P.op("dve", lambda e: e.tensor_tensor(out=A["CF"], in0=A["T2"], in1=A["LD"], op=ALU.add), reads=[AK("T2"), AK("LD")], writes=[AK("CF")])
                        yield
                        P.op("act", lambda e: e.activation(out=A["E2"], in_=A["CF"], func=AF.Exp), reads=[AK("CF")], writes=[AK("E2")])
                        P.op("act", lambda e: e.activation(out=A["E3"], in_=A["CF"], func=AF.Exp, scale=-1.0), reads=[AK("CF"), AK("T2"), AK("E1")], writes=[AK("E3")])
                    yield
                    i = 0
                    for (n, a, b) in (("QB", "KKN", "E1"), ("RB", "FR", "E2"), ("KB", "KD", "E3"), ("BB", "BD", "E3")):
                        for h in (0, 1):
                            eng = "dve" if i % 2 == 0 else "pool"
                            i += 1
                            P.op(eng, lambda e, n=n, a=a, b=b, h=h: e.tensor_tensor(out=EXP[n][64 * h:64 * h + 64, :, 64 * h:64 * h + 64], in0=v3(a)[64 * h:64 * h + 64], in1=v3(b)[64 * h:64 * h + 64], op=ALU.mult), reads=[AK(a), AK(b), ("EXP", n)], writes=[("EXP", n)])
                        yield

                def units_pre(d):
                    E = lambda n, c: EXP[n][:, c, :]
                    for c in range(4):
                        P.op("pe", lambda e: e.transpose(out=PST[:, c * 128:(c + 1) * 128], in_=E("KB", c), identity=IDB[:]), reads=EK + ["IDB"], writes=["PST"])
                        P.op("pe", lambda e: e.transpose(out=PST[:, (4 + c) * 128:(5 + c) * 128], in_=E("BB", c), identity=IDB[:]), reads=EK + ["IDB"], writes=["PST"])
                    P.op("act", lambda e: e.copy(out=KTBT, in_=PST[:, 0:1024]), reads=["PST"], writes=["KTBT"])
                    for c in range(4):
                        P.op("pe", lambda e: e.transpose(out=PST[:, c * 128:(c + 1) * 128], in_=E("VB", c), identity=IDB[:]), reads=EK + ["IDB"], writes=["PST"])
                    P.op("act", lambda e: e.copy(out=VT4, in_=PST[:, 0:512].rearrange("p (c t) -> p c t", t=128)), reads=["PST"], writes=["VT4"])
                    yield
                    mb = lambda j: MASK(d, j).unsqueeze(1).to_broadcast([128, 4, 128])
                    grams = ((("QB", "BB"), 0, XX[0][:, :, 0, :], ("XX", 0)), (("BB", "QB"), 1, XX[0][:, :, 1, :], ("XX", 0)),
                             (("KB", "QB"), 2, GR["AKK"], "GAKK"), (("KB", "RB"), 3, GR["ARK"], "GARK"), (("BB", "RB"), 3, GR["ARB"], "GARB"))
                    for gi, ((lh, rh), mj, dst, dk) in enumerate(grams):
                        pg, pgk = psum()
                        for c in range(4):
                            P.op("pe", lambda e: e.matmul(pg[:, c * 128:(c + 1) * 128], lhsT=E(lh, c), rhs=E(rh, c), start=True, stop=True), reads=EK, writes=[pgk])
                        pg3 = pg[:, 0:512].rearrange("p (c t) -> p c t", t=128)
                        P.op("dve", lambda e: e.tensor_tensor(out=dst, in0=pg3, in1=mb(mj), op=ALU.mult), reads=[pgk, "CST"], writes=[dk])
                        if gi == 1:
                            P.op("dve", lambda e: e.scalar_tensor_tensor(out=TTM[0], in0=pg3, scalar=-1.0, in1=mb(mj), op0=ALU.mult, op1=ALU.mult), reads=[pgk, "CST"], writes=[("TT", 0)])
                        yield
                    cur = 0
                    for k in range(5):
                        nx = 1 - cur
                        pxs = [psum(), psum()]
                        for c in range(4):
                            px, pxk = pxs[c // 2]
                            cc = c % 2
                            P.op("pe", lambda e: e.matmul(px[:, (2 * cc) * 128:(2 * cc + 1) * 128], lhsT=XX[cur][:, c, 1, :], rhs=XX[cur][:, c, 0, :], start=True, stop=True), reads=[("XX", cur)], writes=[pxk])
                            P.op("pe", lambda e: e.matmul(px[:, (2 * cc + 1) * 128:(2 * cc + 2) * 128], lhsT=XX[cur][:, c, 0, :], rhs=XX[cur][:, c, 1, :], start=True, stop=True), reads=[("XX", cur)], writes=[pxk])
                        for hlf in (0, 1):
                            px, pxk = pxs[hlf]
                            dstv = XX[nx][:, 2 * hlf:2 * hlf + 2, :, :]
                            srcv = px[:, 0:512].rearrange("p (c x t) -> p c x t", x=2, t=128)
                            if hlf == 0:
                                P.op("act", lambda e: e.copy(out=dstv, in_=srcv), reads=[pxk], writes=[("XX", nx, hlf)])
                            else:
                                P.op("act", lambda e: e.copy(out=dstv, in_=srcv), reads=[pxk], writes=[("XX", nx, hlf)])
                        yield
                        pT, pTk = psum()
                        for c in range(4):
                            xk = ("XX", nx, c // 2)
                            P.op("pe", lambda e: e.matmul(pT[:, c * 128:(c + 1) * 128], lhsT=XX[nx][:, c, 0, :], rhs=TTM[cur][:, c, :], start=True, stop=False), reads=[xk, ("TT", cur)], writes=[pTk])
                            P.op("pe", lambda e: e.matmul(pT[:, c * 128:(c + 1) * 128], lhsT=IDB[:], rhs=TTM[cur][:, c, :], start=False, stop=False), reads=["IDB", ("TT", cur)], writes=[pTk])
                            P.op("pe", lambda e: e.matmul(pT[:, c * 128:(c + 1) * 128], lhsT=IDB[:], rhs=XX[nx][:, c, 1, :], start=False, stop=True), reads=["IDB", xk], writes=[pTk])
                        P.op("act", lambda e: e.copy(out=TTM[nx], in_=pT[:, 0:512].rearrange("p (c t) -> p c t", t=128)), reads=[pTk], writes=[("TT", nx)])
                        P.st[P._k(("XX", nx))] = [list(P.st[P._k(("XX", nx, 0))][0]) + list(P.st[P._k(("XX", nx, 1))][0]), []]
                        cur = nx
                        yield
                    return cur

                def chain(d, c, tti, odst, okey, obase):
                    E = lambda n: EXP[n][:, c, :]
                    p3, p3k = psum()
                    P.op("pe", lambda e: e.matmul(p3[:, 0:128], lhsT=E("QB"), rhs=CHB["MB"], start=True, stop=False), reads=EK + ["CMB"], writes=[p3k])
                    P.op("pe", lambda e: e.matmul(p3[:, 0:128], lhsT=GR["AKK"][:, c, :], rhs=VT4[:, c, :], start=False, stop=True), reads=["GAKK", "VT4"], writes=[p3k])
                    P.op("act", lambda e: e.copy(out=CHB["X1B"], in_=p3[:, 0:128]), reads=[p3k], writes=["CX1B"])
                    yield
                    p4, p4k = psum()
                    P.op("pe", lambda e: e.matmul(p4[:, 0:128], lhsT=TTM[tti][:, c, :], rhs=CHB["X1B"], start=True, stop=False), reads=[("TT", tti), "CX1B"], writes=[p4k])
                    P.op("pe", lambda e: e.matmul(p4[:, 0:128], lhsT=IDB[:], rhs=CHB["X1B"], start=False, stop=True), reads=["IDB", "CX1B"], writes=[p4k])
                    P.op("dve", lambda e: e.tensor_scalar(out=CHB["NU"], in0=p4[:, 0:128], scalar1=-1.0, scalar2=None, op0=ALU.mult), reads=[p4k], writes=["CNU"])
                    yield
                    p6, p6k = psum()
                    P.op("pe", lambda e: e.matmul(p6[:, 0:128], lhsT=KT4[:, c, :], rhs=VT4[:, c, :], start=True, stop=False), reads=["KTBT", "VT4"], writes=[p6k])
                    P.op("pe", lambda e: e.matmul(p6[:, 0:128], lhsT=BT4[:, c, :], rhs=CHB["NU"], start=False, stop=True), reads=["KTBT", "CNU"], writes=[p6k])
                    p5, p5k = psum()
                    P.op("pe", lambda e: e.matmul(p5[:, 0:128], lhsT=CHB["MB"], rhs=E("RB"), start=True, stop=False), reads=EK + ["CMB"], writes=[p5k])
                    P.op("pe", lambda e: e.matmul(p5[:, 0:128], lhsT=VT4[:, c, :], rhs=GR["ARK"][:, c, :], start=False, stop=False), reads=["VT4", "GARK"], writes=[p5k])
                    P.op("pe", lambda e: e.matmul(p5[:, 0:128], lhsT=CHB["NU"], rhs=GR["ARB"][:, c, :], start=False, stop=True), reads=["CNU", "GARB"], writes=[p5k])
                    P.op("dve", lambda e: e.tensor_tensor(out=CH["MT"], in0=p6[:, 0:128], in1=CH["M"], op=ALU.add), reads=[p6k, "CM"], writes=["CMT"])
                    yield
                    P.op("act", lambda e: e.activation(out=CHB["MB"], in_=CH["MT"], func=AF.Copy, scale=GAM[:, c:c + 1]), reads=["CMT", "GAM"], writes=["CMB"])
                    P.op("dve", lambda e: e.tensor_scalar(out=CH["M"], in0=CH["MT"], scalar1=GAM[:, c:c + 1], scalar2=None, op0=ALU.mult), reads=["CMT", "GAM"], writes=["CM"])
                    for h in (0, 1):
                        P.op("act", lambda e, h=h: e.copy(out=odst[64 * h:64 * h + 64, obase:obase + 64], in_=p5[64 * h:64 * h + 64, 64 * h:64 * h + 64]), reads=[p5k], writes=[okey])
                    yield

                def finalize(hp, off, s0):
                    OBK = AK("OB")
                    P.op("dve", lambda e: e.tensor_tensor(out=A["OB"], in0=A["OB"], in1=OF[:, s0:s0 + 256], op=ALU.add), reads=[OBK, "OF"], writes=[OBK])
                    pt, pk = psum()
                    P.op("pe", lambda e: e.matmul(pt[:, 0:256], lhsT=ONESBD, rhs=A["OB"], start=True, stop=True), reads=["CST", OBK], writes=[pk])
                    yield
                    P.op("dve", lambda e: e.tensor_tensor(out=A["OB"], in0=A["OB"], in1=pt[:, 0:256], op=ALU.subtract), reads=[pk, OBK], writes=[OBK])
                    P.op("pool", lambda e: e.tensor_tensor(out=A["T1"], in0=A["OB"], in1=A["OB"], op=ALU.mult), reads=[OBK], writes=[AK("T1")])
                    yield
                    pt2, pk2 = psum()
                    P.op("pe", lambda e: e.matmul(pt2[:, 0:256], lhsT=ONESBD, rhs=A["T1"], start=True, stop=True), reads=["CST", AK("T1")], writes=[pk2])
                    P.op("act", lambda e: e.activation(out=A["T2"], in_=pt2[:, 0:256], func=AF.Sqrt, bias=CST2[:, 1601:1602], scale=1.0), reads=[pk2, "CST2"], writes=[AK("T2")])
                    yield
                    P.op("dve", lambda e: e.reciprocal(out=A["T2"], in_=A["T2"]), reads=[AK("T2")], writes=[AK("T2")])
                    P.op("pool", lambda e: e.tensor_tensor(out=A["T1"], in0=A["A0"], in1=A["A1"], op=ALU.add), reads=[AK("A0"), AK("A1"), AK("T1")], writes=[AK("T1")])
                    yield
                    P.op("dve", lambda e: e.tensor_tensor(out=A["OB"], in0=A["OB"], in1=A["T2"], op=ALU.mult), reads=[OBK, AK("T2")], writes=[OBK])
                    P.op("dve", lambda e: e.tensor_scalar(out=A["OB"], in0=A["OB"], scalar1=ppc(l, 41 + hp), scalar2=ppc(l, 43 + hp), op0=ALU.mult, op1=ALU.add), reads=[OBK, "PPK"], writes=[OBK])
                    P.op("dve", lambda e: e.tensor_scalar(out=A["T1"], in0=A["T1"], scalar1=-2.0, scalar2=ppc(l, 37 + hp), op0=ALU.add, op1=ALU.mult), reads=[AK("T1"), "PPK"], writes=[AK("T1")])
                    yield
                    P.op("dve", lambda e: e.scalar_tensor_tensor(out=A["T1"], in0=A["T1"], scalar=2.0, in1=A["FK"], op0=ALU.add, op1=ALU.mult), reads=[AK("T1"), AK("FK")], writes=[AK("T1")])
                    P.op("dve", lambda e: e.scalar_tensor_tensor(out=A["T1"], in0=A["T1"], scalar=ppc(l, 39 + hp), in1=A["FR"], op0=ALU.mult, op1=ALU.mult), reads=[AK("T1"), "PPK", AK("FR")], writes=[AK("T1")])
                    yield
                    pt3, pk3 = psum()
                    P.op("pe", lambda e: e.matmul(pt3[:, 0:256], lhsT=ONESBD, rhs=A["T1"], start=True, stop=True), reads=["CST", AK("T1")], writes=[pk3])
                    P.op("dve", lambda e: e.scalar_tensor_tensor(out=A["BON"], in0=pt3[:, 0:256], scalar=64.0, in1=A["FV"], op0=ALU.mult, op1=ALU.mult), reads=[pk3, AK("FV")], writes=[AK("BON")])
                    yield
                    pt4, pk4 = psum()
                    P.op("pe", lambda e: e.matmul(pt4[:, 0:256], lhsT=G2[:, 0, hp * 128:(hp + 1) * 128], rhs=A["FSG"], start=True, stop=True), reads=["G2", AK("FSG")], writes=[pk4])
                    P.op("dve", lambda e: e.tensor_tensor(out=A["OB"], in0=A["OB"], in1=A["BON"], op=ALU.add), reads=[OBK, AK("BON")], writes=[OBK])
                    yield
                    P.op("dve", lambda e: e.tensor_tensor(out=OCT[:, hp, off + s0:off + s0 + 256], in0=A["OB"], in1=pt4[:, 0:256], op=ALU.mult), reads=[OBK, pk4], writes=[("OCTW", hp)])
                    yield

                def init_state(is_sample, d, hp):
                    P.op("pool", lambda e: e.memset(CH["M"], 0.0), writes=["CM"])
                    if is_sample:
                        for h in (0, 1):
                            P.dma("sp", CH["M"][64 * h:64 * h + 64, 64 * h:64 * h + 64], stT[l, d, hp, 64 * h:64 * h + 64, :], "stin", writes=["CM"])
                    P.op("pool", lambda e: e.tensor_copy(out=CHB["MB"], in_=CH["M"]), reads=["CM"], writes=["CMB"])

                def emit_state(seq_idx, d, hp):
                    pt, pk = psum()
                    P.op("pe", lambda e: e.transpose(out=pt[:, 0:128], in_=CH["M"], identity=IDENT), reads=["CM", "CST"], writes=[pk])
                    P.op("act", lambda e: e.copy(out=CH["MT"], in_=pt[:, 0:128]), reads=[pk], writes=["CMT"])
                    for h in (0, 1):
                        P.dma("sp", nst[l, seq_idx, d, 2 * hp + h], CH["MT"][64 * h:64 * h + 64, 64 * h:64 * h + 64], "ost", reads=["CMT"])

                def job(off, NTs, is_sample, seq_idx, hp):
                    nseg = NTs // 256
                    sweeps = [((0, 1), [0])] if nseg == 1 else [((0,), list(range(nseg))), ((1,), list(reversed(range(nseg))))]
                    for (dirs, segs) in sweeps:
                        if nseg > 1:
                            init_state(is_sample, dirs[0], hp)
                        for sg in segs:
                            s0 = sg * 256
                            yield from prep_shared(hp, off, NTs, s0)
                            for d in dirs:
                                if nseg == 1:
                                    init_state(is_sample, d, hp)
                                yield from prep_dir(hp, d)
                                tti = yield from units_pre(d)
                                for c in ([0, 1, 2, 3] if d == 0 else [3, 2, 1, 0]):
                                    if d == 0:
                                        yield from chain(d, c, tti, OF, "OF", s0 + 64 * c)
                                    else:
                                        yield from chain(d, c, tti, A["OB"], AK("OB"), 64 * c)
                                if nseg == 1 and not is_sample:
                                    emit_state(seq_idx, d, hp)
                                    yield
                                if d == 1:
                                    yield from finalize(hp, off, s0)
                        if nseg > 1 and not is_sample:
                            emit_state(seq_idx, dirs[0], hp)
                            yield
                return job

            npsum[0] = 7
            jobs = [make_lane(0), make_lane(1)]
            assert o[0] <= 14336, o[0]

            def lane_stream(li):
                for (off, NTs, is_sample, seq_idx) in pass_seqs:
                    yield from jobs[li](off, NTs, is_sample, seq_idx, li)

            gens = [(0, lane_stream(0)), (1, lane_stream(1))]
            if bg is not None:
                gens.append((2, bg))
            while gens:
                for (li, g) in list(gens):
                    P.lane = li
                    try:
                        next(g)
                    except StopIteration:
                        gens.remove((li, g))
            P.lane = None
            npsum[0] = 5
            P.barrier()

        def attn_phase(l, tiles, NT, is_sample, pass_seqs):
            NTT = NT // 128
            nrow = NT // 64
            o = [0]
            def arrb(n):
                v = AR[:, o[0]:o[0] + n // 2].bitcast(BF16)
                o[0] += n // 2
                return v
            def arrf(n):
                v = AR[:, o[0]:o[0] + n]
                o[0] += n
                return v
            NKC = 512 if is_sample else 0
            QKT = arrb(4 * NT).rearrange("p (c t) -> p c t", t=NT)
            GQT = arrb(4 * NT).rearrange("p (c t) -> p c t", t=NT)
            GK2 = arrb(2 * (NT + NKC)).rearrange("p (c t) -> p c t", t=NT + NKC)
            VNA = arrb(nrow * 4 * 64).rearrange("p (r h c) -> p r h c", h=4, c=64)
            VG = arrb((NTT + NKC // 128) * 2 * 64).rearrange("p (t h c) -> p t h c", h=2, c=64)
            tok_off = o[0]
            TOK = arrf(1280)
            TK2 = arrf(768)
            RS = arrf(16)
            PT = [arrb(512) for _ in range(3)]
            SBs = [arrf(512), MODB[:, 0:512]]
            sbrr = [0]
            RC = MODB[:, 512:1024]
            if is_sample:
                KCT = arrb(2 * 512).rearrange("p (c t) -> p c t", t=512)
                VCN = arrb(4 * 4 * 64).rearrange("p (t h c) -> p t h c", h=4, c=64)
                BR = arrf(960)
            assert o[0] <= 14336, o[0]
            ptrr = [0]
            def ptile():
                i = ptrr[0] % 3
                ptrr[0] += 1
                return PT[i], ("PT", i)

            P.dma("sp", QKN[:], qkn[l:l + 1, :].partition_broadcast(128)[:, 0, :], "c0", writes=["QKN"])
            if is_sample:
                for t in range(4):
                    P.dma("sp", TOK[:, 0:256], cnak[l, t * 128:(t + 1) * 128, :], "ctx", writes=["TOK"])
                    P.op("pool", lambda e: e.tensor_copy(out=TMPB[:, 0:256], in_=TOK[:, 0:256]), reads=["TOK"], writes=["TMPB"])
                    for cc in range(2):
                        P.op("pe", lambda e, cc=cc: e.transpose(out=PST[:, cc * 128:(cc + 1) * 128], in_=TMPB[:, cc * 128:(cc + 1) * 128], identity=IDB[:]), reads=["TMPB", "IDB"], writes=["PST"])
                    P.op("act", lambda e, t=t: e.copy(out=KCT[:, :, t * 128:(t + 1) * 128], in_=PST[:, 0:256].rearrange("p (c t) -> p c t", t=128)), reads=["PST"], writes=["KCT"])
                    P.dma("sp", TOK[:, 256:512], cnav[l, t * 128:(t + 1) * 128, :], "ctx", writes=["TOK2"])
                    P.op("pool", lambda e, t=t: e.tensor_copy(out=VCN[:, t, :, :], in_=TOK[:, 256:512].rearrange("p (h c) -> p h c", c=64)), reads=["TOK2"], writes=["VCN"])
                    P.dma("sp", TOK[:, 512:640], cgk[l, t * 128:(t + 1) * 128, :], "ctx", writes=["TOK3"])
                    for kv in (0, 1):
                        P.op("pool", lambda e, kv=kv: e.tensor_copy(out=TMPB[:, 256 + 128 * kv:384 + 128 * kv].rearrange("p (r c) -> p r c", c=64), in_=TOK[:, 512 + 64 * kv:576 + 64 * kv].unsqueeze(1).to_broadcast([128, 2, 64])), reads=["TOK3"], writes=["TMPB2"])
                    for kv in (0, 1):
                        P.op("pe", lambda e, kv=kv: e.transpose(out=PST[:, 256 + 128 * kv:384 + 128 * kv], in_=TMPB[:, 256 + 128 * kv:384 + 128 * kv], identity=IDB[:]), reads=["TMPB2", "IDB"], writes=["PST2"])
                    P.op("act", lambda e, t=t: e.copy(out=GK2[:, :, t * 128:(t + 1) * 128], in_=PST[:, 256:512].rearrange("p (c t) -> p c t", t=128)), reads=["PST2"], writes=["GKT"])
                    P.dma("sp", TOK[:, 640:768], cgv[l, t * 128:(t + 1) * 128, :], "ctx", writes=["TOK4"])
                    P.op("pool", lambda e, t=t: e.tensor_copy(out=VG[:, t, :, :], in_=TOK[:, 640:768].rearrange("p (h c) -> p h c", c=64)), reads=["TOK4"], writes=["VG"])
                P.barrier()

            SUB = int(os.environ.get("ATT_SUB", "9"))
            if SUB < 1:
                P.barrier(); return
            wv, wk = load_w(w_in[l], 0, 8, 1152, 512)
            for cc in range(4):
                for g0 in range(0, NT, 512):
                    gn = min(512, NT - g0)
                    pt, pk = psum()
                    for k in range(8):
                        P.op("pe", lambda e, pt=pt, k=k, cc=cc, g0=g0, gn=gn: e.matmul(pt[:, 0:gn], lhsT=wv[:, k, cc * 128:(cc + 1) * 128], rhs=XMT[:, k, g0:g0 + gn], start=(k == 0), stop=(k == 7)), reads=[wk, "XMT"], writes=[pk])
                    P.op("act", lambda e, pt=pt, cc=cc, g0=g0, gn=gn: e.copy(out=QKT[:, cc, g0:g0 + gn], in_=pt[:, 0:gn]), reads=[pk], writes=["QKT"])
            if SUB < 2:
                P.barrier(); return
            wa, wak = load_w(w_in[l], 0, 8, 1408, 512)
            wb, wbk = load_w(w_in[l], 0, 8, 1920, 512)
            wc, wck = load_w(w_in[l], 0, 8, 2432, 256)
            LB = [dict(TOK=TOK, TK2=TK2, RS=RS, TMPB=TMPB),
                  dict(TOK=LNB[:, 0:1280], TK2=LNB[:, 1280:2048], RS=MODB[:, 1024:1040], TMPB=TMPF[:].bitcast(BF16)[:, 0:1024])]

            def tile_job(i, tt, li):
                TOK_ = LB[li]["TOK"]; TK2_ = LB[li]["TK2"]; RS_ = LB[li]["RS"]; TMPB_ = LB[li]["TMPB"]
                kx = lambda k: k + "_%d" % li
                pa, pak = psum()
                for k in range(8):
                    P.op("pe", lambda e, k=k: e.matmul(pa[:, 0:512], lhsT=XMT[:, k, i * 128:(i + 1) * 128], rhs=wa[:, k, :], start=(k == 0), stop=(k == 7)), reads=[wak, "XMT"], writes=[pak])
                P.op("act", lambda e: e.copy(out=TOK_[:, 0:512], in_=pa[:, 0:512]), reads=[pak], writes=[kx("TOK")])
                pb, pbk = psum()
                for k in range(8):
                    P.op("pe", lambda e, k=k: e.matmul(pb[:, 0:512], lhsT=XMT[:, k, i * 128:(i + 1) * 128], rhs=wb[:, k, :], start=(k == 0), stop=(k == 7)), reads=[wbk, "XMT"], writes=[pbk])
                P.op("act", lambda e: e.copy(out=TOK_[:, 512:1024], in_=pb[:, 0:512]), reads=[pbk], writes=[kx("TOK2")])
                pc, pck = psum()
                for k in range(8):
                    P.op("pe", lambda e, k=k: e.matmul(pc[:, 0:256], lhsT=XMT[:, k, i * 128:(i + 1) * 128], rhs=wc[:, k, :], start=(k == 0), stop=(k == 7)), reads=[wck, "XMT"], writes=[pck])
                P.op("act", lambda e: e.copy(out=TOK_[:, 1024:1280], in_=pc[:, 0:256]), reads=[pck], writes=[kx("TOK3")])
                for rr in (0, 1):
                    pv_, pvk = psum()
                    for k in range(8):
                        P.op("pe", lambda e, k=k: e.matmul(pv_[0:64, 0:256], lhsT=XMT[:, k, i * 128 + rr * 64:i * 128 + rr * 64 + 64], rhs=wa[:, k, 256:512], start=(k == 0), stop=(k == 7)), reads=[wak, "XMT"], writes=[pvk])
                    P.op("act", lambda e: e.copy(out=VNA[0:64, 2 * i + rr, :, :], in_=pv_[0:64, 0:256].rearrange("p (h c) -> p h c", c=64)), reads=[pvk], writes=[("VNA", 2 * i + rr)])
                yield
                QK = TOK_[:, 512:1152].rearrange("p (h c) -> p h c", c=64)
                T2v = TK2_[:, 0:640].rearrange("p (h c) -> p h c", c=64)
                TK = [kx("TOK2"), kx("TOK3")]
                P.op("dve", lambda e: e.tensor_tensor(out=T2v, in0=QK, in1=QK, op=ALU.mult), reads=TK, writes=[kx("TK2")])
                P.op("dve", lambda e: e.tensor_reduce(out=RS_[:, 0:10], in_=T2v, axis=mybir.AxisListType.X, op=ALU.add), reads=[kx("TK2")], writes=[kx("RS")])
                yield
                P.op("act", lambda e: e.activation(out=RS_[:, 0:10], in_=RS_[:, 0:10], func=AF.Sqrt, bias=CST2[:, 1600:1601], scale=1.0 / 64.0), reads=[kx("RS"), "CST2"], writes=[kx("RS")])
                yield
                P.op("dve", lambda e: e.reciprocal(out=RS_[:, 0:10], in_=RS_[:, 0:10]), reads=[kx("RS")], writes=[kx("RS")])
                yield
                P.op("dve", lambda e: e.tensor_tensor(out=QK, in0=QK, in1=RS_[:, 0:10].unsqueeze(2).to_broadcast([128, 10, 64]), op=ALU.mult), reads=[kx("RS")] + TK, writes=TK)
                P.op("dve", lambda e: e.tensor_tensor(out=QK[:, 0:8, :], in0=QK[:, 0:8, :], in1=QKN[:, 0:64].unsqueeze(1).to_broadcast([128, 8, 64]), op=ALU.mult), reads=["QKN"] + TK, writes=TK)
                P.op("dve", lambda e: e.tensor_tensor(out=QK[:, 8:10, :], in0=QK[:, 8:10, :], in1=QKN[:, 64:128].unsqueeze(1).to_broadcast([128, 2, 64]), op=ALU.mult), reads=["QKN"] + TK, writes=TK)
                yield
                if not is_sample:
                    r0 = i * 128
                    P.dma("sp", nak[l, r0:r0 + 128, :], TOK_[:, 0:256], "oc", reads=[kx("TOK")])
                    P.dma("sp", nav[l, r0:r0 + 128, :], TOK_[:, 256:512], "oc", reads=[kx("TOK")])
                    P.dma("sp", ngk[l, r0:r0 + 128, :], TOK_[:, 1024:1152], "oc", reads=[kx("TOK3")])
                    P.dma("sp", ngv[l, r0:r0 + 128, :], TOK_[:, 1152:1280], "oc", reads=[kx("TOK3")])
                else:
                    Q4 = TOK_[:, 512:1152].rearrange("p (h a c) -> p h a c", a=4, c=16)
                    T4 = TK2_[:, 0:640].rearrange("p (h a c) -> p h a c", a=4, c=16)
                    rp = ROPE[:, i, :]
                    for ax in (0, 1):
                        cosb = rp[:, 16 * ax:16 * ax + 16].unsqueeze(1).to_broadcast([128, 10, 16])
                        sinb = rp[:, 32 + 16 * ax:48 + 16 * ax].unsqueeze(1).to_broadcast([128, 10, 16])
                        x1 = Q4[:, :, 2 * ax, :]; x2 = Q4[:, :, 2 * ax + 1, :]
                        t1 = T4[:, :, 0, :]; t2 = T4[:, :, 1, :]
                        P.op("dve", lambda e: e.tensor_tensor(out=t1, in0=x1, in1=sinb, op=ALU.mult), reads=TK + ["CST2"], writes=[kx("TK2")])
                        P.op("dve", lambda e: e.tensor_tensor(out=t2, in0=x2, in1=sinb, op=ALU.mult), reads=TK + ["CST2"], writes=[kx("TK2")])
                        yield
                        P.op("dve", lambda e: e.tensor_tensor(out=x1, in0=x1, in1=cosb, op=ALU.mult), reads=TK + ["CST2", kx("TK2")], writes=TK)
                        P.op("dve", lambda e: e.tensor_tensor(out=x2, in0=x2, in1=cosb, op=ALU.mult), reads=TK + ["CST2", kx("TK2")], writes=TK)
                        yield
                        P.op("dve", lambda e: e.tensor_tensor(out=x1, in0=x1, in1=t2, op=ALU.subtract), reads=TK + [kx("TK2")], writes=TK)
                        P.op("dve", lambda e: e.tensor_tensor(out=x2, in0=x2, in1=t1, op=ALU.add), reads=TK + [kx("TK2")], writes=TK)
                        yield
                P.op("pool", lambda e: e.tensor_copy(out=TMPB_[:, 0:512], in_=TOK_[:, 512:1024]), reads=TK, writes=[kx("TMPB")])
                for kv in (0, 1):
                    P.op("pool", lambda e, kv=kv: e.tensor_copy(out=TMPB_[:, 512 + 128 * kv:640 + 128 * kv].rearrange("p (r c) -> p r c", c=64), in_=TOK_[:, 1024 + 64 * kv:1088 + 64 * kv].unsqueeze(1).to_broadcast([128, 2, 64])), reads=TK, writes=[kx("TMPB")])
                P.op("pool", lambda e: e.tensor_copy(out=VG[:, NKC // 128 + i, :, :], in_=TOK_[:, 1152:1280].rearrange("p (h c) -> p h c", c=64)), reads=[kx("TOK3")], writes=[("VG", i)])
                yield
                for cc in range(6):
                    P.op("pe", lambda e, cc=cc: e.transpose(out=PST[:, cc * 128:(cc + 1) * 128], in_=TMPB_[:, cc * 128:(cc + 1) * 128], identity=IDB[:]), reads=[kx("TMPB"), "IDB"], writes=["PST"])
                P.op("act", lambda e: e.copy(out=GQT[:, :, i * 128:(i + 1) * 128], in_=PST[:, 0:512].rearrange("p (c t) -> p c t", t=128)), reads=["PST"], writes=[("GQT", i)])
                P.op("act", lambda e: e.copy(out=GK2[:, :, NKC + i * 128:NKC + (i + 1) * 128], in_=PST[:, 512:768].rearrange("p (c t) -> p c t", t=128)), reads=["PST"], writes=[("GKT", i)])
                yield

            def tl_stream(li):
                for i, tt in enumerate(tiles):
                    if i % 2 == li:
                        yield from tile_job(i, tt, li)
            gens = [tl_stream(0), tl_stream(1)]
            while gens:
                for g in list(gens):
                    try:
                        next(g)
                    except StopIteration:
                        gens.remove(g)
            for base, cnt_ in (("GQT", NTT), ("GKT", NTT), ("VG", NTT), ("VNA", 2 * NTT)):
                evs = list(P.st.get(base, [[], []])[0])
                for i_ in range(cnt_):
                    evs += list(P.st.get((base, i_), [[], []])[0])
                P.st[base] = [evs, []]
            if SUB < 3:
                P.barrier(); return
            PRE["wout"] = (load_w(w_out[l], 0, 8, 0, 512), load_w(w_out[l], 0, 8, 512, 512))
            nvt = NTT + NKC // 128
            VGA = AR[:, tok_off:tok_off + nvt * 128].bitcast(BF16).rearrange("p (t h c) -> p t h c", h=2, c=128)
            tl0 = ["TOK_0", "TOK2_0", "TOK3_0", "TK2_0"]
            P.op("pool", lambda e: e.memset(VGA[:, :, :, 64:128], 1.0), writes=["VGA1"] + tl0)
            P.op("act", lambda e: e.copy(out=VGA[:, :, :, 0:64], in_=VG), reads=["VG"], writes=["VGA"] + tl0)
            def finish_head(acc, acck, acs, acsk, chunk, half, q0, qn):
                P.op("dve", lambda e: e.reciprocal(out=RC[0:64, 0:qn], in_=acs[0:64, 0:qn]), reads=[acsk], writes=["RC"])
                P.op("dve", lambda e: e.tensor_tensor(out=OCT[64 * half:64 * half + 64, chunk, q0:q0 + qn], in0=acc[0:64, 0:qn], in1=RC[0:64, 0:qn], op=ALU.mult), reads=[acck, "RC"], writes=["OCT"])

            def score_block(qT, kT, nk, qn, bias=None):
                ps_, psk = psum()
                P.op("pe", lambda e: e.matmul(ps_[0:nk, 0:qn], lhsT=kT, rhs=qT, start=True, stop=True), reads=["QKT", "GQT", "GKT", "KCT"], writes=[psk])
                pt_, ptk = ptile()
                if bias is None:
                    P.op("act", lambda e: e.activation(out=pt_[0:nk, 0:qn], in_=ps_[0:nk, 0:qn], func=AF.Exp, scale=0.125), reads=[psk], writes=[ptk])
                else:
                    si = sbrr[0] % 2
                    sbrr[0] += 1
                    SB = SBs[si]
                    P.op("dve", lambda e: e.scalar_tensor_tensor(out=SB[0:nk, 0:qn], in0=ps_[0:nk, 0:qn], scalar=0.125, in1=bias, op0=ALU.mult, op1=ALU.add), reads=[psk, "BR"], writes=[("SB", si)])
                    P.op("act", lambda e: e.activation(out=pt_[0:nk, 0:qn], in_=SB[0:nk, 0:qn], func=AF.Exp), reads=[("SB", si)], writes=[ptk])
                return pt_, ptk

            VK = ["VNA", "VG", "VCN", "ONESB"]

            def pv(acc, acck, acs, acsk, vT, nk, pt_, ptk, c0, n, first):
                P.op("pe", lambda e: e.matmul(acc[0:64, c0:c0 + n], lhsT=vT, rhs=pt_[0:nk, 0:n], start=first, stop=False), reads=VK + [ptk], writes=[acck])
                P.op("pe", lambda e: e.matmul(acs[0:64, c0:c0 + n], lhsT=ONESB[0:nk, :], rhs=pt_[0:nk, 0:n], start=first, stop=False), reads=VK + [ptk], writes=[acsk])

            if is_sample:
                qblocks = [(0, 512, 0, 16), (512, 512, 0, 16)]
            else:
                qblocks = [(off, NTs, off // 64, (off + NTs) // 64) for (off, NTs, _s, _i) in pass_seqs]
            def pv_aug(acc, acck, vT, nk, pt_, ptk, c0, n, first):
                P.op("pe", lambda e: e.matmul(acc[:, c0:c0 + n], lhsT=vT, rhs=pt_[0:nk, 0:n], start=first, stop=False), reads=["VGA", "VGA1", ptk], writes=[acck])

            def finish_head_aug(acc, acck, chunk, half, q0, qn):
                P.op("dve", lambda e: e.reciprocal(out=RC[64:128, 0:qn], in_=acc[64:128, 0:qn]), reads=[acck], writes=["RC"])
                P.op("dve", lambda e: e.tensor_tensor(out=OCT[64 * half:64 * half + 64, chunk, q0:q0 + qn], in0=acc[0:64, 0:qn], in1=RC[64:128, 0:qn], op=ALU.mult), reads=[acck, "RC"], writes=["OCT"])

            def run_blocks(blocks, acc, acck, acs, acsk, aug=False):
                sc = {}
                if blocks:
                    b = blocks[0]
                    sc[0] = score_block(b[0], b[1], b[2], b[3], bias=b[4])
                for i, b in enumerate(blocks):
                    if i + 1 < len(blocks):
                        nb = blocks[i + 1]
                        sc[i + 1] = score_block(nb[0], nb[1], nb[2], nb[3], bias=nb[4])
                    pt_, ptk = sc.pop(i)
                    if aug:
                        pv_aug(acc, acck, b[5], b[2], pt_, ptk, b[6], b[3], i == 0)
                    else:
                        pv(acc, acck, acs, acsk, b[5], b[2], pt_, ptk, b[6], b[3], i == 0)

            for h in range(4):
                hb = 64 * (h % 2)
                qch = h // 2
                kch = 2 + h // 2
                if is_sample:
                    P.dma("sp", BR[0:64, :], nab[l, h], "ctx", writes=["BR"])
                    P.op("dve", lambda e: e.tensor_tensor(out=BR[0:64, :].rearrange("p (a c) -> p a c", c=64), in0=BR[0:64, :].rearrange("p (a c) -> p a c", c=64), in1=COLM.unsqueeze(1).to_broadcast([64, 15, 64]), op=ALU.add), reads=["BR", "CST2"], writes=["BR"])
                for (q0, qn, kr0, kr1) in qblocks:
                    acc, acck = psacc(0)
                    acs, acsk = psacc(1)
                    qT = QKT[hb:hb + 64, qch, q0:q0 + qn]
                    blocks = []
                    if is_sample:
                        for t in range(4):
                            blocks.append((qT, KCT[hb:hb + 64, h // 2, t * 128:(t + 1) * 128], 128, qn, None, VCN[:, t, h, :], 0))
                        rows = range(q0 // 64, (q0 + qn) // 64)
                        for j in range(16):
                            att = [r for r in rows if min(max(r - 4, 0), 8) <= j <= min(max(r - 4, 0), 8) + 7]
                            if not att:
                                continue
                            rlo, rhi = att[0], att[-1]
                            nr = rhi - rlo + 1
                            e0 = rlo - j + 7
                            c0 = (rlo - q0 // 64) * 64
                            blocks.append((QKT[hb:hb + 64, qch, rlo * 64:(rhi + 1) * 64], QKT[hb:hb + 64, kch, j * 64:(j + 1) * 64], 64, nr * 64,
                                           BR[0:64, e0 * 64:(e0 + nr) * 64], VNA[0:64, j, h, :], c0))
                    else:
                        for j in range(kr0, kr1):
                            blocks.append((qT, QKT[hb:hb + 64, kch, j * 64:(j + 1) * 64], 64, qn, None, VNA[0:64, j, h, :], 0))
                    run_blocks(blocks, acc, acck, acs, acsk)
                    finish_head(acc, acck, acs, acsk, 2 + h // 2, h % 2, q0, qn)
            for h in range(8):
                hb = 64 * (h % 2)
                kvh = h // 4
                for (q0, qn, kr0, kr1) in qblocks:
                    acc, acck = psacc(0)
                    acs, acsk = psacc(1)
                    qT = GQT[hb:hb + 64, h // 2, q0:q0 + qn]
                    kts = list(range(NKC // 128)) + [NKC // 128 + t for t in range(kr0 // 2, kr1 // 2)]
                    blocks = [(qT, GK2[hb:hb + 64, kvh, t * 128:(t + 1) * 128], 128, qn, None, VGA[:, t, kvh, :], 0) for t in kts]
                    run_blocks(blocks, acc, acck, acs, acsk, aug=True)
                    finish_head_aug(acc, acck, 4 + h // 2, h % 2, q0, qn)
            P.barrier()

        def dense_out(l, tiles, NT, ci):
            load_mod(l, ci, 2048, 1024, MODB[:, 0:1024])
            load_ln(l, ln1w, ln1b)
            if "wout" in PRE:
                (wv0, wk0), (wv1, wk1) = PRE.pop("wout")
            else:
                wv0, wk0 = load_w(w_out[l], 0, 8, 0, 512)
                wv1, wk1 = load_w(w_out[l], 0, 8, 512, 512)
            for i, tt in enumerate(tiles):
                for (wv, wk, n0) in ((wv0, wk0, 0), (wv1, wk1, 512)):
                    pt, pk = psum()
                    for k in range(8):
                        P.op("pe", lambda e, k=k, i=i, pt=pt, wv=wv: e.matmul(pt[:, 0:512], lhsT=OCT[:, k, i * 128:(i + 1) * 128], rhs=wv[:, k, :], start=(k == 0), stop=(k == 7)), reads=[wk, "OCT"], writes=[pk])
                    P.op("dve", lambda e, pt=pt, n0=n0: e.tensor_tensor(out=TMPF[:, n0:n0 + 512], in0=pt[:, 0:512], in1=MODB[:, n0:n0 + 512], op=ALU.mult), reads=[pk, "MODB"], writes=["TMPF"])
                    P.op("dve", lambda e, tt=tt, n0=n0: e.scalar_tensor_tensor(out=X[:, tt, n0:n0 + 512], in0=X[:, tt, n0:n0 + 512], scalar=ALPHA, in1=TMPF[:, n0:n0 + 512], op0=ALU.mult, op1=ALU.add), reads=["TMPF", ("X", tt)], writes=[("X", tt)])
                layer_norm(tt, l)

        def ffn(l, tiles, NT, ci):
            load_mod(l, ci, 3072, 2048, MODB[:, 0:2048])
            pre_g = load_w(w_fi[l], 0, 8, 0, 512)
            pre_u = load_w(w_fi[l], 0, 8, DFF, 512)
            modulate_transpose(tiles, MODB[:, 0:1024], MODB[:, 1024:2048])
            load_mod(l, ci, 5120, 1024, MODB[:, 0:1024])
            load_ln(l, ln2w, ln2b)
            HT = AR[:, 0:11 * NT // 2].bitcast(BF16).rearrange("p (j t) -> p j t", t=NT)
            GS = AR[:, 6000:6512]
            for half in range(2):
                for jb in range(3):
                    j0 = half * 11 + jb * 4
                    nj = min(4, half * 11 + 11 - j0)
                    if half == 0 and jb == 0:
                        (wg, wgk), (wu, wuk) = pre_g, pre_u
                    else:
                        wg, wgk = load_w(w_fi[l], 0, 8, j0 * 128, nj * 128)
                        wu, wuk = load_w(w_fi[l], 0, 8, DFF + j0 * 128, nj * 128)
                    for jj in range(nj):
                        for g0 in range(0, NT, 512):
                            gn = min(512, NT - g0)
                            pg, pgk = psum()
                            for k in range(8):
                                P.op("pe", lambda e, k=k, pg=pg, jj=jj, g0=g0, gn=gn: e.matmul(pg[:, 0:gn], lhsT=wg[:, k, jj * 128:(jj + 1) * 128], rhs=XMT[:, k, g0:g0 + gn], start=(k == 0), stop=(k == 7)), reads=[wgk, "XMT"], writes=[pgk])
                            pu, puk = psum()
                            for k in range(8):
                                P.op("pe", lambda e, k=k, pu=pu, jj=jj, g0=g0, gn=gn: e.matmul(pu[:, 0:gn], lhsT=wu[:, k, jj * 128:(jj + 1) * 128], rhs=XMT[:, k, g0:g0 + gn], start=(k == 0), stop=(k == 7)), reads=[wuk, "XMT"], writes=[puk])
                            P.op("act", lambda e, pg=pg, gn=gn: e.activation(out=GS[:, 0:gn], in_=pg[:, 0:gn], func=AF.Silu), reads=[pgk], writes=["GS"])
                            P.op("dve", lambda e, pu=pu, gn=gn, g0=g0, jj=jj, jb=jb: e.tensor_tensor(out=HT[:, jb * 4 + jj, g0:g0 + gn], in0=GS[:, 0:gn], in1=pu[:, 0:gn], op=ALU.mult), reads=[puk, "GS"], writes=["HT"])
                for n0 in (0, 512):
                    wa_, wak_ = load_w(w_fo[l], half * 11, 8, n0, 512)
                    wb_, wbk_ = load_w(w_fo[l], half * 11 + 8, 3, n0, 512)
                    for i, tt in enumerate(tiles):
                        pt, pk = psum()
                        for j in range(11):
                            wv, wk, jj = (wa_, wak_, j) if j < 8 else (wb_, wbk_, j - 8)
                            P.op("pe", lambda e, j=j, jj=jj, wv=wv, pt=pt, i=i: e.matmul(pt[:, 0:512], lhsT=HT[:, j, i * 128:(i + 1) * 128], rhs=wv[:, jj, :], start=(j == 0), stop=(j == 10)), reads=[wk, "HT"], writes=[pk])
                        P.op("dve", lambda e, pt=pt, n0=n0: e.tensor_tensor(out=TMPF[:, n0:n0 + 512], in0=pt[:, 0:512], in1=MODB[:, n0:n0 + 512], op=ALU.mult), reads=[pk, "MODB"], writes=["TMPF"])
                        if half == 0:
                            P.op("dve", lambda e, tt=tt, n0=n0: e.scalar_tensor_tensor(out=X[:, tt, n0:n0 + 512], in0=X[:, tt, n0:n0 + 512], scalar=ALPHA, in1=TMPF[:, n0:n0 + 512], op0=ALU.mult, op1=ALU.add), reads=["TMPF", ("X", tt)], writes=[("X", tt)])
                        else:
                            P.op("dve", lambda e, tt=tt, n0=n0: e.tensor_tensor(out=X[:, tt, n0:n0 + 512], in0=X[:, tt, n0:n0 + 512], in1=TMPF[:, n0:n0 + 512], op=ALU.add), reads=["TMPF", ("X", tt)], writes=[("X", tt)])
            for tt in tiles:
                layer_norm(tt, l)

        seqs = [([0, 1, 2, 3], 0, False, [(0, 256, False, 0), (256, 256, False, 1)]), (list(range(4, 12)), 1, True, [(0, 1024, True, 0)])]
        if os.environ.get("SEQS"):
            seqs = [seqs[int(c)] for c in os.environ["SEQS"]]
        for l in range(depth):
            P.barrier()
            P.dma("sp", W2T[:, 0, :], w2[l], "c0", writes=["W2T"])
            P.dma("sp", A2T[:, 0, :], a2[l], "c0", writes=["A2T"])
            P.dma("sp", G2[:, 0, :], g2[l], "c0", writes=["G2"])
            for (tiles, ci, is_sample, pass_seqs) in seqs:
                NT = 128 * len(tiles)
                if stage < 1:
                    continue
                load_mod(l, ci, 0, 2048, MODB[:, 0:2048])
                if stage >= 2:
                    WR_ = WS[:].rearrange("p s c -> p (s c)")[:, 0:8 * 1152].rearrange("p (k c) -> p k c", c=1152)
                    P.dma("pool", WR_, w_in[l].rearrange("(k p) c -> p k c", p=128)[:, :, 0:1152], "ws0", writes=[("ws", 0), ("ws", 1), ("ws", 2)])
                modulate_transpose(tiles, MODB[:, 0:1024], MODB[:, 1024:2048])
                P.barrier()
                if stage >= 2:
                    rwkv_phase(l, NT, pass_seqs, bg=(mods_gen(l + 1) if (is_sample and l + 1 < depth) else None))
                if stage >= 3:
                    attn_phase(l, tiles, NT, is_sample, pass_seqs)
                if stage >= 4:
                    dense_out(l, tiles, NT, ci)
                if stage >= 5:
                    ffn(l, tiles, NT, ci)
        for t in range(12):
            P.dma("sp", y[t * 128:(t + 1) * 128, :], X[:, t, :], "yo", reads=[("X", t)])
        P.barrier()
        P.emit()
    return nc


def _consts():
    cst = np.zeros((128, 128 * 12), np.float32)
    cst[:, 0:128] = np.eye(128, dtype=np.float32)
    bd = np.zeros((128, 128), np.float32)
    bd[0:64, 0:64] = 1.0; bd[64:128, 64:128] = 1.0
    cst[:, 128:256] = bd / 64.0
    i = np.arange(64)[:, None]; j = np.arange(64)[None, :]
    SL = (i > j).astype(np.float32); SU = (i < j).astype(np.float32)
    LI = (i >= j).astype(np.float32); UI = (i <= j).astype(np.float32)
    def blk(m):
        o = np.zeros((128, 128), np.float32); o[0:64, 0:64] = m; o[64:, 64:] = m; return o
    per = {0: [SL, SU, SU, UI, UI], 1: [SU, SL, SL, LI, LI]}
    for d in (0, 1):
        for k, m in enumerate(per[d]):
            cst[:, 256 + (d * 5 + k) * 128: 256 + (d * 5 + k + 1) * 128] = blk(m)
    cst2 = np.zeros((128, 1603), np.float32)
    rm = np.ones(1024, np.float32); rm[::64] = 0.0
    cst2[:, 0:1024] = rm[None, :]
    t = np.arange(1024)
    inv = 10000.0 ** (-np.arange(16, dtype=np.float32) / 16.0)
    angr = (t // 64).astype(np.float32)[:, None] * inv[None, :]
    angc = (t % 64).astype(np.float32)[:, None] * inv[None, :]
    tab = np.concatenate([np.cos(angr), np.cos(angc), np.sin(angr), np.sin(angc)], 1).astype(np.float32)
    cst2[:, 1024:1536] = tab.reshape(8, 128, 64).transpose(1, 0, 2).reshape(128, 512)
    cq = np.arange(64)[None, :]; ck = np.arange(64)[:, None]
    c0 = np.clip(cq - 8, 0, 48)
    inwin = (ck >= c0) & (ck < c0 + 16)
    cst2[0:64, 1536:1600] = np.where(inwin, 0.0, NEG).astype(np.float32)
    cst2[:, 1600] = RMS_EPS; cst2[:, 1601] = GN_EPS; cst2[:, 1602] = LN_EPS
    return cst, cst2


_NC_CACHE = {}


def _prep(inp):
    f = lambda k: np.ascontiguousarray(np.asarray(inp[k], dtype=np.float32))
    cst, cst2 = _consts()
    x_prompt = f("x_prompt"); x_sample = f("x_sample")
    conv = f("rwkv_conv"); w0 = f("rwkv_w0"); a0 = f("rwkv_a0")
    kk_ = f("rwkv_k_k"); ka_ = f("rwkv_k_a"); rk_ = f("rwkv_r_k").reshape(NL, 256)
    lw_ = f("rwkv_lnx_w"); lb_ = f("rwkv_lnx_b")
    pp = np.zeros((128, NL * PPL), np.float32)
    for l in range(NL):
        b = l * PPL
        pp[:, b:b + 27] = conv[l].reshape(3, 9, 128).transpose(2, 0, 1).reshape(128, 27)
        pp[:, b + 27:b + 31] = w0[l].reshape(2, 2, 128).transpose(2, 0, 1).reshape(128, 4)
        pp[:, b + 31:b + 35] = a0[l].reshape(2, 2, 128).transpose(2, 0, 1).reshape(128, 4)
        for j, arr in enumerate((kk_, ka_, rk_, lw_, lb_)):
            pp[:, b + 35 + 2 * j:b + 37 + 2 * j] = arr[l].reshape(2, 128).T
    rpb = f("na_rpb")
    ck = np.arange(64)[:, None]; cq = np.arange(64)[None, :]
    dc = np.clip(ck - cq, -15, 15) + 15
    e = np.arange(15)
    nab = rpb[:, :, 14 - e][:, :, :, dc]
    nab = np.ascontiguousarray(nab.transpose(0, 1, 3, 2, 4)).reshape(NL, 4, 64, 15 * 64)
    qkn = np.concatenate([f("gqa_q_norm"), f("gqa_k_norm")], 1)
    c = f("c"); c_ctx = f("c_ctx")
    st = f("state_rwkv")
    shared = {
        "w_mod": f("w_mod"), "b_mod": f("b_mod"), "w_in": f("w_in"), "w_out": f("w_out"),
        "w_ffn_in": f("w_ffn_in"), "w_ffn_out": f("w_ffn_out"), "pp": pp,
        "rwkv_w2": f("rwkv_w2").reshape(NL, 128, 256), "rwkv_a2": f("rwkv_a2").reshape(NL, 128, 256), "rwkv_g2": f("rwkv_g2"),
        "ln1_w": f("ln1_w"), "ln1_b": f("ln1_b"), "ln2_w": f("ln2_w"), "ln2_b": f("ln2_b"),
        "qkn": qkn, "nab": nab, "cst": cst, "cst2": cst2,
    }
    in_maps = []
    for i in range(8):
        b = i // 4
        m = dict(shared)
        m["xin"] = np.concatenate([x_prompt[2 * i], x_prompt[2 * i + 1], x_sample[b]], 0)
        cc = np.stack([c_ctx, c[b]], 1)
        m["c2"] = np.ascontiguousarray(cc.reshape(8, 128, 2).transpose(1, 0, 2).reshape(128, 16))
        s = st[b].transpose(0, 1, 2, 4, 3)
        m["stT"] = np.ascontiguousarray(s.reshape(NL, 2, 2, 128, 64))
        m["cnak"] = np.ascontiguousarray(f("cache_na_k")[b].reshape(NL, PAST, 256))
        m["cnav"] = np.ascontiguousarray(f("cache_na_v")[b].reshape(NL, PAST, 256))
        m["cgk"] = np.ascontiguousarray(f("cache_gqa_k")[b].reshape(NL, PAST, 128))
        m["cgv"] = np.ascontiguousarray(f("cache_gqa_v")[b].reshape(NL, PAST, 128))
        in_maps.append(m)
    return in_maps


def kernel(**inp):
    depth = NL
    if depth not in _NC_CACHE:
        _NC_CACHE[depth] = build(depth)
    nc = _NC_CACHE[depth]
    in_maps = _prep(inp)
    res = run_bass_kernel_spmd(nc, in_maps, core_ids=list(range(8)))
    R = res.results
    y_prompt = np.stack([R[i // 2]["y"][(i % 2) * 256:(i % 2) * 256 + 256] for i in range(16)], 0)
    y_sample = np.stack([R[0]["y"][512:], R[4]["y"][512:]], 0)
    nst = np.stack([R[i // 2]["nst"][:, i % 2] for i in range(16)], 0)
    def cache(name, hh):
        return np.stack([R[i // 2][name][:, (i % 2) * 256:(i % 2) * 256 + 256].reshape(NL, 256, hh, 64) for i in range(16)], 0)
    return (y_prompt.astype(np.float32), y_sample.astype(np.float32), nst.astype(np.float32),
            cache("nak", 4), cache("nav", 4), cache("ngk", 2), cache("ngv", 2))
```

```python
import numpy as np
import concourse.bass as bass
import concourse.mybir as mybir
from concourse.bass_utils import run_bass_kernel_spmd
from contextlib import ExitStack
import types
import os

F32 = mybir.dt.float32
BF16 = mybir.dt.bfloat16
AF = mybir.ActivationFunctionType
ALU = mybir.AluOpType

D = 1024
NL = 4
DIN = 2688
DFF = 2816
PAST = 512
ALPHA = (2 * NL) ** 0.25
LN_EPS = 1e-5
RMS_EPS = 1e-6
GN_EPS = 64e-5
NEG = -30000.0
PPL = 45

ENGS = ("pe", "act", "dve", "pool", "sp")


def _freeze(fn):
    if fn.__closure__ is None:
        return fn
    cells = []
    for c in fn.__closure__:
        try:
            cells.append(types.CellType(c.cell_contents))
        except ValueError:
            cells.append(c)
    return types.FunctionType(fn.__code__, fn.__globals__, fn.__name__, fn.__defaults__, tuple(cells))


class Prog:
    def __init__(self, nc, stack):
        self.nc = nc
        self.stack = stack
        self.q = {e: [] for e in ENGS}
        self.cnt = {e: 0 for e in ENGS}
        self.sem = {e: stack.enter_context(nc.semaphore("s_" + e)) for e in ENGS}
        self.known = {e: {} for e in ENGS}
        self.st = {}
        self.dsem = {}
        self.lane = None

    def sb(self, name, shape, dt=F32):
        return self.stack.enter_context(self.nc.sbuf_tensor(name, list(shape), dt))

    def ps(self, name, shape, dt=F32):
        return self.stack.enter_context(self.nc.psum_tensor(name, list(shape), dt))

    def dma_sem(self, name):
        if name not in self.dsem:
            self.dsem[name] = [self.stack.enter_context(self.nc.semaphore("d_" + name)), 0]
        return self.dsem[name]

    def _need(self, eng, ev, waits):
        if ev[0] == "eng":
            _, f, idx = ev
            if f == eng:
                if eng == "pe":
                    return
                if idx < self.cnt[eng] - 1:
                    return
            if self.known[eng].get(f, 0) >= idx:
                return
            self.known[eng][f] = idx
            waits.append((self.sem[f], idx))
        else:
            _, name, cnt = ev
            k = "d:" + name
            if self.known[eng].get(k, 0) >= cnt:
                return
            self.known[eng][k] = cnt
            waits.append((self.dsem[name][0], cnt))

    def _deps(self, eng, reads, writes):
        waits = []
        for k in reads:
            s = self.st.setdefault(k, [[], []])
            for ev in s[0]:
                self._need(eng, ev, waits)
            if isinstance(k, tuple) and k[0] == "ps":
                for ev in s[1]:
                    if not (ev[0] == "eng" and ev[1] == eng):
                        self._need(eng, ev, waits)
        for k in writes:
            s = self.st.setdefault(k, [[], []])
            for ev in s[0]:
                self._need(eng, ev, waits)
            for ev in s[1]:
                self._need(eng, ev, waits)
        return waits

    def _commit(self, ev, reads, writes):
        for k in reads:
            if k in writes:
                continue
            s = self.st[k]
            s[1] = [r for r in s[1] if not (r[0] == ev[0] and r[1] == ev[1])] + [ev]
        for k in writes:
            self.st[k] = [[ev], []]

    GLOBAL_KEYS = {"CST", "CST2", "PPK", "XMT", "OCT", "IDB", "A2T", "W2T", "G2", "PST", "SIL", "SILF", "QKN", "MODB", "LNB",
                   "TMPF", "TMPB", "ONESB"}

    def _k(self, k):
        if self.lane is None:
            return k
        if isinstance(k, tuple) and k[0] in ("ps", "ws", "X", "modrow", "L"):
            return k
        if isinstance(k, str) and k in self.GLOBAL_KEYS:
            return k
        return ("L", self.lane, k)

    def op(self, eng, fn, reads=(), writes=()):
        fn = _freeze(fn)
        reads = [self._k(k) for k in reads]
        writes = [self._k(k) for k in writes]
        waits = self._deps(eng, reads, writes)
        self.cnt[eng] += 1
        idx = self.cnt[eng]
        self._commit(("eng", eng, idx), reads, writes)
        sem = self.sem[eng]

        def run(e, waits=waits, fn=fn, sem=sem):
            for (s, v) in waits:
                e.wait_ge(s, v)
            fn(e).then_inc(sem, 1)

        self.q[eng].append(run)

    def dma(self, eng, out, in_, sname, reads=(), writes=()):
        reads = [self._k(k) for k in reads]
        writes = [self._k(k) for k in writes]
        k0 = None
        for k in list(writes) + list(reads):
            if not (isinstance(k, tuple) and k[0] == "modrow"):
                k0 = k
                break
        sname = "k_" + str(k0).replace("(", "").replace(")", "").replace(",", "_").replace(" ", "").replace("'", "")
        waits = self._deps(eng, reads, writes)
        d = self.dma_sem(sname)
        d[1] += 16
        self._commit(("dma", sname, d[1]), reads, writes)
        sem = d[0]

        def run(e, waits=waits, out=out, in_=in_, sem=sem):
            for (s, v) in waits:
                e.wait_ge(s, v)
            e.dma_start(out=out, in_=in_).then_inc(sem, 16)

        self.q[eng].append(run)

    def barrier(self):
        for e in ENGS:
            waits = []
            for f in ENGS:
                if f != e and self.cnt[f] > self.known[e].get(f, 0):
                    self.known[e][f] = self.cnt[f]
                    waits.append((self.sem[f], self.cnt[f]))
            for name, (sem, cnt) in self.dsem.items():
                k = "d:" + name
                if cnt > self.known[e].get(k, 0):
                    self.known[e][k] = cnt
                    waits.append((sem, cnt))

            def run(eh, waits=waits):
                for (s, v) in waits:
                    eh.wait_ge(s, v)

            self.q[e].append(run)

    def emit(self):
        nc = self.nc
        with nc.Block() as block:
            @block.tensor
            def _(e):
                for r in self.q["pe"]:
                    r(e)

            @block.scalar
            def _(e):
                for r in self.q["act"]:
                    r(e)

            @block.vector
            def _(e):
                for r in self.q["dve"]:
                    r(e)

            @block.gpsimd
            def _(e):
                for r in self.q["pool"]:
                    r(e)

            @block.sync
            def _(e):
                for r in self.q["sp"]:
                    r(e)


def build(depth=NL, stage=99):
    nc = bass.Bass("TRN2", target_bir_lowering=False)

    def din(name, shape):
        return nc.dram_tensor(name, list(shape), F32, kind="ExternalInput").ap()

    def dout(name, shape):
        return nc.dram_tensor(name, list(shape), F32, kind="ExternalOutput").ap()

    xin = din("xin", [1536, D])
    c2 = din("c2", [128, 16])
    stT = din("stT", [NL, 2, 2, 128, 64])
    cnak = din("cnak", [NL, PAST, 256])
    cnav = din("cnav", [NL, PAST, 256])
    cgk = din("cgk", [NL, PAST, 128])
    cgv = din("cgv", [NL, PAST, 128])
    w_mod = din("w_mod", [NL, D, 6 * D])
    b_mod = din("b_mod", [NL, 6 * D])
    w_in = din("w_in", [NL, D, DIN])
    w_out = din("w_out", [NL, D, D])
    w_fi = din("w_ffn_in", [NL, D, 2 * DFF])
    w_fo = din("w_ffn_out", [NL, DFF, D])
    pp = din("pp", [128, NL * PPL])
    w2 = din("rwkv_w2", [NL, 128, 256])
    a2 = din("rwkv_a2", [NL, 128, 256])
    g2 = din("rwkv_g2", [NL, 128, 256])
    ln1w = din("ln1_w", [NL, D]); ln1b = din("ln1_b", [NL, D])
    ln2w = din("ln2_w", [NL, D]); ln2b = din("ln2_b", [NL, D])
    qkn = din("qkn", [NL, 128])
    nab = din("nab", [NL, 4, 64, 15 * 64])
    cst = din("cst", [128, 128 * 12])
    cst2 = din("cst2", [128, 1603])

    y = dout("y", [1536, D])
    nst = dout("nst", [NL, 2, 2, 4, 64, 64])
    nak = dout("nak", [NL, 512, 256]); nav = dout("nav", [NL, 512, 256])
    ngk = dout("ngk", [NL, 512, 128]); ngv = dout("ngv", [NL, 512, 128])
    modrow = nc.dram_tensor("modrow", [NL, 2, 6 * D], F32, kind="Internal").ap()

    with ExitStack() as stack:
        P = Prog(nc, stack)
        X = P.sb("X", [128, 12, D])
        XMT = P.sb("XMT", [128, 8, 1024], BF16)
        OCT = P.sb("OCT", [128, 8, 1024], BF16)
        MODB = P.sb("MODB", [128, 2048])
        LNB = P.sb("LNB", [128, 2048])
        WS = P.sb("WS", [128, 4, 8 * 512], BF16)
        AR = P.sb("AR", [128, 14336])
        CST = P.sb("CST", [128, 128 * 12])
        CST2 = P.sb("CST2", [128, 1603])
        PPK = P.sb("PPK", [128, NL * PPL])
        W2T = P.sb("W2T", [128, 1, 256]); A2T = P.sb("A2T", [128, 1, 256]); G2 = P.sb("G2", [128, 1, 256])
        ONESB = P.sb("ONESB", [128, 64], BF16)
        SIL = P.sb("SIL", [128, 16], BF16)
        SILF = P.sb("SILF", [128, 16])
        QKN = P.sb("QKN", [128, 128])
        IDB = P.sb("IDB", [128, 128], BF16)
        TMPF = P.sb("TMPF", [128, 1024])
        TMPB = P.sb("TMPB", [128, 1024], BF16)
        STAT = P.sb("STAT", [128, 64])
        PSB = [P.ps("psb%d" % i, [128, 512]) for i in range(7)]
        PST = P.ps("pst", [128, 1024], BF16)

        IDENT = CST[:, 0:128]
        ONESBD = CST[:, 128:256]
        def MASK(d, j):
            return CST[:, 256 + (d * 5 + j) * 128: 256 + (d * 5 + j + 1) * 128]
        def MASK4(d):
            return CST[:, 256 + d * 5 * 128: 256 + (d * 5 + 4) * 128]
        RMASK = CST2[:, 0:1024]
        ROPE = CST2[:, 1024:1024 + 512].rearrange("p (t c) -> p t c", c=64)
        COLM = CST2[0:64, 1536:1600]

        psrr = [0]
        npsum = [5]
        def psum():
            i = psrr[0] % npsum[0]
            psrr[0] += 1
            return PSB[i], ("ps", i)
        def psacc(i):
            return PSB[5 + i], ("ps", 5 + i)

        P.dma("sp", CST[:], cst[:, :], "c0", writes=["CST"])
        P.dma("sp", CST2[:], cst2[:, :], "c0", writes=["CST2"])
        P.dma("sp", PPK[:], pp[:, :], "c0", writes=["PPK"])
        P.dma("sp", SILF[:], c2[:, :], "c0", writes=["SILF"])
        P.dma("pool", IDB[:], cst[:, 0:128], "c1", writes=["IDB"])
        P.op("pool", lambda e: e.memset(ONESB[:], 1.0), writes=["ONESB"])
        for t in range(12):
            P.dma("sp", X[:, t, :], xin[t * 128:(t + 1) * 128, :], "xin", writes=[("X", t)])
        P.op("act", lambda e: e.activation(out=SIL[:], in_=SILF[:], func=AF.Silu), reads=["SILF"], writes=["SIL"])

        wsrr = [0]
        def load_w(src2d, k0, nk, c0, ncols):
            s = wsrr[0] % 4
            wsrr[0] += 1
            view = WS[:, s, 0:nk * ncols].rearrange("p (k c) -> p k c", c=ncols)
            src = src2d.rearrange("(k p) c -> p k c", p=128)[:, k0:k0 + nk, c0:c0 + ncols]
            P.dma("pool", view, src, "ws%d" % s, writes=[("ws", s)])
            return view, ("ws", s)

        MR = AR[0:2, 0:6144]
        BM = AR[0:2, 6144:12288]
        for l in range(1):
            P.dma("sp", BM, b_mod[l:l + 1, :].partition_broadcast(2)[:, 0, :], "c0", writes=["BM"])
            for nt in range(12):
                wv, wk = load_w(w_mod[l], 0, 8, nt * 512, 512)
                pt, pk = psum()
                for k in range(8):
                    P.op("pe", lambda e, pt=pt, wv=wv, k=k: e.matmul(pt[0:2, :], lhsT=SIL[:, 2 * k:2 * k + 2], rhs=wv[:, k, :], start=(k == 0), stop=(k == 7)),
                         reads=["SIL", wk], writes=[pk])
                addone = 1.0 if nt in (2, 3, 8, 9) else 0.0
                P.op("dve", lambda e, pt=pt, nt=nt, addone=addone: e.scalar_tensor_tensor(out=MR[:, nt * 512:(nt + 1) * 512], in0=pt[0:2, :], scalar=addone, in1=BM[:, nt * 512:(nt + 1) * 512], op0=ALU.add, op1=ALU.add),
                     reads=[pk, "BM"], writes=["MR"])
            P.dma("sp", modrow[l], MR, "mr", reads=["MR"], writes=[("modrow", l)])
        P.barrier()

        PRE = {}

        def mods_gen(l1):
            wvs = [WS[:, 3, 0:2048].rearrange("p (k c) -> p k c", c=256), WS[:, 3, 2048:4096].rearrange("p (k c) -> p k c", c=256)]
            MRS = [MODB[0:2, 1536:1792], MODB[0:2, 1792:2048]]
            BMS = [LNB[0:2, 1536:1792], LNB[0:2, 1792:2048]]
            src = w_mod[l1].rearrange("(k p) c -> p k c", p=128)

            def issue(j):
                b = j % 2
                P.dma("sp", BMS[b], b_mod[l1:l1 + 1, j * 256:(j + 1) * 256].partition_broadcast(2)[:, 0, :], "bms", writes=[("BMS", b)])
                P.dma("pool", wvs[b], src[:, :, j * 256:(j + 1) * 256], "ws3", writes=[("wm", b)])

            issue(0)
            for _ in range(6):
                yield
            for j in range(24):
                b = j % 2
                if j + 1 < 24:
                    issue(j + 1)
                for _ in range(4):
                    yield
                pt, pk = psum()
                for k in range(8):
                    P.op("pe", lambda e, k=k: e.matmul(pt[0:2, 0:256], lhsT=SIL[:, 2 * k:2 * k + 2], rhs=wvs[b][:, k, :], start=(k == 0), stop=(k == 7)), reads=["SIL", ("wm", b)], writes=[pk])
                yield
                addone = 1.0 if (j // 2) in (2, 3, 8, 9) else 0.0
                P.op("dve", lambda e: e.scalar_tensor_tensor(out=MRS[b], in0=pt[0:2, 0:256], scalar=addone, in1=BMS[b], op0=ALU.add, op1=ALU.add), reads=[pk, ("BMS", b)], writes=[("MRS", b)])
                yield
                P.dma("sp", modrow[l1, :, j * 256:(j + 1) * 256], MRS[b], "mrs", reads=[("MRS", b)], writes=[("modrow", l1)])
                yield

        def load_mod(l, ci, c0, n, dst):
            P.dma("sp", dst, modrow[l, ci:ci + 1, c0:c0 + n].partition_broadcast(128)[:, 0, :], "md", reads=[("modrow", l)], writes=["MODB"])

        def load_ln(l, wsrc, bsrc):
            P.dma("sp", LNB[:, 0:1024], wsrc[l:l + 1, :].partition_broadcast(128)[:, 0, :], "ln", writes=["LNB"])
            P.dma("sp", LNB[:, 1024:2048], bsrc[l:l + 1, :].partition_broadcast(128)[:, 0, :], "ln", writes=["LNB"])

        def ppc(l, j):
            return PPK[:, l * PPL + j: l * PPL + j + 1]

        def modulate_transpose(tiles, mod_sh, mod_sc):
            for i, tt in enumerate(tiles):
                P.op("dve", lambda e, tt=tt: e.tensor_tensor(out=TMPF[:], in0=X[:, tt, :], in1=mod_sc, op=ALU.mult), reads=[("X", tt), "MODB"], writes=["TMPF"])
                P.op("dve", lambda e: e.tensor_tensor(out=TMPB[:], in0=TMPF[:], in1=mod_sh, op=ALU.add), reads=["TMPF", "MODB"], writes=["TMPB"])
                for k in range(8):
                    P.op("pe", lambda e, k=k: e.transpose(out=PST[:, k * 128:(k + 1) * 128], in_=TMPB[:, k * 128:(k + 1) * 128], identity=IDB[:]), reads=["TMPB", "IDB"], writes=["PST"])
                P.op("act", lambda e, i=i: e.copy(out=XMT[:, :, i * 128:(i + 1) * 128], in_=PST[:].rearrange("p (k t) -> p k t", t=128)), reads=["PST"], writes=["XMT"])

        def layer_norm(tt, l):
            P.op("dve", lambda e: e.bn_stats(out=STAT[:, 0:6], in_=X[:, tt, 0:512]), reads=[("X", tt)], writes=["STAT"])
            P.op("dve", lambda e: e.bn_stats(out=STAT[:, 6:12], in_=X[:, tt, 512:1024]), reads=[("X", tt)], writes=["STAT"])
            P.op("dve", lambda e: e.bn_aggr(out=STAT[:, 12:14], in_=STAT[:, 0:12].rearrange("p (a b) -> p a b", b=6)), reads=["STAT"], writes=["STAT"])
            P.op("act", lambda e: e.activation(out=STAT[:, 14:15], in_=STAT[:, 13:14], func=AF.Sqrt, bias=CST2[:, 1602:1603], scale=1.0), reads=["STAT", "CST2"], writes=["STAT2"])
            P.op("dve", lambda e: e.reciprocal(out=STAT[:, 15:16], in_=STAT[:, 14:15]), reads=["STAT2"], writes=["STAT3"])
            P.op("dve", lambda e: e.tensor_scalar(out=X[:, tt, :], in0=X[:, tt, :], scalar1=STAT[:, 12:13], scalar2=STAT[:, 15:16], op0=ALU.subtract, op1=ALU.mult), reads=["STAT", "STAT3", ("X", tt)], writes=[("X", tt)])
            P.op("dve", lambda e: e.tensor_tensor(out=X[:, tt, :], in0=X[:, tt, :], in1=LNB[:, 0:1024], op=ALU.mult), reads=["LNB", ("X", tt)], writes=[("X", tt)])
            P.op("dve", lambda e: e.tensor_tensor(out=X[:, tt, :], in0=X[:, tt, :], in1=LNB[:, 1024:2048], op=ALU.add), reads=["LNB", ("X", tt)], writes=[("X", tt)])

        def rwkv_phase(l, NT, pass_seqs, bg=None):
            WR = WS[:].rearrange("p s c -> p (s c)")[:, 0:8 * 1152].rearrange("p (k c) -> p k c", c=1152)
            WRK = [("ws", 0), ("ws", 1), ("ws", 2)]
            o = [0]
            def arr(n):
                v = AR[:, o[0]:o[0] + n]
                o[0] += n
                return v
            def arrb16(n):
                v = AR[:, o[0]:o[0] + n // 2].bitcast(BF16)
                o[0] += n // 2
                return v
            canon = {"OB": "E2", "BON": "E1", "E3": "T2"}
            names = ["F6", "FAD", "FSG", "FR", "FK", "FV", "KKN", "A0", "A1", "LD", "CF", "KD", "BD", "E1", "E2", "T1", "T2"]

            def make_lane(li):
                A_ = {n: arr(256) for n in names}
                A = dict(A_)
                for a_, b_ in canon.items():
                    A[a_] = A_[b_]
                AK = lambda n: ("A", canon.get(n, n))
                STG = arr(258)
                OF = arr(1024)
                TOT = arr(4)
                GAM = arr(4)
                EXP = {n: arrb16(512).rearrange("p (c t) -> p c t", t=128) for n in ["QB", "RB", "KB", "BB", "VB"]}
                if li == 0:
                    ub = MODB[:].bitcast(BF16)
                    ub2 = LNB[:].bitcast(BF16)
                    KTBT = ub[:, 0:1024]
                    VT4 = ub[:, 1024:1536].rearrange("p (c t) -> p c t", t=128)
                    gr = [ub[:, 1536 + 512 * i:2048 + 512 * i].rearrange("p (c t) -> p c t", t=128) for i in range(3)]
                    XX = [ub2[:, 1024 * i:1024 * (i + 1)].rearrange("p (c x t) -> p c x t", x=2, t=128) for i in range(2)]
                    TTM = [ub2[:, 2048 + 512 * i:2560 + 512 * i].rearrange("p (c t) -> p c t", t=128) for i in range(2)]
                else:
                    uo = OCT[:, 2:8, :].rearrange("p c t -> p (c t)")
                    KTBT = uo[:, 0:1024]
                    VT4 = uo[:, 1024:1536].rearrange("p (c t) -> p c t", t=128)
                    gr = [uo[:, 1536 + 512 * i:2048 + 512 * i].rearrange("p (c t) -> p c t", t=128) for i in range(3)]
                    XX = [uo[:, 3072 + 1024 * i:3072 + 1024 * (i + 1)].rearrange("p (c x t) -> p c x t", x=2, t=128) for i in range(2)]
                    TTM = [uo[:, 5120 + 512 * i:5632 + 512 * i].rearrange("p (c t) -> p c t", t=128) for i in range(2)]
                GR = {"AKK": gr[0], "ARK": gr[1], "ARB": gr[2]}
                KT4 = KTBT[:, 0:512].rearrange("p (c t) -> p c t", t=128)
                BT4 = KTBT[:, 512:1024].rearrange("p (c t) -> p c t", t=128)
                CH = {n: TMPF[:, li * 384 + i * 128:li * 384 + (i + 1) * 128] for i, n in enumerate(["X1F", "M", "MT"])}
                CHB = {n: TMPB[:, li * 384 + i * 128:li * 384 + (i + 1) * 128] for i, n in enumerate(["X1B", "NU", "MB"])}
                P.lane = li
                for n in EXP:
                    P.op("pool", lambda e, n=n: e.memset(EXP[n], 0.0), writes=[("EXP", n)])
                P.lane = None
                v3 = lambda n: A[n].rearrange("p (c t) -> p c t", t=64)
                EK = [("EXP", n) for n in ("QB", "RB", "KB", "BB", "VB")]

                def proj_conv(fc, name, off, NTs, s0):
                    dst = A[name]
                    lo = max(s0 - 1, 0)
                    hi = min(s0 + 257, NTs)
                    sh = lo - (s0 - 1)
                    n = hi - lo
                    S = STG; SK = "STG"
                    pt, pk = psum()
                    for k in range(8):
                        P.op("pe", lambda e, k=k: e.matmul(pt[:, 0:n], lhsT=WR[:, k, fc * 128:(fc + 1) * 128], rhs=XMT[:, k, off + lo:off + hi], start=(k == 0), stop=(k == 7)),
                             reads=WRK + ["XMT"], writes=[pk])
                    if sh > 0:
                        P.op("pool", lambda e: e.memset(S[:, 0:1], 0.0), writes=[SK])
                    if sh + n < 258:
                        P.op("pool", lambda e: e.memset(S[:, 257:258], 0.0), writes=[SK])
                    P.op("act", lambda e: e.copy(out=S[:, sh:sh + n], in_=pt[:, 0:n]), reads=[pk], writes=[SK])
                    P.op("act", lambda e: e.activation(out=dst, in_=S[:, 1:257], func=AF.Copy, scale=ppc(l, 9 + fc)), reads=[SK, "PPK"], writes=[AK(name)])
                    P.op("dve", lambda e: e.scalar_tensor_tensor(out=dst, in0=S[:, 0:256], scalar=ppc(l, fc), in1=dst, op0=ALU.mult, op1=ALU.add), reads=[SK, "PPK", AK(name)], writes=[AK(name)])
                    P.op("dve", lambda e: e.scalar_tensor_tensor(out=dst, in0=S[:, 2:258], scalar=ppc(l, 18 + fc), in1=dst, op0=ALU.mult, op1=ALU.add), reads=[SK, "PPK", AK(name)], writes=[AK(name)])

                def prep_shared(hp, off, NTs, s0):
                    for fc, name in ((6, "F6"), (7, "FAD"), (8, "FSG"), (hp, "FR"), (2 + hp, "FK"), (4 + hp, "FV")):
                        proj_conv(fc, name, off, NTs, s0)
                        yield
                    P.op("act", lambda e: e.activation(out=A["F6"], in_=A["F6"], func=AF.Tanh), reads=[AK("F6")], writes=[AK("F6")])
                    P.op("act", lambda e: e.activation(out=A["FSG"], in_=A["FSG"], func=AF.Sigmoid), reads=[AK("FSG")], writes=[AK("FSG")])
                    for dd in (0, 1):
                        pt, pk = psum()
                        P.op("pe", lambda e: e.matmul(pt[:, 0:256], lhsT=A2T[64 * dd:64 * dd + 64, 0, hp * 128:(hp + 1) * 128], rhs=A["FAD"][64 * dd:64 * dd + 64, :], start=True, stop=True), reads=["A2T", AK("FAD")], writes=[pk])
                        P.op("act", lambda e: e.activation(out=A["A%d" % dd], in_=pt[:, 0:256], func=AF.Sigmoid, bias=ppc(l, 31 + dd * 2 + hp), scale=1.0), reads=[pk, "PPK"], writes=[AK("A%d" % dd)])
                    P.op("dve", lambda e: e.tensor_scalar(out=A["T1"], in0=A["FK"], scalar1=ppc(l, 35 + hp), scalar2=None, op0=ALU.mult), reads=[AK("FK"), "PPK"], writes=[AK("T1")])
                    P.op("pool", lambda e: e.tensor_tensor(out=A["T2"], in0=A["T1"], in1=A["T1"], op=ALU.mult), reads=[AK("T1")], writes=[AK("T2")])
                    yield
                    pt, pk = psum()
                    P.op("pe", lambda e: e.matmul(pt[:, 0:256], lhsT=ONESBD, rhs=A["T2"], start=True, stop=True), reads=["CST", AK("T2")], writes=[pk])
                    P.op("act", lambda e: e.activation(out=A["T2"], in_=pt[:, 0:256], func=AF.Sqrt, scale=64.0), reads=[pk], writes=[AK("T2")])
                    yield
                    P.op("dve", lambda e: e.tensor_scalar(out=A["T2"], in0=A["T2"], scalar1=1e-12, scalar2=None, op0=ALU.max), reads=[AK("T2")], writes=[AK("T2")])
                    P.op("dve", lambda e: e.reciprocal(out=A["T2"], in_=A["T2"]), reads=[AK("T2")], writes=[AK("T2")])
                    yield
                    P.op("dve", lambda e: e.tensor_tensor(out=A["KKN"], in0=A["T1"], in1=A["T2"], op=ALU.mult), reads=[AK("T1"), AK("T2")], writes=[AK("KKN")])
                    for h in (0, 1):
                        P.op("pool", lambda e, h=h: e.tensor_copy(out=EXP["VB"][64 * h:64 * h + 64, :, 64 * h:64 * h + 64], in_=v3("FV")[64 * h:64 * h + 64]), reads=[AK("FV"), ("EXP", "VB")], writes=[("EXP", "VB")])
                    yield

                def prep_dir(hp, d):
                    pt, pk = psum()
                    P.op("pe", lambda e: e.matmul(pt[:, 0:256], lhsT=W2T[64 * d:64 * d + 64, 0, hp * 128:(hp + 1) * 128], rhs=A["F6"][64 * d:64 * d + 64, :], start=True, stop=True), reads=["W2T", AK("F6")], writes=[pk])
                    P.op("act", lambda e: e.activation(out=A["LD"], in_=pt[:, 0:256], func=AF.Sigmoid, bias=ppc(l, 27 + d * 2 + hp), scale=1.0), reads=[pk, "PPK"], writes=[AK("LD")])
                    Ad = A["A%d" % d]; AdK = AK("A%d" % d)
                    P.op("pool", lambda e: e.tensor_tensor(out=A["BD"], in0=Ad, in1=A["KKN"], op=ALU.mult), reads=[AdK, AK("KKN")], writes=[AK("BD")])
                    yield
                    P.op("dve", lambda e: e.tensor_scalar(out=A["LD"], in0=A["LD"], scalar1=-0.6065306597126334, scalar2=None, op0=ALU.mult), reads=[AK("LD")], writes=[AK("LD")])
                    P.op("dve", lambda e: e.tensor_scalar(out=A["T1"], in0=Ad, scalar1=ppc(l, 37 + hp), scalar2=ppc(l, 37 + hp), op0=ALU.mult, op1=ALU.subtract), reads=[AdK, "PPK"], writes=[AK("T1")])
                    yield
                    P.op("dve", lambda e: e.tensor_tensor_scan(out=A["CF"], data0=RMASK[:, 0:256], data1=A["LD"], initial=0.0, op0=ALU.mult, op1=ALU.add), reads=[AK("LD"), "CST2"], writes=[AK("CF")])
                    P.op("dve", lambda e: e.scalar_tensor_tensor(out=A["KD"], in0=A["T1"], scalar=1.0, in1=A["FK"], op0=ALU.add, op1=ALU.mult), reads=[AK("T1"), AK("FK")], writes=[AK("KD")])
                    yield
                    CF3 = A["CF"].rearrange("p (c t) -> p c t", t=64)
                    P.op("dve", lambda e: e.tensor_copy(out=TOT.rearrange("p (c o) -> p c o", o=1), in_=CF3[:, :, 63:64]), reads=[AK("CF")], writes=["TOT"])
                    yield
                    P.op("act", lambda e: e.activation(out=GAM, in_=TOT, func=AF.Exp), reads=["TOT"], writes=["GAM"])
                    TOTB = TOT.rearrange("p (c o) -> p c o", o=1).to_broadcast([128, 4, 64])
                    if d == 0:
                        P.op("dve", lambda e: e.tensor_tensor(out=A["T2"], in0=A["CF"], in1=A["LD"], op=ALU.subtract), reads=[AK("CF"), AK("LD")], writes=[AK("T2")])
                        P.op("act", lambda e: e.activation(out=A["E2"], in_=A["CF"], func=AF.Exp), reads=[AK("CF")], writes=[AK("E2")])
                        yield
                        P.op("act", lambda e: e.activation(out=A["E1"], in_=A["T2"], func=AF.Exp), reads=[AK("T2")], writes=[AK("E1")])
                        yield
                        P.op("act", lambda e: e.activation(out=A["E3"], in_=A["CF"], func=AF.Exp, scale=-1.0), reads=[AK("CF"), AK("T2")], writes=[AK("E3")])
                    else:
                        P.op("dve", lambda e: e.tensor_tensor(out=v3("T2"), in0=TOTB, in1=v3("CF"), op=ALU.subtract), reads=["TOT", AK("CF")], writes=[AK("T2")])
                        yield
                        P.op("act", lambda e: e.activation(out=A["E1"], in_=A["T2"], func=AF.Exp), reads=[AK("T2")], writes=[AK("E1")])
                        P.op("dve", lambda e: e.tensor_tensor(out=A["CF"], in0=A["T2"], in1=A["LD"], op=ALU.add), reads=[AK("T2"), AK("LD")], writes=[AK("CF")])
                        yield
                        P.op("act", lambda e: e.activation(out=A["E2"], in_=A["CF"], func=AF.Exp), reads=[AK("CF")], writes=[AK("E2")])
                        P.op("act", lambda e: e.activation(out=A["E3"], in_=A["CF"], func=AF.Exp, scale=-1.0), reads=[AK("CF"), AK("T2"), AK("E1")], writes=[AK("E3")])
                    yield
                    i = 0
                    for (n, a, b) in (("QB", "KKN", "E1"), ("RB", "FR", "E2"), ("KB", "KD", "E3"), ("BB", "BD", "E3")):
                        for h in (0, 1):
                            eng = "dve" if i % 2 == 0 else "pool"
                            i += 1
                            P.op(eng, lambda e, n=n, a=a, b=b, h=h: e.tensor_tensor(out=EXP[n][64 * h:64 * h + 64, :, 64 * h:64 * h + 64], in0=v3(a)[64 * h:64 * h + 64], in1=v3(b)[64 * h:64 * h + 64], op=ALU.mult), reads=[AK(a), AK(b), ("EXP", n)], writes=[("EXP", n)])
                        yield

                def units_pre(d):
                    E = lambda n, c: EXP[n][:, c, :]
                    for c in range(4):
                        P.op("pe", lambda e: e.transpose(out=PST[:, c * 128:(c + 1) * 128], in_=E("KB", c), identity=IDB[:]), reads=EK + ["IDB"], writes=["PST"])
                        P.op("pe", lambda e: e.transpose(out=PST[:, (4 + c) * 128:(5 + c) * 128], in_=E("BB", c), identity=IDB[:]), reads=EK + ["IDB"], writes=["PST"])
                    P.op("act", lambda e: e.copy(out=KTBT, in_=PST[:, 0:1024]), reads=["PST"], writes=["KTBT"])
                    for c in range(4):
                        P.op("pe", lambda e: e.transpose(out=PST[:, c * 128:(c + 1) * 128], in_=E("VB", c), identity=IDB[:]), reads=EK + ["IDB"], writes=["PST"])
                    P.op("act", lambda e: e.copy(out=VT4, in_=PST[:, 0:512].rearrange("p (c t) -> p c t", t=128)), reads=["PST"], writes=["VT4"])
                    yield
                    mb = lambda j: MASK(d, j).unsqueeze(1).to_broadcast([128, 4, 128])
                    grams = ((("QB", "BB"), 0, XX[0][:, :, 0, :], ("XX", 0)), (("BB", "QB"), 1, XX[0][:, :, 1, :], ("XX", 0)),
                             (("KB", "QB"), 2, GR["AKK"], "GAKK"), (("KB", "RB"), 3, GR["ARK"], "GARK"), (("BB", "RB"), 3, GR["ARB"], "GARB"))
                    for gi, ((lh, rh), mj, dst, dk) in enumerate(grams):
                        pg, pgk = psum()
                        for c in range(4):
                            P.op("pe", lambda e: e.matmul(pg[:, c * 128:(c + 1) * 128], lhsT=E(lh, c), rhs=E(rh, c), start=True, stop=True), reads=EK, writes=[pgk])
                        pg3 = pg[:, 0:512].rearrange("p (c t) -> p c t", t=128)
                        P.op("dve", lambda e: e.tensor_tensor(out=dst, in0=pg3, in1=mb(mj), op=ALU.mult), reads=[pgk, "CST"], writes=[dk])
                        if gi == 1:
                            P.op("dve", lambda e: e.scalar_tensor_tensor(out=TTM[0], in0=pg3, scalar=-1.0, in1=mb(mj), op0=ALU.mult, op1=ALU.mult), reads=[pgk, "CST"], writes=[("TT", 0)])
                        yield
                    cur = 0
                    for k in range(5):
                        nx = 1 - cur
                        pxs = [psum(), psum()]
                        for c in range(4):
                            px, pxk = pxs[c // 2]
                            cc = c % 2
                            P.op("pe", lambda e: e.matmul(px[:, (2 * cc) * 128:(2 * cc + 1) * 128], lhsT=XX[cur][:, c, 1, :], rhs=XX[cur][:, c, 0, :], start=True, stop=True), reads=[("XX", cur)], writes=[pxk])
                            P.op("pe", lambda e: e.matmul(px[:, (2 * cc + 1) * 128:(2 * cc + 2) * 128], lhsT=XX[cur][:, c, 0, :], rhs=XX[cur][:, c, 1, :], start=True, stop=True), reads=[("XX", cur)], writes=[pxk])
                        for hlf in (0, 1):
                            px, pxk = pxs[hlf]
                            dstv = XX[nx][:, 2 * hlf:2 * hlf + 2, :, :]
                            srcv = px[:, 0:512].rearrange("p (c x t) -> p c x t", x=2, t=128)
                            if hlf == 0:
                                P.op("act", lambda e: e.copy(out=dstv, in_=srcv), reads=[pxk], writes=[("XX", nx, hlf)])
                            else:
                                P.op("act", lambda e: e.copy(out=dstv, in_=srcv), reads=[pxk], writes=[("XX", nx, hlf)])
                        yield
                        pT, pTk = psum()
                        for c in range(4):
                            xk = ("XX", nx, c // 2)
                            P.op("pe", lambda e: e.matmul(pT[:, c * 128:(c + 1) * 128], lhsT=XX[nx][:, c, 0, :], rhs=TTM[cur][:, c, :], start=True, stop=False), reads=[xk, ("TT", cur)], writes=[pTk])
                            P.op("pe", lambda e: e.matmul(pT[:, c * 128:(c + 1) * 128], lhsT=IDB[:], rhs=TTM[cur][:, c, :], start=False, stop=False), reads=["IDB", ("TT", cur)], writes=[pTk])
                            P.op("pe", lambda e: e.matmul(pT[:, c * 128:(c + 1) * 128], lhsT=IDB[:], rhs=XX[nx][:, c, 1, :], start=False, stop=True), reads=["IDB", xk], writes=[pTk])
                        P.op("act", lambda e: e.copy(out=TTM[nx], in_=pT[:, 0:512].rearrange("p (c t) -> p c t", t=128)), reads=[pTk], writes=[("TT", nx)])
                        P.st[P._k(("XX", nx))] = [list(P.st[P._k(("XX", nx, 0))][0]) + list(P.st[P._k(("XX", nx, 1))][0]), []]
                        cur = nx
                        yield
                    return cur

                def chain(d, c, tti, odst, okey, obase):
                    E = lambda n: EXP[n][:, c, :]
                    p3, p3k = psum()
                    P.op("pe", lambda e: e.matmul(p3[:, 0:128], lhsT=E("QB"), rhs=CHB["MB"], start=True, stop=False), reads=EK + ["CMB"], writes=[p3k])
                    P.op("pe", lambda e: e.matmul(p3[:, 0:128], lhsT=GR["AKK"][:, c, :], rhs=VT4[:, c, :], start=False, stop=True), reads=["GAKK", "VT4"], writes=[p3k])
                    P.op("act", lambda e: e.copy(out=CHB["X1B"], in_=p3[:, 0:128]), reads=[p3k], writes=["CX1B"])
                    yield
                    p4, p4k = psum()
                    P.op("pe", lambda e: e.matmul(p4[:, 0:128], lhsT=TTM[tti][:, c, :], rhs=CHB["X1B"], start=True, stop=False), reads=[("TT", tti), "CX1B"], writes=[p4k])
                    P.op("pe", lambda e: e.matmul(p4[:, 0:128], lhsT=IDB[:], rhs=CHB["X1B"], start=False, stop=True), reads=["IDB", "CX1B"], writes=[p4k])
                    P.op("dve", lambda e: e.tensor_scalar(out=CHB["NU"], in0=p4[:, 0:128], scalar1=-1.0, scalar2=None, op0=ALU.mult), reads=[p4k], writes=["CNU"])
                    yield
                    p6, p6k = psum()
                    P.op("pe", lambda e: e.matmul(p6[:, 0:128], lhsT=KT4[:, c, :], rhs=VT4[:, c, :], start=True, stop=False), reads=["KTBT", "VT4"], writes=[p6k])
                    P.op("pe", lambda e: e.matmul(p6[:, 0:128], lhsT=BT4[:, c, :], rhs=CHB["NU"], start=False, stop=True), reads=["KTBT", "CNU"], writes=[p6k])
                    p5, p5k = psum()
                    P.op("pe", lambda e: e.matmul(p5[:, 0:128], lhsT=CHB["MB"], rhs=E("RB"), start=True, stop=False), reads=EK + ["CMB"], writes=[p5k])
                    P.op("pe", lambda e: e.matmul(p5[:, 0:128], lhsT=VT4[:, c, :], rhs=GR["ARK"][:, c, :], start=False, stop=False), reads=["VT4", "GARK"], writes=[p5k])
                    P.op("pe", lambda e: e.matmul(p5[:, 0:128], lhsT=CHB["NU"], rhs=GR["ARB"][:, c, :], start=False, stop=True), reads=["CNU", "GARB"], writes=[p5k])
                    P.op("dve", lambda e: e.tensor_tensor(out=CH["MT"], in0=p6[:, 0:128], in1=CH["M"], op=ALU.add), reads=[p6k, "CM"], writes=["CMT"])
                    yield
                    P.op("act", lambda e: e.activation(out=CHB["MB"], in_=CH["MT"], func=AF.Copy, scale=GAM[:, c:c + 1]), reads=["CMT", "GAM"], writes=["CMB"])
                    P.op("dve", lambda e: e.tensor_scalar(out=CH["M"], in0=CH["MT"], scalar1=GAM[:, c:c + 1], scalar2=None, op0=ALU.mult), reads=["CMT", "GAM"], writes=["CM"])
                    for h in (0, 1):
                        P.op("act", lambda e, h=h: e.copy(out=odst[64 * h:64 * h + 64, obase:obase + 64], in_=p5[64 * h:64 * h + 64, 64 * h:64 * h + 64]), reads=[p5k], writes=[okey])
                    yield

                def finalize(hp, off, s0):
                    OBK = AK("OB")
                    P.op("dve", lambda e: e.tensor_tensor(out=A["OB"], in0=A["OB"], in1=OF[:, s0:s0 + 256], op=ALU.add), reads=[OBK, "OF"], writes=[OBK])
                    pt, pk = psum()
                    P.op("pe", lambda e: e.matmul(pt[:, 0:256], lhsT=ONESBD, rhs=A["OB"], start=True, stop=True), reads=["CST", OBK], writes=[pk])
                    yield
                    P.op("dve", lambda e: e.tensor_tensor(out=A["OB"], in0=A["OB"], in1=pt[:, 0:256], op=ALU.subtract), reads=[pk, OBK], writes=[OBK])
                    P.op("pool", lambda e: e.tensor_tensor(out=A["T1"], in0=A["OB"], in1=A["OB"], op=ALU.mult), reads=[OBK], writes=[AK("T1")])
                    yield
                    pt2, pk2 = psum()
                    P.op("pe", lambda e: e.matmul(pt2[:, 0:256], lhsT=ONESBD, rhs=A["T1"], start=True, stop=True), reads=["CST", AK("T1")], writes=[pk2])
                    P.op("act", lambda e: e.activation(out=A["T2"], in_=pt2[:, 0:256], func=AF.Sqrt, bias=CST2[:, 1601:1602], scale=1.0), reads=[pk2, "CST2"], writes=[AK("T2")])
                    yield
                    P.op("dve", lambda e: e.reciprocal(out=A["T2"], in_=A["T2"]), reads=[AK("T2")], writes=[AK("T2")])
                    P.op("pool", lambda e: e.tensor_tensor(out=A["T1"], in0=A["A0"], in1=A["A1"], op=ALU.add), reads=[AK("A0"), AK("A1"), AK("T1")], writes=[AK("T1")])
                    yield
                    P.op("dve", lambda e: e.tensor_tensor(out=A["OB"], in0=A["OB"], in1=A["T2"], op=ALU.mult), reads=[OBK, AK("T2")], writes=[OBK])
                    P.op("dve", lambda e: e.tensor_scalar(out=A["OB"], in0=A["OB"], scalar1=ppc(l, 41 + hp), scalar2=ppc(l, 43 + hp), op0=ALU.mult, op1=ALU.add), reads=[OBK, "PPK"], writes=[OBK])
                    P.op("dve", lambda e: e.tensor_scalar(out=A["T1"], in0=A["T1"], scalar1=-2.0, scalar2=ppc(l, 37 + hp), op0=ALU.add, op1=ALU.mult), reads=[AK("T1"), "PPK"], writes=[AK("T1")])
                    yield
                    P.op("dve", lambda e: e.scalar_tensor_tensor(out=A["T1"], in0=A["T1"], scalar=2.0, in1=A["FK"], op0=ALU.add, op1=ALU.mult), reads=[AK("T1"), AK("FK")], writes=[AK("T1")])
                    P.op("dve", lambda e: e.scalar_tensor_tensor(out=A["T1"], in0=A["T1"], scalar=ppc(l, 39 + hp), in1=A["FR"], op0=ALU.mult, op1=ALU.mult), reads=[AK("T1"), "PPK", AK("FR")], writes=[AK("T1")])
                    yield
                    pt3, pk3 = psum()
                    P.op("pe", lambda e: e.matmul(pt3[:, 0:256], lhsT=ONESBD, rhs=A["T1"], start=True, stop=True), reads=["CST", AK("T1")], writes=[pk3])
                    P.op("dve", lambda e: e.scalar_tensor_tensor(out=A["BON"], in0=pt3[:, 0:256], scalar=64.0, in1=A["FV"], op0=ALU.mult, op1=ALU.mult), reads=[pk3, AK("FV")], writes=[AK("BON")])
                    yield
                    pt4, pk4 = psum()
                    P.op("pe", lambda e: e.matmul(pt4[:, 0:256], lhsT=G2[:, 0, hp * 128:(hp + 1) * 128], rhs=A["FSG"], start=True, stop=True), reads=["G2", AK("FSG")], writes=[pk4])
                    P.op("dve", lambda e: e.tensor_tensor(out=A["OB"], in0=A["OB"], in1=A["BON"], op=ALU.add), reads=[OBK, AK("BON")], writes=[OBK])
                    yield
                    P.op("dve", lambda e: e.tensor_tensor(out=OCT[:, hp, off + s0:off + s0 + 256], in0=A["OB"], in1=pt4[:, 0:256], op=ALU.mult), reads=[OBK, pk4], writes=[("OCTW", hp)])
                    yield

                def init_state(is_sample, d, hp):
                    P.op("pool", lambda e: e.memset(CH["M"], 0.0), writes=["CM"])
                    if is_sample:
                        for h in (0, 1):
                            P.dma("sp", CH["M"][64 * h:64 * h + 64, 64 * h:64 * h + 64], stT[l, d, hp, 64 * h:64 * h + 64, :], "stin", writes=["CM"])
                    P.op("pool", lambda e: e.tensor_copy(out=CHB["MB"], in_=CH["M"]), reads=["CM"], writes=["CMB"])

                def emit_state(seq_idx, d, hp):
                    pt, pk = psum()
                    P.op("pe", lambda e: e.transpose(out=pt[:, 0:128], in_=CH["M"], identity=IDENT), reads=["CM", "CST"], writes=[pk])
                    P.op("act", lambda e: e.copy(out=CH["MT"], in_=pt[:, 0:128]), reads=[pk], writes=["CMT"])
                    for h in (0, 1):
                        P.dma("sp", nst[l, seq_idx, d, 2 * hp + h], CH["MT"][64 * h:64 * h + 64, 64 * h:64 * h + 64], "ost", reads=["CMT"])

                def job(off, NTs, is_sample, seq_idx, hp):
                    nseg = NTs // 256
                    sweeps = [((0, 1), [0])] if nseg == 1 else [((0,), list(range(nseg))), ((1,), list(reversed(range(nseg))))]
                    for (dirs, segs) in sweeps:
                        if nseg > 1:
                            init_state(is_sample, dirs[0], hp)
                        for sg in segs:
                            s0 = sg * 256
                            yield from prep_shared(hp, off, NTs, s0)
                            for d in dirs:
                                if nseg == 1:
                                    init_state(is_sample, d, hp)
                                yield from prep_dir(hp, d)
                                tti = yield from units_pre(d)
                                for c in ([0, 1, 2, 3] if d == 0 else [3, 2, 1, 0]):
                                    if d == 0:
                                        yield from chain(d, c, tti, OF, "OF", s0 + 64 * c)
                                    else:
                                        yield from chain(d, c, tti, A["OB"], AK("OB"), 64 * c)
                                if nseg == 1 and not is_sample:
                                    emit_state(seq_idx, d, hp)
                                    yield
                                if d == 1:
                                    yield from finalize(hp, off, s0)
                        if nseg > 1 and not is_sample:
                            emit_state(seq_idx, dirs[0], hp)
                            yield
                return job

            npsum[0] = 7
            jobs = [make_lane(0), make_lane(1)]
            assert o[0] <= 14336, o[0]

            def lane_stream(li):
                for (off, NTs, is_sample, seq_idx) in pass_seqs:
                    yield from jobs[li](off, NTs, is_sample, seq_idx, li)

            gens = [(0, lane_stream(0)), (1, lane_stream(1))]
            if bg is not None:
                gens.append((2, bg))
            while gens:
                for (li, g) in list(gens):
                    P.lane = li
                    try:
                        next(g)
                    except StopIteration:
                        gens.remove((li, g))
            P.lane = None
            npsum[0] = 5
            P.barrier()

        def attn_phase(l, tiles, NT, is_sample, pass_seqs):
            NTT = NT // 128
            nrow = NT // 64
            o = [0]
            def arrb(n):
                v = AR[:, o[0]:o[0] + n // 2].bitcast(BF16)
                o[0] += n // 2
                return v
            def arrf(n):
                v = AR[:, o[0]:o[0] + n]
                o[0] += n
                return v
            NKC = 512 if is_sample else 0
            QKT = arrb(4 * NT).rearrange("p (c t) -> p c t", t=NT)
            GQT = arrb(4 * NT).rearrange("p (c t) -> p c t", t=NT)
            GK2 = arrb(2 * (NT + NKC)).rearrange("p (c t) -> p c t", t=NT + NKC)
            VNA = arrb(nrow * 4 * 64).rearrange("p (r h c) -> p r h c", h=4, c=64)
            VG = arrb((NTT + NKC // 128) * 2 * 64).rearrange("p (t h c) -> p t h c", h=2, c=64)
            tok_off = o[0]
            TOK = arrf(1280)
            TK2 = arrf(768)
            RS = arrf(16)
            PT = [arrb(512) for _ in range(3)]
            SBs = [arrf(512), MODB[:, 0:512]]
            sbrr = [0]
            RC = MODB[:, 512:1024]
            if is_sample:
                KCT = arrb(2 * 512).rearrange("p (c t) -> p c t", t=512)
                VCN = arrb(4 * 4 * 64).rearrange("p (t h c) -> p t h c", h=4, c=64)
                BR = arrf(960)
            assert o[0] <= 14336, o[0]
            ptrr = [0]
            def ptile():
                i = ptrr[0] % 3
                ptrr[0] += 1
                return PT[i], ("PT", i)

            P.dma("sp", QKN[:], qkn[l:l + 1, :].partition_broadcast(128)[:, 0, :], "c0", writes=["QKN"])
            if is_sample:
                for t in range(4):
                    P.dma("sp", TOK[:, 0:256], cnak[l, t * 128:(t + 1) * 128, :], "ctx", writes=["TOK"])
                    P.op("pool", lambda e: e.tensor_copy(out=TMPB[:, 0:256], in_=TOK[:, 0:256]), reads=["TOK"], writes=["TMPB"])
                    for cc in range(2):
                        P.op("pe", lambda e, cc=cc: e.transpose(out=PST[:, cc * 128:(cc + 1) * 128], in_=TMPB[:, cc * 128:(cc + 1) * 128], identity=IDB[:]), reads=["TMPB", "IDB"], writes=["PST"])
                    P.op("act", lambda e, t=t: e.copy(out=KCT[:, :, t * 128:(t + 1) * 128], in_=PST[:, 0:256].rearrange("p (c t) -> p c t", t=128)), reads=["PST"], writes=["KCT"])
                    P.dma("sp", TOK[:, 256:512], cnav[l, t * 128:(t + 1) * 128, :], "ctx", writes=["TOK2"])
                    P.op("pool", lambda e, t=t: e.tensor_copy(out=VCN[:, t, :, :], in_=TOK[:, 256:512].rearrange("p (h c) -> p h c", c=64)), reads=["TOK2"], writes=["VCN"])
                    P.dma("sp", TOK[:, 512:640], cgk[l, t * 128:(t + 1) * 128, :], "ctx", writes=["TOK3"])
                    for kv in (0, 1):
                        P.op("pool", lambda e, kv=kv: e.tensor_copy(out=TMPB[:, 256 + 128 * kv:384 + 128 * kv].rearrange("p (r c) -> p r c", c=64), in_=TOK[:, 512 + 64 * kv:576 + 64 * kv].unsqueeze(1).to_broadcast([128, 2, 64])), reads=["TOK3"], writes=["TMPB2"])
                    for kv in (0, 1):
                        P.op("pe", lambda e, kv=kv: e.transpose(out=PST[:, 256 + 128 * kv:384 + 128 * kv], in_=TMPB[:, 256 + 128 * kv:384 + 128 * kv], identity=IDB[:]), reads=["TMPB2", "IDB"], writes=["PST2"])
                    P.op("act", lambda e, t=t: e.copy(out=GK2[:, :, t * 128:(t + 1) * 128], in_=PST[:, 256:512].rearrange("p (c t) -> p c t", t=128)), reads=["PST2"], writes=["GKT"])
                    P.dma("sp", TOK[:, 640:768], cgv[l, t * 128:(t + 1) * 128, :], "ctx", writes=["TOK4"])
                    P.op("pool", lambda e, t=t: e.tensor_copy(out=VG[:, t, :, :], in_=TOK[:, 640:768].rearrange("p (h c) -> p h c", c=64)), reads=["TOK4"], writes=["VG"])
                P.barrier()

            SUB = int(os.environ.get("ATT_SUB", "9"))
            if SUB < 1:
                P.barrier(); return
            wv, wk = load_w(w_in[l], 0, 8, 1152, 512)
            for cc in range(4):
                for g0 in range(0, NT, 512):
                    gn = min(512, NT - g0)
                    pt, pk = psum()
                    for k in range(8):
                        P.op("pe", lambda e, pt=pt, k=k, cc=cc, g0=g0, gn=gn: e.matmul(pt[:, 0:gn], lhsT=wv[:, k, cc * 128:(cc + 1) * 128], rhs=XMT[:, k, g0:g0 + gn], start=(k == 0), stop=(k == 7)), reads=[wk, "XMT"], writes=[pk])
                    P.op("act", lambda e, pt=pt, cc=cc, g0=g0, gn=gn: e.copy(out=QKT[:, cc, g0:g0 + gn], in_=pt[:, 0:gn]), reads=[pk], writes=["QKT"])
            if SUB < 2:
                P.barrier(); return
            wa, wak = load_w(w_in[l], 0, 8, 1408, 512)
            wb, wbk = load_w(w_in[l], 0, 8, 1920, 512)
            wc, wck = load_w(w_in[l], 0, 8, 2432, 256)
            LB = [dict(TOK=TOK, TK2=TK2, RS=RS, TMPB=TMPB),
                  dict(TOK=LNB[:, 0:1280], TK2=LNB[:, 1280:2048], RS=MODB[:, 1024:1040], TMPB=TMPF[:].bitcast(BF16)[:, 0:1024])]

            def tile_job(i, tt, li):
                TOK_ = LB[li]["TOK"]; TK2_ = LB[li]["TK2"]; RS_ = LB[li]["RS"]; TMPB_ = LB[li]["TMPB"]
                kx = lambda k: k + "_%d" % li
                pa, pak = psum()
                for k in range(8):
                    P.op("pe", lambda e, k=k: e.matmul(pa[:, 0:512], lhsT=XMT[:, k, i * 128:(i + 1) * 128], rhs=wa[:, k, :], start=(k == 0), stop=(k == 7)), reads=[wak, "XMT"], writes=[pak])
                P.op("act", lambda e: e.copy(out=TOK_[:, 0:512], in_=pa[:, 0:512]), reads=[pak], writes=[kx("TOK")])
                pb, pbk = psum()
                for k in range(8):
                    P.op("pe", lambda e, k=k: e.matmul(pb[:, 0:512], lhsT=XMT[:, k, i * 128:(i + 1) * 128], rhs=wb[:, k, :], start=(k == 0), stop=(k == 7)), reads=[wbk, "XMT"], writes=[pbk])
                P.op("act", lambda e: e.copy(out=TOK_[:, 512:1024], in_=pb[:, 0:512]), reads=[pbk], writes=[kx("TOK2")])
                pc, pck = psum()
                for k in range(8):
                    P.op("pe", lambda e, k=k: e.matmul(pc[:, 0:256], lhsT=XMT[:, k, i * 128:(i + 1) * 128], rhs=wc[:, k, :], start=(k == 0), stop=(k == 7)), reads=[wck, "XMT"], writes=[pck])
                P.op("act", lambda e: e.copy(out=TOK_[:, 1024:1280], in_=pc[:, 0:256]), reads=[pck], writes=[kx("TOK3")])
                for rr in (0, 1):
                    pv_, pvk = psum()
                    for k in range(8):
                        P.op("pe", lambda e, k=k: e.matmul(pv_[0:64, 0:256], lhsT=XMT[:, k, i * 128 + rr * 64:i * 128 + rr * 64 + 64], rhs=wa[:, k, 256:512], start=(k == 0), stop=(k == 7)), reads=[wak, "XMT"], writes=[pvk])
                    P.op("act", lambda e: e.copy(out=VNA[0:64, 2 * i + rr, :, :], in_=pv_[0:64, 0:256].rearrange("p (h c) -> p h c", c=64)), reads=[pvk], writes=[("VNA", 2 * i + rr)])
                yield
                QK = TOK_[:, 512:1152].rearrange("p (h c) -> p h c", c=64)
                T2v = TK2_[:, 0:640].rearrange("p (h c) -> p h c", c=64)
                TK = [kx("TOK2"), kx("TOK3")]
                P.op("dve", lambda e: e.tensor_tensor(out=T2v, in0=QK, in1=QK, op=ALU.mult), reads=TK, writes=[kx("TK2")])
                P.op("dve", lambda e: e.tensor_reduce(out=RS_[:, 0:10], in_=T2v, axis=mybir.AxisListType.X, op=ALU.add), reads=[kx("TK2")], writes=[kx("RS")])
                yield
                P.op("act", lambda e: e.activation(out=RS_[:, 0:10], in_=RS_[:, 0:10], func=AF.Sqrt, bias=CST2[:, 1600:1601], scale=1.0 / 64.0), reads=[kx("RS"), "CST2"], writes=[kx("RS")])
                yield
                P.op("dve", lambda e: e.reciprocal(out=RS_[:, 0:10], in_=RS_[:, 0:10]), reads=[kx("RS")], writes=[kx("RS")])
                yield
                P.op("dve", lambda e: e.tensor_tensor(out=QK, in0=QK, in1=RS_[:, 0:10].unsqueeze(2).to_broadcast([128, 10, 64]), op=ALU.mult), reads=[kx("RS")] + TK, writes=TK)
                P.op("dve", lambda e: e.tensor_tensor(out=QK[:, 0:8, :], in0=QK[:, 0:8, :], in1=QKN[:, 0:64].unsqueeze(1).to_broadcast([128, 8, 64]), op=ALU.mult), reads=["QKN"] + TK, writes=TK)
                P.op("dve", lambda e: e.tensor_tensor(out=QK[:, 8:10, :], in0=QK[:, 8:10, :], in1=QKN[:, 64:128].unsqueeze(1).to_broadcast([128, 2, 64]), op=ALU.mult), reads=["QKN"] + TK, writes=TK)
                yield
                if not is_sample:
                    r0 = i * 128
                    P.dma("sp", nak[l, r0:r0 + 128, :], TOK_[:, 0:256], "oc", reads=[kx("TOK")])
                    P.dma("sp", nav[l, r0:r0 + 128, :], TOK_[:, 256:512], "oc", reads=[kx("TOK")])
                    P.dma("sp", ngk[l, r0:r0 + 128, :], TOK_[:, 1024:1152], "oc", reads=[kx("TOK3")])
                    P.dma("sp", ngv[l, r0:r0 + 128, :], TOK_[:, 1152:1280], "oc", reads=[kx("TOK3")])
                else:
                    Q4 = TOK_[:, 512:1152].rearrange("p (h a c) -> p h a c", a=4, c=16)
                    T4 = TK2_[:, 0:640].rearrange("p (h a c) -> p h a c", a=4, c=16)
                    rp = ROPE[:, i, :]
                    for ax in (0, 1):
                        cosb = rp[:, 16 * ax:16 * ax + 16].unsqueeze(1).to_broadcast([128, 10, 16])
                        sinb = rp[:, 32 + 16 * ax:48 + 16 * ax].unsqueeze(1).to_broadcast([128, 10, 16])
                        x1 = Q4[:, :, 2 * ax, :]; x2 = Q4[:, :, 2 * ax + 1, :]
                        t1 = T4[:, :, 0, :]; t2 = T4[:, :, 1, :]
                        P.op("dve", lambda e: e.tensor_tensor(out=t1, in0=x1, in1=sinb, op=ALU.mult), reads=TK + ["CST2"], writes=[kx("TK2")])
                        P.op("dve", lambda e: e.tensor_tensor(out=t2, in0=x2, in1=sinb, op=ALU.mult), reads=TK + ["CST2"], writes=[kx("TK2")])
                        yield
                        P.op("dve", lambda e: e.tensor_tensor(out=x1, in0=x1, in1=cosb, op=ALU.mult), reads=TK + ["CST2", kx("TK2")], writes=TK)
                        P.op("dve", lambda e: e.tensor_tensor(out=x2, in0=x2, in1=cosb, op=ALU.mult), reads=TK + ["CST2", kx("TK2")], writes=TK)
                        yield
                        P.op("dve", lambda e: e.tensor_tensor(out=x1, in0=x1, in1=t2, op=ALU.subtract), reads=TK + [kx("TK2")], writes=TK)
                        P.op("dve", lambda e: e.tensor_tensor(out=x2, in0=x2, in1=t1, op=ALU.add), reads=TK + [kx("TK2")], writes=TK)
                        yield
                P.op("pool", lambda e: e.tensor_copy(out=TMPB_[:, 0:512], in_=TOK_[:, 512:1024]), reads=TK, writes=[kx("TMPB")])
                for kv in (0, 1):
                    P.op("pool", lambda e, kv=kv: e.tensor_copy(out=TMPB_[:, 512 + 128 * kv:640 + 128 * kv].rearrange("p (r c) -> p r c", c=64), in_=TOK_[:, 1024 + 64 * kv:1088 + 64 * kv].unsqueeze(1).to_broadcast([128, 2, 64])), reads=TK, writes=[kx("TMPB")])
                P.op("pool", lambda e: e.tensor_copy(out=VG[:, NKC // 128 + i, :, :], in_=TOK_[:, 1152:1280].rearrange("p (h c) -> p h c", c=64)), reads=[kx("TOK3")], writes=[("VG", i)])
                yield
                for cc in range(6):
                    P.op("pe", lambda e, cc=cc: e.transpose(out=PST[:, cc * 128:(cc + 1) * 128], in_=TMPB_[:, cc * 128:(cc + 1) * 128], identity=IDB[:]), reads=[kx("TMPB"), "IDB"], writes=["PST"])
                P.op("act", lambda e: e.copy(out=GQT[:, :, i * 128:(i + 1) * 128], in_=PST[:, 0:512].rearrange("p (c t) -> p c t", t=128)), reads=["PST"], writes=[("GQT", i)])
                P.op("act", lambda e: e.copy(out=GK2[:, :, NKC + i * 128:NKC + (i + 1) * 128], in_=PST[:, 512:768].rearrange("p (c t) -> p c t", t=128)), reads=["PST"], writes=[("GKT", i)])
                yield

            def tl_stream(li):
                for i, tt in enumerate(tiles):
                    if i % 2 == li:
                        yield from tile_job(i, tt, li)
            gens = [tl_stream(0), tl_stream(1)]
            while gens:
                for g in list(gens):
                    try:
                        next(g)
                    except StopIteration:
                        gens.remove(g)
            for base, cnt_ in (("GQT", NTT), ("GKT", NTT), ("VG", NTT), ("VNA", 2 * NTT)):
                evs = list(P.st.get(base, [[], []])[0])
                for i_ in range(cnt_):
                    evs += list(P.st.get((base, i_), [[], []])[0])
                P.st[base] = [evs, []]
            if SUB < 3:
                P.barrier(); return
            PRE["wout"] = (load_w(w_out[l], 0, 8, 0, 512), load_w(w_out[l], 0, 8, 512, 512))
            nvt = NTT + NKC // 128
            VGA = AR[:, tok_off:tok_off + nvt * 128].bitcast(BF16).rearrange("p (t h c) -> p t h c", h=2, c=128)
            tl0 = ["TOK_0", "TOK2_0", "TOK3_0", "TK2_0"]
            P.op("pool", lambda e: e.memset(VGA[:, :, :, 64:128], 1.0), writes=["VGA1"] + tl0)
            P.op("act", lambda e: e.copy(out=VGA[:, :, :, 0:64], in_=VG), reads=["VG"], writes=["VGA"] + tl0)
            def finish_head(acc, acck, acs, acsk, chunk, half, q0, qn):
                P.op("dve", lambda e: e.reciprocal(out=RC[0:64, 0:qn], in_=acs[0:64, 0:qn]), reads=[acsk], writes=["RC"])
                P.op("dve", lambda e: e.tensor_tensor(out=OCT[64 * half:64 * half + 64, chunk, q0:q0 + qn], in0=acc[0:64, 0:qn], in1=RC[0:64, 0:qn], op=ALU.mult), reads=[acck, "RC"], writes=["OCT"])

            def score_block(qT, kT, nk, qn, bias=None):
                ps_, psk = psum()
                P.op("pe", lambda e: e.matmul(ps_[0:nk, 0:qn], lhsT=kT, rhs=qT, start=True, stop=True), reads=["QKT", "GQT", "GKT", "KCT"], writes=[psk])
                pt_, ptk = ptile()
                if bias is None:
                    P.op("act", lambda e: e.activation(out=pt_[0:nk, 0:qn], in_=ps_[0:nk, 0:qn], func=AF.Exp, scale=0.125), reads=[psk], writes=[ptk])
                else:
                    si = sbrr[0] % 2
                    sbrr[0] += 1
                    SB = SBs[si]
                    P.op("dve", lambda e: e.scalar_tensor_tensor(out=SB[0:nk, 0:qn], in0=ps_[0:nk, 0:qn], scalar=0.125, in1=bias, op0=ALU.mult, op1=ALU.add), reads=[psk, "BR"], writes=[("SB", si)])
                    P.op("act", lambda e: e.activation(out=pt_[0:nk, 0:qn], in_=SB[0:nk, 0:qn], func=AF.Exp), reads=[("SB", si)], writes=[ptk])
                return pt_, ptk

            VK = ["VNA", "VG", "VCN", "ONESB"]

            def pv(acc, acck, acs, acsk, vT, nk, pt_, ptk, c0, n, first):
                P.op("pe", lambda e: e.matmul(acc[0:64, c0:c0 + n], lhsT=vT, rhs=pt_[0:nk, 0:n], start=first, stop=False), reads=VK + [ptk], writes=[acck])
                P.op("pe", lambda e: e.matmul(acs[0:64, c0:c0 + n], lhsT=ONESB[0:nk, :], rhs=pt_[0:nk, 0:n], start=first, stop=False), reads=VK + [ptk], writes=[acsk])

            if is_sample:
                qblocks = [(0, 512, 0, 16), (512, 512, 0, 16)]
            else:
                qblocks = [(off, NTs, off // 64, (off + NTs) // 64) for (off, NTs, _s, _i) in pass_seqs]
            def pv_aug(acc, acck, vT, nk, pt_, ptk, c0, n, first):
                P.op("pe", lambda e: e.matmul(acc[:, c0:c0 + n], lhsT=vT, rhs=pt_[0:nk, 0:n], start=first, stop=False), reads=["VGA", "VGA1", ptk], writes=[acck])

            def finish_head_aug(acc, acck, chunk, half, q0, qn):
                P.op("dve", lambda e: e.reciprocal(out=RC[64:128, 0:qn], in_=acc[64:128, 0:qn]), reads=[acck], writes=["RC"])
                P.op("dve", lambda e: e.tensor_tensor(out=OCT[64 * half:64 * half + 64, chunk, q0:q0 + qn], in0=acc[0:64, 0:qn], in1=RC[64:128, 0:qn], op=ALU.mult), reads=[acck, "RC"], writes=["OCT"])

            def run_blocks(blocks, acc, acck, acs, acsk, aug=False):
                sc = {}
                LA = 2
                for i0_ in range(min(LA, len(blocks))):
                    b = blocks[i0_]
                    sc[i0_] = score_block(b[0], b[1], b[2], b[3], bias=b[4])
                for i, b in enumerate(blocks):
                    if i + LA < len(blocks):
                        nb = blocks[i + LA]
                        sc[i + LA] = score_block(nb[0], nb[1], nb[2], nb[3], bias=nb[4])
                    pt_, ptk = sc.pop(i)
                    if aug:
                        pv_aug(acc, acck, b[5], b[2], pt_, ptk, b[6], b[3], i == 0)
                    else:
                        pv(acc, acck, acs, acsk, b[5], b[2], pt_, ptk, b[6], b[3], i == 0)

            for h in range(4):
                hb = 64 * (h % 2)
                qch = h // 2
                kch = 2 + h // 2
                if is_sample:
                    P.dma("sp", BR[0:64, :], nab[l, h], "ctx", writes=["BR"])
                    P.op("dve", lambda e: e.tensor_tensor(out=BR[0:64, :].rearrange("p (a c) -> p a c", c=64), in0=BR[0:64, :].rearrange("p (a c) -> p a c", c=64), in1=COLM.unsqueeze(1).to_broadcast([64, 15, 64]), op=ALU.add), reads=["BR", "CST2"], writes=["BR"])
                for (q0, qn, kr0, kr1) in qblocks:
                    acc, acck = psacc(0)
                    acs, acsk = psacc(1)
                    qT = QKT[hb:hb + 64, qch, q0:q0 + qn]
                    blocks = []
                    if is_sample:
                        for t in range(4):
                            blocks.append((qT, KCT[hb:hb + 64, h // 2, t * 128:(t + 1) * 128], 128, qn, None, VCN[:, t, h, :], 0))
                        rows = range(q0 // 64, (q0 + qn) // 64)
                        for j in range(16):
                            att = [r for r in rows if min(max(r - 4, 0), 8) <= j <= min(max(r - 4, 0), 8) + 7]
                            if not att:
                                continue
                            rlo, rhi = att[0], att[-1]
                            nr = rhi - rlo + 1
                            e0 = rlo - j + 7
                            c0 = (rlo - q0 // 64) * 64
                            blocks.append((QKT[hb:hb + 64, qch, rlo * 64:(rhi + 1) * 64], QKT[hb:hb + 64, kch, j * 64:(j + 1) * 64], 64, nr * 64,
                                           BR[0:64, e0 * 64:(e0 + nr) * 64], VNA[0:64, j, h, :], c0))
                    else:
                        for j in range(kr0, kr1):
                            blocks.append((qT, QKT[hb:hb + 64, kch, j * 64:(j + 1) * 64], 64, qn, None, VNA[0:64, j, h, :], 0))
                    run_blocks(blocks, acc, acck, acs, acsk)
                    finish_head(acc, acck, acs, acsk, 2 + h // 2, h % 2, q0, qn)
            for h in range(8):
                hb = 64 * (h % 2)
                kvh = h // 4
                for (q0, qn, kr0, kr1) in qblocks:
                    acc, acck = psacc(0)
                    acs, acsk = psacc(1)
                    qT = GQT[hb:hb + 64, h // 2, q0:q0 + qn]
                    kts = list(range(NKC // 128)) + [NKC // 128 + t for t in range(kr0 // 2, kr1 // 2)]
                    blocks = [(qT, GK2[hb:hb + 64, kvh, t * 128:(t + 1) * 128], 128, qn, None, VGA[:, t, kvh, :], 0) for t in kts]
                    run_blocks(blocks, acc, acck, acs, acsk, aug=True)
                    finish_head_aug(acc, acck, 4 + h // 2, h % 2, q0, qn)
            P.barrier()

        def dense_out(l, tiles, NT, ci):
            load_mod(l, ci, 2048, 1024, MODB[:, 0:1024])
            load_ln(l, ln1w, ln1b)
            if "wout" in PRE:
                (wv0, wk0), (wv1, wk1) = PRE.pop("wout")
            else:
                wv0, wk0 = load_w(w_out[l], 0, 8, 0, 512)
                wv1, wk1 = load_w(w_out[l], 0, 8, 512, 512)
            for i, tt in enumerate(tiles):
                for (wv, wk, n0) in ((wv0, wk0, 0), (wv1, wk1, 512)):
                    pt, pk = psum()
                    for k in range(8):
                        P.op("pe", lambda e, k=k, i=i, pt=pt, wv=wv: e.matmul(pt[:, 0:512], lhsT=OCT[:, k, i * 128:(i + 1) * 128], rhs=wv[:, k, :], start=(k == 0), stop=(k == 7)), reads=[wk, "OCT"], writes=[pk])
                    P.op("dve", lambda e, pt=pt, n0=n0: e.tensor_tensor(out=TMPF[:, n0:n0 + 512], in0=pt[:, 0:512], in1=MODB[:, n0:n0 + 512], op=ALU.mult), reads=[pk, "MODB"], writes=["TMPF"])
                    P.op("dve", lambda e, tt=tt, n0=n0: e.scalar_tensor_tensor(out=X[:, tt, n0:n0 + 512], in0=X[:, tt, n0:n0 + 512], scalar=ALPHA, in1=TMPF[:, n0:n0 + 512], op0=ALU.mult, op1=ALU.add), reads=["TMPF", ("X", tt)], writes=[("X", tt)])
                layer_norm(tt, l)

        def ffn(l, tiles, NT, ci):
            load_mod(l, ci, 3072, 2048, MODB[:, 0:2048])
            pre_g = load_w(w_fi[l], 0, 8, 0, 512)
            pre_u = load_w(w_fi[l], 0, 8, DFF, 512)
            modulate_transpose(tiles, MODB[:, 0:1024], MODB[:, 1024:2048])
            load_mod(l, ci, 5120, 1024, MODB[:, 0:1024])
            load_ln(l, ln2w, ln2b)
            HT = AR[:, 0:11 * NT // 2].bitcast(BF16).rearrange("p (j t) -> p j t", t=NT)
            GS = AR[:, 6000:6512]
            for half in range(2):
                for jb in range(3):
                    j0 = half * 11 + jb * 4
                    nj = min(4, half * 11 + 11 - j0)
                    if half == 0 and jb == 0:
                        (wg, wgk), (wu, wuk) = pre_g, pre_u
                    else:
                        wg, wgk = load_w(w_fi[l], 0, 8, j0 * 128, nj * 128)
                        wu, wuk = load_w(w_fi[l], 0, 8, DFF + j0 * 128, nj * 128)
                    for jj in range(nj):
                        for g0 in range(0, NT, 512):
                            gn = min(512, NT - g0)
                            pg, pgk = psum()
                            for k in range(8):
                                P.op("pe", lambda e, k=k, pg=pg, jj=jj, g0=g0, gn=gn: e.matmul(pg[:, 0:gn], lhsT=wg[:, k, jj * 128:(jj + 1) * 128], rhs=XMT[:, k, g0:g0 + gn], start=(k == 0), stop=(k == 7)), reads=[wgk, "XMT"], writes=[pgk])
                            pu, puk = psum()
                            for k in range(8):
                                P.op("pe", lambda e, k=k, pu=pu, jj=jj, g0=g0, gn=gn: e.matmul(pu[:, 0:gn], lhsT=wu[:, k, jj * 128:(jj + 1) * 128], rhs=XMT[:, k, g0:g0 + gn], start=(k == 0), stop=(k == 7)), reads=[wuk, "XMT"], writes=[puk])
                            P.op("act", lambda e, pg=pg, gn=gn: e.activation(out=GS[:, 0:gn], in_=pg[:, 0:gn], func=AF.Silu), reads=[pgk], writes=["GS"])
                            P.op("dve", lambda e, pu=pu, gn=gn, g0=g0, jj=jj, jb=jb: e.tensor_tensor(out=HT[:, jb * 4 + jj, g0:g0 + gn], in0=GS[:, 0:gn], in1=pu[:, 0:gn], op=ALU.mult), reads=[puk, "GS"], writes=["HT"])
                for n0 in (0, 512):
                    wa_, wak_ = load_w(w_fo[l], half * 11, 8, n0, 512)
                    wb_, wbk_ = load_w(w_fo[l], half * 11 + 8, 3, n0, 512)
                    for i, tt in enumerate(tiles):
                        pt, pk = psum()
                        for j in range(11):
                            wv, wk, jj = (wa_, wak_, j) if j < 8 else (wb_, wbk_, j - 8)
                            P.op("pe", lambda e, j=j, jj=jj, wv=wv, pt=pt, i=i: e.matmul(pt[:, 0:512], lhsT=HT[:, j, i * 128:(i + 1) * 128], rhs=wv[:, jj, :], start=(j == 0), stop=(j == 10)), reads=[wk, "HT"], writes=[pk])
                        P.op("dve", lambda e, pt=pt, n0=n0: e.tensor_tensor(out=TMPF[:, n0:n0 + 512], in0=pt[:, 0:512], in1=MODB[:, n0:n0 + 512], op=ALU.mult), reads=[pk, "MODB"], writes=["TMPF"])
                        if half == 0:
                            P.op("dve", lambda e, tt=tt, n0=n0: e.scalar_tensor_tensor(out=X[:, tt, n0:n0 + 512], in0=X[:, tt, n0:n0 + 512], scalar=ALPHA, in1=TMPF[:, n0:n0 + 512], op0=ALU.mult, op1=ALU.add), reads=["TMPF", ("X", tt)], writes=[("X", tt)])
                        else:
                            P.op("dve", lambda e, tt=tt, n0=n0: e.tensor_tensor(out=X[:, tt, n0:n0 + 512], in0=X[:, tt, n0:n0 + 512], in1=TMPF[:, n0:n0 + 512], op=ALU.add), reads=["TMPF", ("X", tt)], writes=[("X", tt)])
            for tt in tiles:
                layer_norm(tt, l)

        seqs = [([0, 1, 2, 3], 0, False, [(0, 256, False, 0), (256, 256, False, 1)]), (list(range(4, 12)), 1, True, [(0, 1024, True, 0)])]
        if os.environ.get("SEQS"):
            seqs = [seqs[int(c)] for c in os.environ["SEQS"]]
        for l in range(depth):
            P.barrier()
            P.dma("sp", W2T[:, 0, :], w2[l], "c0", writes=["W2T"])
            P.dma("sp", A2T[:, 0, :], a2[l], "c0", writes=["A2T"])
            P.dma("sp", G2[:, 0, :], g2[l], "c0", writes=["G2"])
            for (tiles, ci, is_sample, pass_seqs) in seqs:
                NT = 128 * len(tiles)
                if stage < 1:
                    continue
                load_mod(l, ci, 0, 2048, MODB[:, 0:2048])
                if stage >= 2:
                    WR_ = WS[:].rearrange("p s c -> p (s c)")[:, 0:8 * 1152].rearrange("p (k c) -> p k c", c=1152)
                    P.dma("pool", WR_, w_in[l].rearrange("(k p) c -> p k c", p=128)[:, :, 0:1152], "ws0", writes=[("ws", 0), ("ws", 1), ("ws", 2)])
                modulate_transpose(tiles, MODB[:, 0:1024], MODB[:, 1024:2048])
                P.barrier()
                if stage >= 2:
                    rwkv_phase(l, NT, pass_seqs, bg=(mods_gen(l + 1) if (is_sample and l + 1 < depth) else None))
                if stage >= 3:
                    attn_phase(l, tiles, NT, is_sample, pass_seqs)
                if stage >= 4:
                    dense_out(l, tiles, NT, ci)
                if stage >= 5:
                    ffn(l, tiles, NT, ci)
        for t in range(12):
            P.dma("sp", y[t * 128:(t + 1) * 128, :], X[:, t, :], "yo", reads=[("X", t)])
        P.barrier()
        P.emit()
    return nc


def _consts():
    cst = np.zeros((128, 128 * 12), np.float32)
    cst[:, 0:128] = np.eye(128, dtype=np.float32)
    bd = np.zeros((128, 128), np.float32)
    bd[0:64, 0:64] = 1.0; bd[64:128, 64:128] = 1.0
    cst[:, 128:256] = bd / 64.0
    i = np.arange(64)[:, None]; j = np.arange(64)[None, :]
    SL = (i > j).astype(np.float32); SU = (i < j).astype(np.float32)
    LI = (i >= j).astype(np.float32); UI = (i <= j).astype(np.float32)
    def blk(m):
        o = np.zeros((128, 128), np.float32); o[0:64, 0:64] = m; o[64:, 64:] = m; return o
    per = {0: [SL, SU, SU, UI, UI], 1: [SU, SL, SL, LI, LI]}
    for d in (0, 1):
        for k, m in enumerate(per[d]):
            cst[:, 256 + (d * 5 + k) * 128: 256 + (d * 5 + k + 1) * 128] = blk(m)
    cst2 = np.zeros((128, 1603), np.float32)
    rm = np.ones(1024, np.float32); rm[::64] = 0.0
    cst2[:, 0:1024] = rm[None, :]
    t = np.arange(1024)
    inv = 10000.0 ** (-np.arange(16, dtype=np.float32) / 16.0)
    angr = (t // 64).astype(np.float32)[:, None] * inv[None, :]
    angc = (t % 64).astype(np.float32)[:, None] * inv[None, :]
    tab = np.concatenate([np.cos(angr), np.cos(angc), np.sin(angr), np.sin(angc)], 1).astype(np.float32)
    cst2[:, 1024:1536] = tab.reshape(8, 128, 64).transpose(1, 0, 2).reshape(128, 512)
    cq = np.arange(64)[None, :]; ck = np.arange(64)[:, None]
    c0 = np.clip(cq - 8, 0, 48)
    inwin = (ck >= c0) & (ck < c0 + 16)
    cst2[0:64, 1536:1600] = np.where(inwin, 0.0, NEG).astype(np.float32)
    cst2[:, 1600] = RMS_EPS; cst2[:, 1601] = GN_EPS; cst2[:, 1602] = LN_EPS
    return cst, cst2


_NC_CACHE = {}


def _prep(inp):
    f = lambda k: np.ascontiguousarray(np.asarray(inp[k], dtype=np.float32))
    cst, cst2 = _consts()
    x_prompt = f("x_prompt"); x_sample = f("x_sample")
    conv = f("rwkv_conv"); w0 = f("rwkv_w0"); a0 = f("rwkv_a0")
    kk_ = f("rwkv_k_k"); ka_ = f("rwkv_k_a"); rk_ = f("rwkv_r_k").reshape(NL, 256)
    lw_ = f("rwkv_lnx_w"); lb_ = f("rwkv_lnx_b")
    pp = np.zeros((128, NL * PPL), np.float32)
    for l in range(NL):
        b = l * PPL
        pp[:, b:b + 27] = conv[l].reshape(3, 9, 128).transpose(2, 0, 1).reshape(128, 27)
        pp[:, b + 27:b + 31] = w0[l].reshape(2, 2, 128).transpose(2, 0, 1).reshape(128, 4)
        pp[:, b + 31:b + 35] = a0[l].reshape(2, 2, 128).transpose(2, 0, 1).reshape(128, 4)
        for j, arr in enumerate((kk_, ka_, rk_, lw_, lb_)):
            pp[:, b + 35 + 2 * j:b + 37 + 2 * j] = arr[l].reshape(2, 128).T
    rpb = f("na_rpb")
    ck = np.arange(64)[:, None]; cq = np.arange(64)[None, :]
    dc = np.clip(ck - cq, -15, 15) + 15
    e = np.arange(15)
    nab = rpb[:, :, 14 - e][:, :, :, dc]
    nab = np.ascontiguousarray(nab.transpose(0, 1, 3, 2, 4)).reshape(NL, 4, 64, 15 * 64)
    qkn = np.concatenate([f("gqa_q_norm"), f("gqa_k_norm")], 1)
    c = f("c"); c_ctx = f("c_ctx")
    st = f("state_rwkv")
    shared = {
        "w_mod": f("w_mod"), "b_mod": f("b_mod"), "w_in": f("w_in"), "w_out": f("w_out"),
        "w_ffn_in": f("w_ffn_in"), "w_ffn_out": f("w_ffn_out"), "pp": pp,
        "rwkv_w2": f("rwkv_w2").reshape(NL, 128, 256), "rwkv_a2": f("rwkv_a2").reshape(NL, 128, 256), "rwkv_g2": f("rwkv_g2"),
        "ln1_w": f("ln1_w"), "ln1_b": f("ln1_b"), "ln2_w": f("ln2_w"), "ln2_b": f("ln2_b"),
        "qkn": qkn, "nab": nab, "cst": cst, "cst2": cst2,
    }
    in_maps = []
    for i in range(8):
        b = i // 4
        m = dict(shared)
        m["xin"] = np.concatenate([x_prompt[2 * i], x_prompt[2 * i + 1], x_sample[b]], 0)
        cc = np.stack([c_ctx, c[b]], 1)
        m["c2"] = np.ascontiguousarray(cc.reshape(8, 128, 2).transpose(1, 0, 2).reshape(128, 16))
        s = st[b].transpose(0, 1, 2, 4, 3)
        m["stT"] = np.ascontiguousarray(s.reshape(NL, 2, 2, 128, 64))
        m["cnak"] = np.ascontiguousarray(f("cache_na_k")[b].reshape(NL, PAST, 256))
        m["cnav"] = np.ascontiguousarray(f("cache_na_v")[b].reshape(NL, PAST, 256))
        m["cgk"] = np.ascontiguousarray(f("cache_gqa_k")[b].reshape(NL, PAST, 128))
        m["cgv"] = np.ascontiguousarray(f("cache_gqa_v")[b].reshape(NL, PAST, 128))
        in_maps.append(m)
    return in_maps


def kernel(**inp):
    depth = NL
    if depth not in _NC_CACHE:
        _NC_CACHE[depth] = build(depth)
    nc = _NC_CACHE[depth]
    in_maps = _prep(inp)
    res = run_bass_kernel_spmd(nc, in_maps, core_ids=list(range(8)))
    R = res.results
    y_prompt = np.stack([R[i // 2]["y"][(i % 2) * 256:(i % 2) * 256 + 256] for i in range(16)], 0)
    y_sample = np.stack([R[0]["y"][512:], R[4]["y"][512:]], 0)
    nst = np.stack([R[i // 2]["nst"][:, i % 2] for i in range(16)], 0)
    def cache(name, hh):
        return np.stack([R[i // 2][name][:, (i % 2) * 256:(i % 2) * 256 + 256].reshape(NL, 256, hh, 64) for i in range(16)], 0)
    return (y_prompt.astype(np.float32), y_sample.astype(np.float32), nst.astype(np.float32),
            cache("nak", 4), cache("nav", 4), cache("ngk", 2), cache("ngv", 2))
```

```python
import numpy as np
import concourse.bass as bass
import concourse.mybir as mybir
from concourse.bass_utils import run_bass_kernel_spmd
from contextlib import ExitStack
import types
import os

F32 = mybir.dt.float32
BF16 = mybir.dt.bfloat16
AF = mybir.ActivationFunctionType
ALU = mybir.AluOpType

D = 1024
NL = 4
DIN = 2688
DFF = 2816
PAST = 512
ALPHA = (2 * NL) ** 0.25
LN_EPS = 1e-5
RMS_EPS = 1e-6
GN_EPS = 64e-5
NEG = -30000.0
PPL = 45

ENGS = ("pe", "act", "dve", "pool", "sp")


def _freeze(fn):
    if fn.__closure__ is None:
        return fn
    cells = []
    for c in fn.__closure__:
        try:
            cells.append(types.CellType(c.cell_contents))
        except ValueError:
            cells.append(c)
    return types.FunctionType(fn.__code__, fn.__globals__, fn.__name__, fn.__defaults__, tuple(cells))


class Prog:
    def __init__(self, nc, stack):
        self.nc = nc
        self.stack = stack
        self.q = {e: [] for e in ENGS}
        self.cnt = {e: 0 for e in ENGS}
        self.sem = {e: stack.enter_context(nc.semaphore("s_" + e)) for e in ENGS}
        self.known = {e: {} for e in ENGS}
        self.st = {}
        self.dsem = {}
        self.lane = None

    def sb(self, name, shape, dt=F32):
        return self.stack.enter_context(self.nc.sbuf_tensor(name, list(shape), dt))

    def ps(self, name, shape, dt=F32):
        return self.stack.enter_context(self.nc.psum_tensor(name, list(shape), dt))

    def dma_sem(self, name):
        if name not in self.dsem:
            self.dsem[name] = [self.stack.enter_context(self.nc.semaphore("d_" + name)), 0]
        return self.dsem[name]

    def _need(self, eng, ev, waits):
        if ev[0] == "eng":
            _, f, idx = ev
            if f == eng:
                if eng == "pe":
                    return
                if idx < self.cnt[eng] - 1:
                    return
            if self.known[eng].get(f, 0) >= idx:
                return
            self.known[eng][f] = idx
            waits.append((self.sem[f], idx))
        else:
            _, name, cnt = ev
            k = "d:" + name
            if self.known[eng].get(k, 0) >= cnt:
                return
            self.known[eng][k] = cnt
            waits.append((self.dsem[name][0], cnt))

    def _deps(self, eng, reads, writes):
        waits = []
        for k in reads:
            s = self.st.setdefault(k, [[], []])
            for ev in s[0]:
                self._need(eng, ev, waits)
            if isinstance(k, tuple) and k[0] == "ps":
                for ev in s[1]:
                    if not (ev[0] == "eng" and ev[1] == eng):
                        self._need(eng, ev, waits)
        for k in writes:
            s = self.st.setdefault(k, [[], []])
            for ev in s[0]:
                self._need(eng, ev, waits)
            for ev in s[1]:
                self._need(eng, ev, waits)
        return waits

    def _commit(self, ev, reads, writes):
        for k in reads:
            if k in writes:
                continue
            s = self.st[k]
            s[1] = [r for r in s[1] if not (r[0] == ev[0] and r[1] == ev[1])] + [ev]
        for k in writes:
            self.st[k] = [[ev], []]

    GLOBAL_KEYS = {"CST", "CST2", "PPK", "XMT", "OCT", "IDB", "A2T", "W2T", "G2", "PST", "SIL", "SILF", "QKN", "MODB", "LNB",
                   "TMPF", "TMPB", "ONESB"}

    def _k(self, k):
        if self.lane is None:
            return k
        if isinstance(k, tuple) and k[0] in ("ps", "ws", "X", "modrow", "L"):
            return k
        if isinstance(k, str) and k in self.GLOBAL_KEYS:
            return k
        return ("L", self.lane, k)

    def op(self, eng, fn, reads=(), writes=()):
        fn = _freeze(fn)
        reads = [self._k(k) for k in reads]
        writes = [self._k(k) for k in writes]
        waits = self._deps(eng, reads, writes)
        self.cnt[eng] += 1
        idx = self.cnt[eng]
        self._commit(("eng", eng, idx), reads, writes)
        sem = self.sem[eng]

        def run(e, waits=waits, fn=fn, sem=sem):
            for (s, v) in waits:
                e.wait_ge(s, v)
            fn(e).then_inc(sem, 1)

        self.q[eng].append(run)

    def dma(self, eng, out, in_, sname, reads=(), writes=()):
        reads = [self._k(k) for k in reads]
        writes = [self._k(k) for k in writes]
        k0 = None
        for k in list(writes) + list(reads):
            if not (isinstance(k, tuple) and k[0] == "modrow"):
                k0 = k
                break
        sname = "k_" + str(k0).replace("(", "").replace(")", "").replace(",", "_").replace(" ", "").replace("'", "")
        waits = self._deps(eng, reads, writes)
        d = self.dma_sem(sname)
        d[1] += 16
        self._commit(("dma", sname, d[1]), reads, writes)
        sem = d[0]

        def run(e, waits=waits, out=out, in_=in_, sem=sem):
            for (s, v) in waits:
                e.wait_ge(s, v)
            e.dma_start(out=out, in_=in_).then_inc(sem, 16)

        self.q[eng].append(run)

    def barrier(self):
        for e in ENGS:
            waits = []
            for f in ENGS:
                if f != e and self.cnt[f] > self.known[e].get(f, 0):
                    self.known[e][f] = self.cnt[f]
                    waits.append((self.sem[f], self.cnt[f]))
            for name, (sem, cnt) in self.dsem.items():
                k = "d:" + name
                if cnt > self.known[e].get(k, 0):
                    self.known[e][k] = cnt
                    waits.append((sem, cnt))

            def run(eh, waits=waits):
                for (s, v) in waits:
                    eh.wait_ge(s, v)

            self.q[e].append(run)

    def emit(self):
        nc = self.nc
        with nc.Block() as block:
            @block.tensor
            def _(e):
                for r in self.q["pe"]:
                    r(e)

            @block.scalar
            def _(e):
                for r in self.q["act"]:
                    r(e)

            @block.vector
            def _(e):
                for r in self.q["dve"]:
                    r(e)

            @block.gpsimd
            def _(e):
                for r in self.q["pool"]:
                    r(e)

            @block.sync
            def _(e):
                for r in self.q["sp"]:
                    r(e)


def build(depth=NL, stage=99):
    nc = bass.Bass("TRN2", target_bir_lowering=False)

    def din(name, shape):
        return nc.dram_tensor(name, list(shape), F32, kind="ExternalInput").ap()

    def dout(name, shape):
        return nc.dram_tensor(name, list(shape), F32, kind="ExternalOutput").ap()

    xin = din("xin", [1536, D])
    c2 = din("c2", [128, 16])
    stT = din("stT", [NL, 2, 2, 128, 64])
    cnak = din("cnak", [NL, PAST, 256])
    cnav = din("cnav", [NL, PAST, 256])
    cgk = din("cgk", [NL, PAST, 128])
    cgv = din("cgv", [NL, PAST, 128])
    w_mod = din("w_mod", [NL, D, 6 * D])
    b_mod = din("b_mod", [NL, 6 * D])
    w_in = din("w_in", [NL, D, DIN])
    w_out = din("w_out", [NL, D, D])
    w_fi = din("w_ffn_in", [NL, D, 2 * DFF])
    w_fo = din("w_ffn_out", [NL, DFF, D])
    pp = din("pp", [128, NL * PPL])
    w2 = din("rwkv_w2", [NL, 128, 256])
    a2 = din("rwkv_a2", [NL, 128, 256])
    g2 = din("rwkv_g2", [NL, 128, 256])
    ln1w = din("ln1_w", [NL, D]); ln1b = din("ln1_b", [NL, D])
    ln2w = din("ln2_w", [NL, D]); ln2b = din("ln2_b", [NL, D])
    qkn = din("qkn", [NL, 128])
    nab = din("nab", [NL, 4, 64, 15 * 64])
    cst = din("cst", [128, 128 * 12])
    cst2 = din("cst2", [128, 1603])

    y = dout("y", [1536, D])
    nst = dout("nst", [NL, 2, 2, 4, 64, 64])
    nak = dout("nak", [NL, 512, 256]); nav = dout("nav", [NL, 512, 256])
    ngk = dout("ngk", [NL, 512, 128]); ngv = dout("ngv", [NL, 512, 128])
    modrow = nc.dram_tensor("modrow", [NL, 2, 6 * D], F32, kind="Internal").ap()

    with ExitStack() as stack:
        P = Prog(nc, stack)
        X = P.sb("X", [128, 12, D])
        XMT = P.sb("XMT", [128, 8, 1024], BF16)
        OCT = P.sb("OCT", [128, 8, 1024], BF16)
        MODB = P.sb("MODB", [128, 2048])
        LNB = P.sb("LNB", [128, 2048])
        WS = P.sb("WS", [128, 4, 8 * 512], BF16)
        AR = P.sb("AR", [128, 14336])
        CST = P.sb("CST", [128, 128 * 12])
        CST2 = P.sb("CST2", [128, 1603])
        PPK = P.sb("PPK", [128, NL * PPL])
        W2T = P.sb("W2T", [128, 1, 256]); A2T = P.sb("A2T", [128, 1, 256]); G2 = P.sb("G2", [128, 1, 256])
        ONESB = P.sb("ONESB", [128, 64], BF16)
        SIL = P.sb("SIL", [128, 16], BF16)
        SILF = P.sb("SILF", [128, 16])
        QKN = P.sb("QKN", [128, 128])
        IDB = P.sb("IDB", [128, 128], BF16)
        TMPF = P.sb("TMPF", [128, 1024])
        TMPB = P.sb("TMPB", [128, 1024], BF16)
        STAT = P.sb("STAT", [128, 64])
        PSB = [P.ps("psb%d" % i, [128, 512]) for i in range(7)]
        PST = P.ps("pst", [128, 1024], BF16)

        IDENT = CST[:, 0:128]
        ONESBD = CST[:, 128:256]
        def MASK(d, j):
            return CST[:, 256 + (d * 5 + j) * 128: 256 + (d * 5 + j + 1) * 128]
        def MASK4(d):
            return CST[:, 256 + d * 5 * 128: 256 + (d * 5 + 4) * 128]
        RMASK = CST2[:, 0:1024]
        ROPE = CST2[:, 1024:1024 + 512].rearrange("p (t c) -> p t c", c=64)
        COLM = CST2[0:64, 1536:1600]

        psrr = [0]
        npsum = [5]
        def psum():
            i = psrr[0] % npsum[0]
            psrr[0] += 1
            return PSB[i], ("ps", i)
        def psacc(i):
            return PSB[5 + i], ("ps", 5 + i)

        P.dma("sp", CST[:], cst[:, :], "c0", writes=["CST"])
        P.dma("sp", CST2[:], cst2[:, :], "c0", writes=["CST2"])
        P.dma("sp", PPK[:], pp[:, :], "c0", writes=["PPK"])
        P.dma("sp", SILF[:], c2[:, :], "c0", writes=["SILF"])
        P.dma("pool", IDB[:], cst[:, 0:128], "c1", writes=["IDB"])
        P.op("pool", lambda e: e.memset(ONESB[:], 1.0), writes=["ONESB"])
        for t in range(12):
            P.dma("sp", X[:, t, :], xin[t * 128:(t + 1) * 128, :], "xin", writes=[("X", t)])
        P.op("act", lambda e: e.activation(out=SIL[:], in_=SILF[:], func=AF.Silu), reads=["SILF"], writes=["SIL"])

        wsrr = [0]
        def load_w(src2d, k0, nk, c0, ncols):
            s = wsrr[0] % 4
            wsrr[0] += 1
            view = WS[:, s, 0:nk * ncols].rearrange("p (k c) -> p k c", c=ncols)
            src = src2d.rearrange("(k p) c -> p k c", p=128)[:, k0:k0 + nk, c0:c0 + ncols]
            P.dma("pool", view, src, "ws%d" % s, writes=[("ws", s)])
            return view, ("ws", s)

        MR = AR[0:2, 0:6144]
        BM = AR[0:2, 6144:12288]
        for l in range(1):
            P.dma("sp", BM, b_mod[l:l + 1, :].partition_broadcast(2)[:, 0, :], "c0", writes=["BM"])
            for nt in range(12):
                wv, wk = load_w(w_mod[l], 0, 8, nt * 512, 512)
                pt, pk = psum()
                for k in range(8):
                    P.op("pe", lambda e, pt=pt, wv=wv, k=k: e.matmul(pt[0:2, :], lhsT=SIL[:, 2 * k:2 * k + 2], rhs=wv[:, k, :], start=(k == 0), stop=(k == 7)),
                         reads=["SIL", wk], writes=[pk])
                addone = 1.0 if nt in (2, 3, 8, 9) else 0.0
                P.op("dve", lambda e, pt=pt, nt=nt, addone=addone: e.scalar_tensor_tensor(out=MR[:, nt * 512:(nt + 1) * 512], in0=pt[0:2, :], scalar=addone, in1=BM[:, nt * 512:(nt + 1) * 512], op0=ALU.add, op1=ALU.add),
                     reads=[pk, "BM"], writes=["MR"])
            P.dma("sp", modrow[l], MR, "mr", reads=["MR"], writes=[("modrow", l)])
        P.barrier()

        PRE = {}

        def mods_gen(l1):
            wvs = [WS[:, 3, 0:2048].rearrange("p (k c) -> p k c", c=256), WS[:, 3, 2048:4096].rearrange("p (k c) -> p k c", c=256)]
            MRS = [MODB[0:2, 1536:1792], MODB[0:2, 1792:2048]]
            BMS = [LNB[0:2, 1536:1792], LNB[0:2, 1792:2048]]
            src = w_mod[l1].rearrange("(k p) c -> p k c", p=128)

            def issue(j):
                b = j % 2
                P.dma("sp", BMS[b], b_mod[l1:l1 + 1, j * 256:(j + 1) * 256].partition_broadcast(2)[:, 0, :], "bms", writes=[("BMS", b)])
                P.dma("pool", wvs[b], src[:, :, j * 256:(j + 1) * 256], "ws3", writes=[("wm", b)])

            issue(0)
            for _ in range(6):
                yield
            for j in range(24):
                b = j % 2
                if j + 1 < 24:
                    issue(j + 1)
                for _ in range(4):
                    yield
                pt, pk = psum()
                for k in range(8):
                    P.op("pe", lambda e, k=k: e.matmul(pt[0:2, 0:256], lhsT=SIL[:, 2 * k:2 * k + 2], rhs=wvs[b][:, k, :], start=(k == 0), stop=(k == 7)), reads=["SIL", ("wm", b)], writes=[pk])
                yield
                addone = 1.0 if (j // 2) in (2, 3, 8, 9) else 0.0
                P.op("dve", lambda e: e.scalar_tensor_tensor(out=MRS[b], in0=pt[0:2, 0:256], scalar=addone, in1=BMS[b], op0=ALU.add, op1=ALU.add), reads=[pk, ("BMS", b)], writes=[("MRS", b)])
                yield
                P.dma("sp", modrow[l1, :, j * 256:(j + 1) * 256], MRS[b], "mrs", reads=[("MRS", b)], writes=[("modrow", l1)])
                yield

        def load_mod(l, ci, c0, n, dst):
            P.dma("sp", dst, modrow[l, ci:ci + 1, c0:c0 + n].partition_broadcast(128)[:, 0, :], "md", reads=[("modrow", l)], writes=["MODB"])

        def load_ln(l, wsrc, bsrc):
            P.dma("sp", LNB[:, 0:1024], wsrc[l:l + 1, :].partition_broadcast(128)[:, 0, :], "ln", writes=["LNB"])
            P.dma("sp", LNB[:, 1024:2048], bsrc[l:l + 1, :].partition_broadcast(128)[:, 0, :], "ln", writes=["LNB"])

        def ppc(l, j):
            return PPK[:, l * PPL + j: l * PPL + j + 1]

        def modulate_transpose(tiles, mod_sh, mod_sc):
            for i, tt in enumerate(tiles):
                P.op("dve", lambda e, tt=tt: e.tensor_tensor(out=TMPF[:], in0=X[:, tt, :], in1=mod_sc, op=ALU.mult), reads=[("X", tt), "MODB"], writes=["TMPF"])
                P.op("dve", lambda e: e.tensor_tensor(out=TMPB[:], in0=TMPF[:], in1=mod_sh, op=ALU.add), reads=["TMPF", "MODB"], writes=["TMPB"])
                for k in range(8):
                    P.op("pe", lambda e, k=k: e.transpose(out=PST[:, k * 128:(k + 1) * 128], in_=TMPB[:, k * 128:(k + 1) * 128], identity=IDB[:]), reads=["TMPB", "IDB"], writes=["PST"])
                P.op("act", lambda e, i=i: e.copy(out=XMT[:, :, i * 128:(i + 1) * 128], in_=PST[:].rearrange("p (k t) -> p k t", t=128)), reads=["PST"], writes=["XMT"])

        def layer_norm(tt, l):
            P.op("dve", lambda e: e.bn_stats(out=STAT[:, 0:6], in_=X[:, tt, 0:512]), reads=[("X", tt)], writes=["STAT"])
            P.op("dve", lambda e: e.bn_stats(out=STAT[:, 6:12], in_=X[:, tt, 512:1024]), reads=[("X", tt)], writes=["STAT"])
            P.op("dve", lambda e: e.bn_aggr(out=STAT[:, 12:14], in_=STAT[:, 0:12].rearrange("p (a b) -> p a b", b=6)), reads=["STAT"], writes=["STAT"])
            P.op("act", lambda e: e.activation(out=STAT[:, 14:15], in_=STAT[:, 13:14], func=AF.Sqrt, bias=CST2[:, 1602:1603], scale=1.0), reads=["STAT", "CST2"], writes=["STAT2"])
            P.op("dve", lambda e: e.reciprocal(out=STAT[:, 15:16], in_=STAT[:, 14:15]), reads=["STAT2"], writes=["STAT3"])
            P.op("dve", lambda e: e.tensor_scalar(out=X[:, tt, :], in0=X[:, tt, :], scalar1=STAT[:, 12:13], scalar2=STAT[:, 15:16], op0=ALU.subtract, op1=ALU.mult), reads=["STAT", "STAT3", ("X", tt)], writes=[("X", tt)])
            P.op("dve", lambda e: e.tensor_tensor(out=X[:, tt, :], in0=X[:, tt, :], in1=LNB[:, 0:1024], op=ALU.mult), reads=["LNB", ("X", tt)], writes=[("X", tt)])
            P.op("dve", lambda e: e.tensor_tensor(out=X[:, tt, :], in0=X[:, tt, :], in1=LNB[:, 1024:2048], op=ALU.add), reads=["LNB", ("X", tt)], writes=[("X", tt)])

        def rwkv_phase(l, NT, pass_seqs, bg=None):
            WR = WS[:].rearrange("p s c -> p (s c)")[:, 0:8 * 1152].rearrange("p (k c) -> p k c", c=1152)
            WRK = [("ws", 0), ("ws", 1), ("ws", 2)]
            o = [0]
            def arr(n):
                v = AR[:, o[0]:o[0] + n]
                o[0] += n
                return v
            def arrb16(n):
                v = AR[:, o[0]:o[0] + n // 2].bitcast(BF16)
                o[0] += n // 2
                return v
            canon = {"OB": "E2", "BON": "E1", "E3": "T2"}
            names = ["F6", "FAD", "FSG", "FR", "FK", "FV", "KKN", "A0", "A1", "LD", "CF", "KD", "BD", "E1", "E2", "T1", "T2"]

            def make_lane(li):
                A_ = {n: arr(256) for n in names}
                A = dict(A_)
                for a_, b_ in canon.items():
                    A[a_] = A_[b_]
                AK = lambda n: ("A", canon.get(n, n))
                STG = arr(258)
                OF = arr(1024)
                TOT = arr(4)
                GAM = arr(4)
                EXP = {n: arrb16(512).rearrange("p (c t) -> p c t", t=128) for n in ["QB", "RB", "KB", "BB", "VB"]}
                if li == 0:
                    ub = MODB[:].bitcast(BF16)
                    ub2 = LNB[:].bitcast(BF16)
                    KTBT = ub[:, 0:1024]
                    VT4 = ub[:, 1024:1536].rearrange("p (c t) -> p c t", t=128)
                    gr = [ub[:, 1536 + 512 * i:2048 + 512 * i].rearrange("p (c t) -> p c t", t=128) for i in range(3)]
                    XX = [ub2[:, 1024 * i:1024 * (i + 1)].rearrange("p (c x t) -> p c x t", x=2, t=128) for i in range(2)]
                    TTM = [ub2[:, 2048 + 512 * i:2560 + 512 * i].rearrange("p (c t) -> p c t", t=128) for i in range(2)]
                else:
                    uo = OCT[:, 2:8, :].rearrange("p c t -> p (c t)")
                    KTBT = uo[:, 0:1024]
                    VT4 = uo[:, 1024:1536].rearrange("p (c t) -> p c t", t=128)
                    gr = [uo[:, 1536 + 512 * i:2048 + 512 * i].rearrange("p (c t) -> p c t", t=128) for i in range(3)]
                    XX = [uo[:, 3072 + 1024 * i:3072 + 1024 * (i + 1)].rearrange("p (c x t) -> p c x t", x=2, t=128) for i in range(2)]
                    TTM = [uo[:, 5120 + 512 * i:5632 + 512 * i].rearrange("p (c t) -> p c t", t=128) for i in range(2)]
                GR = {"AKK": gr[0], "ARK": gr[1], "ARB": gr[2]}
                KT4 = KTBT[:, 0:512].rearrange("p (c t) -> p c t", t=128)
                BT4 = KTBT[:, 512:1024].rearrange("p (c t) -> p c t", t=128)
                CH = {n: TMPF[:, li * 384 + i * 128:li * 384 + (i + 1) * 128] for i, n in enumerate(["X1F", "M", "MT"])}
                CHB = {n: TMPB[:, li * 384 + i * 128:li * 384 + (i + 1) * 128] for i, n in enumerate(["X1B", "NU", "MB"])}
                P.lane = li
                for n in EXP:
                    P.op("pool", lambda e, n=n: e.memset(EXP[n], 0.0), writes=[("EXP", n)])
                P.lane = None
                v3 = lambda n: A[n].rearrange("p (c t) -> p c t", t=64)
                EK = [("EXP", n) for n in ("QB", "RB", "KB", "BB", "VB")]

                def proj_conv(fc, name, off, NTs, s0):
                    dst = A[name]
                    lo = max(s0 - 1, 0)
                    hi = min(s0 + 257, NTs)
                    sh = lo - (s0 - 1)
                    n = hi - lo
                    S = STG; SK = "STG"
                    pt, pk = psum()
                    for k in range(8):
                        P.op("pe", lambda e, k=k: e.matmul(pt[:, 0:n], lhsT=WR[:, k, fc * 128:(fc + 1) * 128], rhs=XMT[:, k, off + lo:off + hi], start=(k == 0), stop=(k == 7)),
                             reads=WRK + ["XMT"], writes=[pk])
                    if sh > 0:
                        P.op("pool", lambda e: e.memset(S[:, 0:1], 0.0), writes=[SK])
                    if sh + n < 258:
                        P.op("pool", lambda e: e.memset(S[:, 257:258], 0.0), writes=[SK])
                    P.op("act", lambda e: e.copy(out=S[:, sh:sh + n], in_=pt[:, 0:n]), reads=[pk], writes=[SK])
                    P.op("act", lambda e: e.activation(out=dst, in_=S[:, 1:257], func=AF.Copy, scale=ppc(l, 9 + fc)), reads=[SK, "PPK"], writes=[AK(name)])
                    P.op("dve", lambda e: e.scalar_tensor_tensor(out=dst, in0=S[:, 0:256], scalar=ppc(l, fc), in1=dst, op0=ALU.mult, op1=ALU.add), reads=[SK, "PPK", AK(name)], writes=[AK(name)])
                    P.op("dve", lambda e: e.scalar_tensor_tensor(out=dst, in0=S[:, 2:258], scalar=ppc(l, 18 + fc), in1=dst, op0=ALU.mult, op1=ALU.add), reads=[SK, "PPK", AK(name)], writes=[AK(name)])

                def prep_shared(hp, off, NTs, s0):
                    for fc, name in ((6, "F6"), (7, "FAD"), (8, "FSG"), (hp, "FR"), (2 + hp, "FK"), (4 + hp, "FV")):
                        proj_conv(fc, name, off, NTs, s0)
                        yield
                    P.op("act", lambda e: e.activation(out=A["F6"], in_=A["F6"], func=AF.Tanh), reads=[AK("F6")], writes=[AK("F6")])
                    P.op("act", lambda e: e.activation(out=A["FSG"], in_=A["FSG"], func=AF.Sigmoid), reads=[AK("FSG")], writes=[AK("FSG")])
                    for dd in (0, 1):
                        pt, pk = psum()
                        P.op("pe", lambda e: e.matmul(pt[:, 0:256], lhsT=A2T[64 * dd:64 * dd + 64, 0, hp * 128:(hp + 1) * 128], rhs=A["FAD"][64 * dd:64 * dd + 64, :], start=True, stop=True), reads=["A2T", AK("FAD")], writes=[pk])
                        P.op("act", lambda e: e.activation(out=A["A%d" % dd], in_=pt[:, 0:256], func=AF.Sigmoid, bias=ppc(l, 31 + dd * 2 + hp), scale=1.0), reads=[pk, "PPK"], writes=[AK("A%d" % dd)])
                    P.op("dve", lambda e: e.tensor_scalar(out=A["T1"], in0=A["FK"], scalar1=ppc(l, 35 + hp), scalar2=None, op0=ALU.mult), reads=[AK("FK"), "PPK"], writes=[AK("T1")])
                    P.op("pool", lambda e: e.tensor_tensor(out=A["T2"], in0=A["T1"], in1=A["T1"], op=ALU.mult), reads=[AK("T1")], writes=[AK("T2")])
                    yield
                    pt, pk = psum()
                    P.op("pe", lambda e: e.matmul(pt[:, 0:256], lhsT=ONESBD, rhs=A["T2"], start=True, stop=True), reads=["CST", AK("T2")], writes=[pk])
                    P.op("act", lambda e: e.activation(out=A["T2"], in_=pt[:, 0:256], func=AF.Sqrt, scale=64.0), reads=[pk], writes=[AK("T2")])
                    yield
                    P.op("dve", lambda e: e.tensor_scalar(out=A["T2"], in0=A["T2"], scalar1=1e-12, scalar2=None, op0=ALU.max), reads=[AK("T2")], writes=[AK("T2")])
                    P.op("dve", lambda e: e.reciprocal(out=A["T2"], in_=A["T2"]), reads=[AK("T2")], writes=[AK("T2")])
                    yield
                    P.op("dve", lambda e: e.tensor_tensor(out=A["KKN"], in0=A["T1"], in1=A["T2"], op=ALU.mult), reads=[AK("T1"), AK("T2")], writes=[AK("KKN")])
                    for h in (0, 1):
                        P.op("pool", lambda e, h=h: e.tensor_copy(out=EXP["VB"][64 * h:64 * h + 64, :, 64 * h:64 * h + 64], in_=v3("FV")[64 * h:64 * h + 64]), reads=[AK("FV"), ("EXP", "VB")], writes=[("EXP", "VB")])
                    yield

                def prep_dir(hp, d):
                    pt, pk = psum()
                    P.op("pe", lambda e: e.matmul(pt[:, 0:256], lhsT=W2T[64 * d:64 * d + 64, 0, hp * 128:(hp + 1) * 128], rhs=A["F6"][64 * d:64 * d + 64, :], start=True, stop=True), reads=["W2T", AK("F6")], writes=[pk])
                    P.op("act", lambda e: e.activation(out=A["LD"], in_=pt[:, 0:256], func=AF.Sigmoid, bias=ppc(l, 27 + d * 2 + hp), scale=1.0), reads=[pk, "PPK"], writes=[AK("LD")])
                    Ad = A["A%d" % d]; AdK = AK("A%d" % d)
                    P.op("pool", lambda e: e.tensor_tensor(out=A["BD"], in0=Ad, in1=A["KKN"], op=ALU.mult), reads=[AdK, AK("KKN")], writes=[AK("BD")])
                    yield
                    P.op("dve", lambda e: e.tensor_scalar(out=A["LD"], in0=A["LD"], scalar1=-0.6065306597126334, scalar2=None, op0=ALU.mult), reads=[AK("LD")], writes=[AK("LD")])
                    P.op("dve", lambda e: e.tensor_scalar(out=A["T1"], in0=Ad, scalar1=ppc(l, 37 + hp), scalar2=ppc(l, 37 + hp), op0=ALU.mult, op1=ALU.subtract), reads=[AdK, "PPK"], writes=[AK("T1")])
                    yield
                    P.op("dve", lambda e: e.tensor_tensor_scan(out=A["CF"], data0=RMASK[:, 0:256], data1=A["LD"], initial=0.0, op0=ALU.mult, op1=ALU.add), reads=[AK("LD"), "CST2"], writes=[AK("CF")])
                    P.op("dve", lambda e: e.scalar_tensor_tensor(out=A["KD"], in0=A["T1"], scalar=1.0, in1=A["FK"], op0=ALU.add, op1=ALU.mult), reads=[AK("T1"), AK("FK")], writes=[AK("KD")])
                    yield
                    CF3 = A["CF"].rearrange("p (c t) -> p c t", t=64)
                    P.op("dve", lambda e: e.tensor_copy(out=TOT.rearrange("p (c o) -> p c o", o=1), in_=CF3[:, :, 63:64]), reads=[AK("CF")], writes=["TOT"])
                    yield
                    P.op("act", lambda e: e.activation(out=GAM, in_=TOT, func=AF.Exp), reads=["TOT"], writes=["GAM"])
                    TOTB = TOT.rearrange("p (c o) -> p c o", o=1).to_broadcast([128, 4, 64])
                    if d == 0:
                        P.op("dve", lambda e: e.tensor_tensor(out=A["T2"], in0=A["CF"], in1=A["LD"], op=ALU.subtract), reads=[AK("CF"), AK("LD")], writes=[AK("T2")])
                        P.op("act", lambda e: e.activation(out=A["E2"], in_=A["CF"], func=AF.Exp), reads=[AK("CF")], writes=[AK("E2")])
                        yield
                        P.op("act", lambda e: e.activation(out=A["E1"], in_=A["T2"], func=AF.Exp), reads=[AK("T2")], writes=[AK("E1")])
                        yield
                        P.op("act", lambda e: e.activation(out=A["E3"], in_=A["CF"], func=AF.Exp, scale=-1.0), reads=[AK("CF"), AK("T2")], writes=[AK("E3")])
                    else:
                        P.op("dve", lambda e: e.tensor_tensor(out=v3("T2"), in0=TOTB, in1=v3("CF"), op=ALU.subtract), reads=["TOT", AK("CF")], writes=[AK("T2")])
                        yield
                        P.op("act", lambda e: e.activation(out=A["E1"], in_=A["T2"], func=AF.Exp), reads=[AK("T2")], writes=[AK("E1")])
                        P.op("dve", lambda e: e.tensor_tensor(out=A["CF"], in0=A["T2"], in1=A["LD"], op=ALU.add), reads=[AK("T2"), AK("LD")], writes=[AK("CF")])
                        yield
                        P.op("act", lambda e: e.activation(out=A["E2"], in_=A["CF"], func=AF.Exp), reads=[AK("CF")], writes=[AK("E2")])
                        P.op("act", lambda e: e.activation(out=A["E3"], in_=A["CF"], func=AF.Exp, scale=-1.0), reads=[AK("CF"), AK("T2"), AK("E1")], writes=[AK("E3")])
                    yield
                    i = 0
                    for (n, a, b) in (("QB", "KKN", "E1"), ("RB", "FR", "E2"), ("KB", "KD", "E3"), ("BB", "BD", "E3")):
                        for h in (0, 1):
                            eng = "dve" if i % 2 == 0 else "pool"
                            i += 1
                            P.op(eng, lambda e, n=n, a=a, b=b, h=h: e.tensor_tensor(out=EXP[n][64 * h:64 * h + 64, :, 64 * h:64 * h + 64], in0=v3(a)[64 * h:64 * h + 64], in1=v3(b)[64 * h:64 * h + 64], op=ALU.mult), reads=[AK(a), AK(b), ("EXP", n)], writes=[("EXP", n)])
                        yield

                def units_pre(d):
                    E = lambda n, c: EXP[n][:, c, :]
                    for c in range(4):
                        P.op("pe", lambda e: e.transpose(out=PST[:, c * 128:(c + 1) * 128], in_=E("KB", c), identity=IDB[:]), reads=EK + ["IDB"], writes=["PST"])
                        P.op("pe", lambda e: e.transpose(out=PST[:, (4 + c) * 128:(5 + c) * 128], in_=E("BB", c), identity=IDB[:]), reads=EK + ["IDB"], writes=["PST"])
                    P.op("act", lambda e: e.copy(out=KTBT, in_=PST[:, 0:1024]), reads=["PST"], writes=["KTBT"])
                    for c in range(4):
                        P.op("pe", lambda e: e.transpose(out=PST[:, c * 128:(c + 1) * 128], in_=E("VB", c), identity=IDB[:]), reads=EK + ["IDB"], writes=["PST"])
                    P.op("act", lambda e: e.copy(out=VT4, in_=PST[:, 0:512].rearrange("p (c t) -> p c t", t=128)), reads=["PST"], writes=["VT4"])
                    yield
                    mb = lambda j: MASK(d, j).unsqueeze(1).to_broadcast([128, 4, 128])
                    grams = ((("QB", "BB"), 0, XX[0][:, :, 0, :], ("XX", 0)), (("BB", "QB"), 1, XX[0][:, :, 1, :], ("XX", 0)),
                             (("KB", "QB"), 2, GR["AKK"], "GAKK"), (("KB", "RB"), 3, GR["ARK"], "GARK"), (("BB", "RB"), 3, GR["ARB"], "GARB"))
                    for gi, ((lh, rh), mj, dst, dk) in enumerate(grams):
                        pg, pgk = psum()
                        for c in range(4):
                            P.op("pe", lambda e: e.matmul(pg[:, c * 128:(c + 1) * 128], lhsT=E(lh, c), rhs=E(rh, c), start=True, stop=True), reads=EK, writes=[pgk])
                        pg3 = pg[:, 0:512].rearrange("p (c t) -> p c t", t=128)
                        P.op("dve", lambda e: e.tensor_tensor(out=dst, in0=pg3, in1=mb(mj), op=ALU.mult), reads=[pgk, "CST"], writes=[dk])
                        if gi == 1:
                            P.op("dve", lambda e: e.scalar_tensor_tensor(out=TTM[0], in0=pg3, scalar=-1.0, in1=mb(mj), op0=ALU.mult, op1=ALU.mult), reads=[pgk, "CST"], writes=[("TT", 0)])
                        yield
                    cur = 0
                    for k in range(5):
                        nx = 1 - cur
                        pxs = [psum(), psum()]
                        for c in range(4):
                            px, pxk = pxs[c // 2]
                            cc = c % 2
                            P.op("pe", lambda e: e.matmul(px[:, (2 * cc) * 128:(2 * cc + 1) * 128], lhsT=XX[cur][:, c, 1, :], rhs=XX[cur][:, c, 0, :], start=True, stop=True), reads=[("XX", cur)], writes=[pxk])
                            P.op("pe", lambda e: e.matmul(px[:, (2 * cc + 1) * 128:(2 * cc + 2) * 128], lhsT=XX[cur][:, c, 0, :], rhs=XX[cur][:, c, 1, :], start=True, stop=True), reads=[("XX", cur)], writes=[pxk])
                        for hlf in (0, 1):
                            px, pxk = pxs[hlf]
                            dstv = XX[nx][:, 2 * hlf:2 * hlf + 2, :, :]
                            srcv = px[:, 0:512].rearrange("p (c x t) -> p c x t", x=2, t=128)
                            if hlf == 0:
                                P.op("act", lambda e: e.copy(out=dstv, in_=srcv), reads=[pxk], writes=[("XX", nx, hlf)])
                            else:
                                P.op("act", lambda e: e.copy(out=dstv, in_=srcv), reads=[pxk], writes=[("XX", nx, hlf)])
                        yield
                        pT, pTk = psum()
                        for c in range(4):
                            xk = ("XX", nx, c // 2)
                            P.op("pe", lambda e: e.matmul(pT[:, c * 128:(c + 1) * 128], lhsT=XX[nx][:, c, 0, :], rhs=TTM[cur][:, c, :], start=True, stop=False), reads=[xk, ("TT", cur)], writes=[pTk])
                            P.op("pe", lambda e: e.matmul(pT[:, c * 128:(c + 1) * 128], lhsT=IDB[:], rhs=TTM[cur][:, c, :], start=False, stop=False), reads=["IDB", ("TT", cur)], writes=[pTk])
                            P.op("pe", lambda e: e.matmul(pT[:, c * 128:(c + 1) * 128], lhsT=IDB[:], rhs=XX[nx][:, c, 1, :], start=False, stop=True), reads=["IDB", xk], writes=[pTk])
                        P.op("act", lambda e: e.copy(out=TTM[nx], in_=pT[:, 0:512].rearrange("p (c t) -> p c t", t=128)), reads=[pTk], writes=[("TT", nx)])
                        P.st[P._k(("XX", nx))] = [list(P.st[P._k(("XX", nx, 0))][0]) + list(P.st[P._k(("XX", nx, 1))][0]), []]
                        cur = nx
                        yield
                    return cur

                def chain(d, c, tti, odst, okey, obase):
                    E = lambda n: EXP[n][:, c, :]
                    p3, p3k = psum()
                    P.op("pe", lambda e: e.matmul(p3[:, 0:128], lhsT=E("QB"), rhs=CHB["MB"], start=True, stop=False), reads=EK + ["CMB"], writes=[p3k])
                    P.op("pe", lambda e: e.matmul(p3[:, 0:128], lhsT=GR["AKK"][:, c, :], rhs=VT4[:, c, :], start=False, stop=True), reads=["GAKK", "VT4"], writes=[p3k])
                    P.op("act", lambda e: e.copy(out=CHB["X1B"], in_=p3[:, 0:128]), reads=[p3k], writes=["CX1B"])
                    yield
                    p4, p4k = psum()
                    P.op("pe", lambda e: e.matmul(p4[:, 0:128], lhsT=TTM[tti][:, c, :], rhs=CHB["X1B"], start=True, stop=False), reads=[("TT", tti), "CX1B"], writes=[p4k])
                    P.op("pe", lambda e: e.matmul(p4[:, 0:128], lhsT=IDB[:], rhs=CHB["X1B"], start=False, stop=True), reads=["IDB", "CX1B"], writes=[p4k])
                    P.op("dve", lambda e: e.tensor_scalar(out=CHB["NU"], in0=p4[:, 0:128], scalar1=-1.0, scalar2=None, op0=ALU.mult), reads=[p4k], writes=["CNU"])
                    yield
                    p6, p6k = psum()
                    P.op("pe", lambda e: e.matmul(p6[:, 0:128], lhsT=KT4[:, c, :], rhs=VT4[:, c, :], start=True, stop=False), reads=["KTBT", "VT4"], writes=[p6k])
                    P.op("pe", lambda e: e.matmul(p6[:, 0:128], lhsT=BT4[:, c, :], rhs=CHB["NU"], start=False, stop=True), reads=["KTBT", "CNU"], writes=[p6k])
                    p5, p5k = psum()
                    P.op("pe", lambda e: e.matmul(p5[:, 0:128], lhsT=CHB["MB"], rhs=E("RB"), start=True, stop=False), reads=EK + ["CMB"], writes=[p5k])
                    P.op("pe", lambda e: e.matmul(p5[:, 0:128], lhsT=VT4[:, c, :], rhs=GR["ARK"][:, c, :], start=False, stop=False), reads=["VT4", "GARK"], writes=[p5k])
                    P.op("pe", lambda e: e.matmul(p5[:, 0:128], lhsT=CHB["NU"], rhs=GR["ARB"][:, c, :], start=False, stop=True), reads=["CNU", "GARB"], writes=[p5k])
                    P.op("dve", lambda e: e.tensor_tensor(out=CH["MT"], in0=p6[:, 0:128], in1=CH["M"], op=ALU.add), reads=[p6k, "CM"], writes=["CMT"])
                    yield
                    P.op("act", lambda e: e.activation(out=CHB["MB"], in_=CH["MT"], func=AF.Copy, scale=GAM[:, c:c + 1]), reads=["CMT", "GAM"], writes=["CMB"])
                    P.op("dve", lambda e: e.tensor_scalar(out=CH["M"], in0=CH["MT"], scalar1=GAM[:, c:c + 1], scalar2=None, op0=ALU.mult), reads=["CMT", "GAM"], writes=["CM"])
                    for h in (0, 1):
                        P.op("act", lambda e, h=h: e.copy(out=odst[64 * h:64 * h + 64, obase:obase + 64], in_=p5[64 * h:64 * h + 64, 64 * h:64 * h + 64]), reads=[p5k], writes=[okey])
                    yield

                def finalize(hp, off, s0):
                    OBK = AK("OB")
                    P.op("dve", lambda e: e.tensor_tensor(out=A["OB"], in0=A["OB"], in1=OF[:, s0:s0 + 256], op=ALU.add), reads=[OBK, "OF"], writes=[OBK])
                    pt, pk = psum()
                    P.op("pe", lambda e: e.matmul(pt[:, 0:256], lhsT=ONESBD, rhs=A["OB"], start=True, stop=True), reads=["CST", OBK], writes=[pk])
                    yield
                    P.op("dve", lambda e: e.tensor_tensor(out=A["OB"], in0=A["OB"], in1=pt[:, 0:256], op=ALU.subtract), reads=[pk, OBK], writes=[OBK])
                    P.op("pool", lambda e: e.tensor_tensor(out=A["T1"], in0=A["OB"], in1=A["OB"], op=ALU.mult), reads=[OBK], writes=[AK("T1")])
                    yield
                    pt2, pk2 = psum()
                    P.op("pe", lambda e: e.matmul(pt2[:, 0:256], lhsT=ONESBD, rhs=A["T1"], start=True, stop=True), reads=["CST", AK("T1")], writes=[pk2])
                    P.op("act", lambda e: e.activation(out=A["T2"], in_=pt2[:, 0:256], func=AF.Sqrt, bias=CST2[:, 1601:1602], scale=1.0), reads=[pk2, "CST2"], writes=[AK("T2")])
                    yield
                    P.op("dve", lambda e: e.reciprocal(out=A["T2"], in_=A["T2"]), reads=[AK("T2")], writes=[AK("T2")])
                    P.op("pool", lambda e: e.tensor_tensor(out=A["T1"], in0=A["A0"], in1=A["A1"], op=ALU.add), reads=[AK("A0"), AK("A1"), AK("T1")], writes=[AK("T1")])
                    yield
                    P.op("dve", lambda e: e.tensor_tensor(out=A["OB"], in0=A["OB"], in1=A["T2"], op=ALU.mult), reads=[OBK, AK("T2")], writes=[OBK])
                    P.op("dve", lambda e: e.tensor_scalar(out=A["OB"], in0=A["OB"], scalar1=ppc(l, 41 + hp), scalar2=ppc(l, 43 + hp), op0=ALU.mult, op1=ALU.add), reads=[OBK, "PPK"], writes=[OBK])
                    P.op("dve", lambda e: e.tensor_scalar(out=A["T1"], in0=A["T1"], scalar1=-2.0, scalar2=ppc(l, 37 + hp), op0=ALU.add, op1=ALU.mult), reads=[AK("T1"), "PPK"], writes=[AK("T1")])
                    yield
                    P.op("dve", lambda e: e.scalar_tensor_tensor(out=A["T1"], in0=A["T1"], scalar=2.0, in1=A["FK"], op0=ALU.add, op1=ALU.mult), reads=[AK("T1"), AK("FK")], writes=[AK("T1")])
                    P.op("dve", lambda e: e.scalar_tensor_tensor(out=A["T1"], in0=A["T1"], scalar=ppc(l, 39 + hp), in1=A["FR"], op0=ALU.mult, op1=ALU.mult), reads=[AK("T1"), "PPK", AK("FR")], writes=[AK("T1")])
                    yield
                    pt3, pk3 = psum()
                    P.op("pe", lambda e: e.matmul(pt3[:, 0:256], lhsT=ONESBD, rhs=A["T1"], start=True, stop=True), reads=["CST", AK("T1")], writes=[pk3])
                    P.op("dve", lambda e: e.scalar_tensor_tensor(out=A["BON"], in0=pt3[:, 0:256], scalar=64.0, in1=A["FV"], op0=ALU.mult, op1=ALU.mult), reads=[pk3, AK("FV")], writes=[AK("BON")])
                    yield
                    pt4, pk4 = psum()
                    P.op("pe", lambda e: e.matmul(pt4[:, 0:256], lhsT=G2[:, 0, hp * 128:(hp + 1) * 128], rhs=A["FSG"], start=True, stop=True), reads=["G2", AK("FSG")], writes=[pk4])
                    P.op("dve", lambda e: e.tensor_tensor(out=A["OB"], in0=A["OB"], in1=A["BON"], op=ALU.add), reads=[OBK, AK("BON")], writes=[OBK])
                    yield
                    P.op("dve", lambda e: e.tensor_tensor(out=OCT[:, hp, off + s0:off + s0 + 256], in0=A["OB"], in1=pt4[:, 0:256], op=ALU.mult), reads=[OBK, pk4], writes=[("OCTW", hp)])
                    yield

                def init_state(is_sample, d, hp):
                    P.op("pool", lambda e: e.memset(CH["M"], 0.0), writes=["CM"])
                    if is_sample:
                        for h in (0, 1):
                            P.dma("sp", CH["M"][64 * h:64 * h + 64, 64 * h:64 * h + 64], stT[l, d, hp, 64 * h:64 * h + 64, :], "stin", writes=["CM"])
                    P.op("pool", lambda e: e.tensor_copy(out=CHB["MB"], in_=CH["M"]), reads=["CM"], writes=["CMB"])

                def emit_state(seq_idx, d, hp):
                    pt, pk = psum()
                    P.op("pe", lambda e: e.transpose(out=pt[:, 0:128], in_=CH["M"], identity=IDENT), reads=["CM", "CST"], writes=[pk])
                    P.op("act", lambda e: e.copy(out=CH["MT"], in_=pt[:, 0:128]), reads=[pk], writes=["CMT"])
                    for h in (0, 1):
                        P.dma("sp", nst[l, seq_idx, d, 2 * hp + h], CH["MT"][64 * h:64 * h + 64, 64 * h:64 * h + 64], "ost", reads=["CMT"])

                def job(off, NTs, is_sample, seq_idx, hp):
                    nseg = NTs // 256
                    sweeps = [((0, 1), [0])] if nseg == 1 else [((0,), list(range(nseg))), ((1,), list(reversed(range(nseg))))]
                    for (dirs, segs) in sweeps:
                        if nseg > 1:
                            init_state(is_sample, dirs[0], hp)
                        for sg in segs:
                            s0 = sg * 256
                            yield from prep_shared(hp, off, NTs, s0)
                            for d in dirs:
                                if nseg == 1:
                                    init_state(is_sample, d, hp)
                                yield from prep_dir(hp, d)
                                tti = yield from units_pre(d)
                                for c in ([0, 1, 2, 3] if d == 0 else [3, 2, 1, 0]):
                                    if d == 0:
                                        yield from chain(d, c, tti, OF, "OF", s0 + 64 * c)
                                    else:
                                        yield from chain(d, c, tti, A["OB"], AK("OB"), 64 * c)
                                if nseg == 1 and not is_sample:
                                    emit_state(seq_idx, d, hp)
                                    yield
                                if d == 1:
                                    yield from finalize(hp, off, s0)
                        if nseg > 1 and not is_sample:
                            emit_state(seq_idx, dirs[0], hp)
                            yield
                return job

            npsum[0] = 7
            jobs = [make_lane(0), make_lane(1)]
            assert o[0] <= 14336, o[0]

            def lane_stream(li):
                for (off, NTs, is_sample, seq_idx) in pass_seqs:
                    yield from jobs[li](off, NTs, is_sample, seq_idx, li)

            gens = [(0, lane_stream(0)), (1, lane_stream(1))]
            if bg is not None:
                gens.append((2, bg))
            while gens:
                for (li, g) in list(gens):
                    P.lane = li
                    try:
                        next(g)
                    except StopIteration:
                        gens.remove((li, g))
            P.lane = None
            npsum[0] = 5
            P.barrier()

        def attn_phase(l, tiles, NT, is_sample, pass_seqs):
            NTT = NT // 128
            nrow = NT // 64
            o = [0]
            def arrb(n):
                v = AR[:, o[0]:o[0] + n // 2].bitcast(BF16)
                o[0] += n // 2
                return v
            def arrf(n):
                v = AR[:, o[0]:o[0] + n]
                o[0] += n
                return v
            NKC = 512 if is_sample else 0
            QKT = arrb(4 * NT).rearrange("p (c t) -> p c t", t=NT)
            GQT = arrb(4 * NT).rearrange("p (c t) -> p c t", t=NT)
            GK2 = arrb(2 * (NT + NKC)).rearrange("p (c t) -> p c t", t=NT + NKC)
            VNA = arrb(nrow * 4 * 64).rearrange("p (r h c) -> p r h c", h=4, c=64)
            VG = arrb((NTT + NKC // 128) * 2 * 64).rearrange("p (t h c) -> p t h c", h=2, c=64)
            tok_off = o[0]
            TOK = arrf(1280)
            TK2 = arrf(768)
            RS = arrf(16)
            PT = [arrb(512) for _ in range(3)] + [MODB[:, 1040:1296].bitcast(BF16)]
            SBs = [arrf(512), MODB[:, 0:512]]
            sbrr = [0]
            RC = MODB[:, 512:1024]
            if is_sample:
                KCT = arrb(2 * 512).rearrange("p (c t) -> p c t", t=512)
                VCN = arrb(4 * 4 * 64).rearrange("p (t h c) -> p t h c", h=4, c=64)
                BR = arrf(960)
            assert o[0] <= 14336, o[0]
            ptrr = [0]
            def ptile():
                i = ptrr[0] % 4
                ptrr[0] += 1
                return PT[i], ("PT", i)

            P.dma("sp", QKN[:], qkn[l:l + 1, :].partition_broadcast(128)[:, 0, :], "c0", writes=["QKN"])
            if is_sample:
                for t in range(4):
                    P.dma("sp", TOK[:, 0:256], cnak[l, t * 128:(t + 1) * 128, :], "ctx", writes=["TOK"])
                    P.op("pool", lambda e: e.tensor_copy(out=TMPB[:, 0:256], in_=TOK[:, 0:256]), reads=["TOK"], writes=["TMPB"])
                    for cc in range(2):
                        P.op("pe", lambda e, cc=cc: e.transpose(out=PST[:, cc * 128:(cc + 1) * 128], in_=TMPB[:, cc * 128:(cc + 1) * 128], identity=IDB[:]), reads=["TMPB", "IDB"], writes=["PST"])
                    P.op("act", lambda e, t=t: e.copy(out=KCT[:, :, t * 128:(t + 1) * 128], in_=PST[:, 0:256].rearrange("p (c t) -> p c t", t=128)), reads=["PST"], writes=["KCT"])
                    P.dma("sp", TOK[:, 256:512], cnav[l, t * 128:(t + 1) * 128, :], "ctx", writes=["TOK2"])
                    P.op("pool", lambda e, t=t: e.tensor_copy(out=VCN[:, t, :, :], in_=TOK[:, 256:512].rearrange("p (h c) -> p h c", c=64)), reads=["TOK2"], writes=["VCN"])
                    P.dma("sp", TOK[:, 512:640], cgk[l, t * 128:(t + 1) * 128, :], "ctx", writes=["TOK3"])
                    for kv in (0, 1):
                        P.op("pool", lambda e, kv=kv: e.tensor_copy(out=TMPB[:, 256 + 128 * kv:384 + 128 * kv].rearrange("p (r c) -> p r c", c=64), in_=TOK[:, 512 + 64 * kv:576 + 64 * kv].unsqueeze(1).to_broadcast([128, 2, 64])), reads=["TOK3"], writes=["TMPB2"])
                    for kv in (0, 1):
                        P.op("pe", lambda e, kv=kv: e.transpose(out=PST[:, 256 + 128 * kv:384 + 128 * kv], in_=TMPB[:, 256 + 128 * kv:384 + 128 * kv], identity=IDB[:]), reads=["TMPB2", "IDB"], writes=["PST2"])
                    P.op("act", lambda e, t=t: e.copy(out=GK2[:, :, t * 128:(t + 1) * 128], in_=PST[:, 256:512].rearrange("p (c t) -> p c t", t=128)), reads=["PST2"], writes=["GKT"])
                    P.dma("sp", TOK[:, 640:768], cgv[l, t * 128:(t + 1) * 128, :], "ctx", writes=["TOK4"])
                    P.op("pool", lambda e, t=t: e.tensor_copy(out=VG[:, t, :, :], in_=TOK[:, 640:768].rearrange("p (h c) -> p h c", c=64)), reads=["TOK4"], writes=["VG"])
                P.barrier()

            SUB = int(os.environ.get("ATT_SUB", "9"))
            if SUB < 1:
                P.barrier(); return
            wv, wk = load_w(w_in[l], 0, 8, 1152, 512)
            for cc in range(4):
                for g0 in range(0, NT, 512):
                    gn = min(512, NT - g0)
                    pt, pk = psum()
                    for k in range(8):
                        P.op("pe", lambda e, pt=pt, k=k, cc=cc, g0=g0, gn=gn: e.matmul(pt[:, 0:gn], lhsT=wv[:, k, cc * 128:(cc + 1) * 128], rhs=XMT[:, k, g0:g0 + gn], start=(k == 0), stop=(k == 7)), reads=[wk, "XMT"], writes=[pk])
                    P.op("act", lambda e, pt=pt, cc=cc, g0=g0, gn=gn: e.copy(out=QKT[:, cc, g0:g0 + gn], in_=pt[:, 0:gn]), reads=[pk], writes=["QKT"])
            if SUB < 2:
                P.barrier(); return
            wa, wak = load_w(w_in[l], 0, 8, 1408, 512)
            wb, wbk = load_w(w_in[l], 0, 8, 1920, 512)
            wc, wck = load_w(w_in[l], 0, 8, 2432, 256)
            LB = [dict(TOK=TOK, TK2=TK2, RS=RS, TMPB=TMPB),
                  dict(TOK=LNB[:, 0:1280], TK2=LNB[:, 1280:2048], RS=MODB[:, 1024:1040], TMPB=TMPF[:].bitcast(BF16)[:, 0:1024])]

            def tile_job(i, tt, li):
                TOK_ = LB[li]["TOK"]; TK2_ = LB[li]["TK2"]; RS_ = LB[li]["RS"]; TMPB_ = LB[li]["TMPB"]
                kx = lambda k: k + "_%d" % li
                pa, pak = psum()
                for k in range(8):
                    P.op("pe", lambda e, k=k: e.matmul(pa[:, 0:512], lhsT=XMT[:, k, i * 128:(i + 1) * 128], rhs=wa[:, k, :], start=(k == 0), stop=(k == 7)), reads=[wak, "XMT"], writes=[pak])
                P.op("act", lambda e: e.copy(out=TOK_[:, 0:512], in_=pa[:, 0:512]), reads=[pak], writes=[kx("TOK")])
                pb, pbk = psum()
                for k in range(8):
                    P.op("pe", lambda e, k=k: e.matmul(pb[:, 0:512], lhsT=XMT[:, k, i * 128:(i + 1) * 128], rhs=wb[:, k, :], start=(k == 0), stop=(k == 7)), reads=[wbk, "XMT"], writes=[pbk])
                P.op("act", lambda e: e.copy(out=TOK_[:, 512:1024], in_=pb[:, 0:512]), reads=[pbk], writes=[kx("TOK2")])
                pc, pck = psum()
                for k in range(8):
                    P.op("pe", lambda e, k=k: e.matmul(pc[:, 0:256], lhsT=XMT[:, k, i * 128:(i + 1) * 128], rhs=wc[:, k, :], start=(k == 0), stop=(k == 7)), reads=[wck, "XMT"], writes=[pck])
                P.op("act", lambda e: e.copy(out=TOK_[:, 1024:1280], in_=pc[:, 0:256]), reads=[pck], writes=[kx("TOK3")])
                for rr in (0, 1):
                    pv_, pvk = psum()
                    for k in range(8):
                        P.op("pe", lambda e, k=k: e.matmul(pv_[0:64, 0:256], lhsT=XMT[:, k, i * 128 + rr * 64:i * 128 + rr * 64 + 64], rhs=wa[:, k, 256:512], start=(k == 0), stop=(k == 7)), reads=[wak, "XMT"], writes=[pvk])
                    P.op("act", lambda e: e.copy(out=VNA[0:64, 2 * i + rr, :, :], in_=pv_[0:64, 0:256].rearrange("p (h c) -> p h c", c=64)), reads=[pvk], writes=[("VNA", 2 * i + rr)])
                yield
                QK = TOK_[:, 512:1152].rearrange("p (h c) -> p h c", c=64)
                T2v = TK2_[:, 0:640].rearrange("p (h c) -> p h c", c=64)
                TK = [kx("TOK2"), kx("TOK3")]
                P.op("dve", lambda e: e.tensor_tensor(out=T2v, in0=QK, in1=QK, op=ALU.mult), reads=TK, writes=[kx("TK2")])
                P.op("dve", lambda e: e.tensor_reduce(out=RS_[:, 0:10], in_=T2v, axis=mybir.AxisListType.X, op=ALU.add), reads=[kx("TK2")], writes=[kx("RS")])
                yield
                P.op("act", lambda e: e.activation(out=RS_[:, 0:10], in_=RS_[:, 0:10], func=AF.Sqrt, bias=CST2[:, 1600:1601], scale=1.0 / 64.0), reads=[kx("RS"), "CST2"], writes=[kx("RS")])
                yield
                P.op("dve", lambda e: e.reciprocal(out=RS_[:, 0:10], in_=RS_[:, 0:10]), reads=[kx("RS")], writes=[kx("RS")])
                yield
                P.op("dve", lambda e: e.tensor_tensor(out=QK, in0=QK, in1=RS_[:, 0:10].unsqueeze(2).to_broadcast([128, 10, 64]), op=ALU.mult), reads=[kx("RS")] + TK, writes=TK)
                P.op("dve", lambda e: e.tensor_tensor(out=QK[:, 0:8, :], in0=QK[:, 0:8, :], in1=QKN[:, 0:64].unsqueeze(1).to_broadcast([128, 8, 64]), op=ALU.mult), reads=["QKN"] + TK, writes=TK)
                P.op("dve", lambda e: e.tensor_tensor(out=QK[:, 8:10, :], in0=QK[:, 8:10, :], in1=QKN[:, 64:128].unsqueeze(1).to_broadcast([128, 2, 64]), op=ALU.mult), reads=["QKN"] + TK, writes=TK)
                yield
                if not is_sample:
                    r0 = i * 128
                    P.dma("sp", nak[l, r0:r0 + 128, :], TOK_[:, 0:256], "oc", reads=[kx("TOK")])
                    P.dma("sp", nav[l, r0:r0 + 128, :], TOK_[:, 256:512], "oc", reads=[kx("TOK")])
                    P.dma("sp", ngk[l, r0:r0 + 128, :], TOK_[:, 1024:1152], "oc", reads=[kx("TOK3")])
                    P.dma("sp", ngv[l, r0:r0 + 128, :], TOK_[:, 1152:1280], "oc", reads=[kx("TOK3")])
                else:
                    Q4 = TOK_[:, 512:1152].rearrange("p (h a c) -> p h a c", a=4, c=16)
                    T4 = TK2_[:, 0:640].rearrange("p (h a c) -> p h a c", a=4, c=16)
                    rp = ROPE[:, i, :]
                    for ax in (0, 1):
                        cosb = rp[:, 16 * ax:16 * ax + 16].unsqueeze(1).to_broadcast([128, 10, 16])
                        sinb = rp[:, 32 + 16 * ax:48 + 16 * ax].unsqueeze(1).to_broadcast([128, 10, 16])
                        x1 = Q4[:, :, 2 * ax, :]; x2 = Q4[:, :, 2 * ax + 1, :]
                        t1 = T4[:, :, 0, :]; t2 = T4[:, :, 1, :]
                        P.op("dve", lambda e: e.tensor_tensor(out=t1, in0=x1, in1=sinb, op=ALU.mult), reads=TK + ["CST2"], writes=[kx("TK2")])
                        P.op("dve", lambda e: e.tensor_tensor(out=t2, in0=x2, in1=sinb, op=ALU.mult), reads=TK + ["CST2"], writes=[kx("TK2")])
                        yield
                        P.op("dve", lambda e: e.tensor_tensor(out=x1, in0=x1, in1=cosb, op=ALU.mult), reads=TK + ["CST2", kx("TK2")], writes=TK)
                        P.op("dve", lambda e: e.tensor_tensor(out=x2, in0=x2, in1=cosb, op=ALU.mult), reads=TK + ["CST2", kx("TK2")], writes=TK)
                        yield
                        P.op("dve", lambda e: e.tensor_tensor(out=x1, in0=x1, in1=t2, op=ALU.subtract), reads=TK + [kx("TK2")], writes=TK)
                        P.op("dve", lambda e: e.tensor_tensor(out=x2, in0=x2, in1=t1, op=ALU.add), reads=TK + [kx("TK2")], writes=TK)
                        yield
                P.op("pool", lambda e: e.tensor_copy(out=TMPB_[:, 0:512], in_=TOK_[:, 512:1024]), reads=TK, writes=[kx("TMPB")])
                for kv in (0, 1):
                    P.op("pool", lambda e, kv=kv: e.tensor_copy(out=TMPB_[:, 512 + 128 * kv:640 + 128 * kv].rearrange("p (r c) -> p r c", c=64), in_=TOK_[:, 1024 + 64 * kv:1088 + 64 * kv].unsqueeze(1).to_broadcast([128, 2, 64])), reads=TK, writes=[kx("TMPB")])
                P.op("pool", lambda e: e.tensor_copy(out=VG[:, NKC // 128 + i, :, :], in_=TOK_[:, 1152:1280].rearrange("p (h c) -> p h c", c=64)), reads=[kx("TOK3")], writes=[("VG", i)])
                yield
                for cc in range(6):
                    P.op("pe", lambda e, cc=cc: e.transpose(out=PST[:, cc * 128:(cc + 1) * 128], in_=TMPB_[:, cc * 128:(cc + 1) * 128], identity=IDB[:]), reads=[kx("TMPB"), "IDB"], writes=["PST"])
                P.op("act", lambda e: e.copy(out=GQT[:, :, i * 128:(i + 1) * 128], in_=PST[:, 0:512].rearrange("p (c t) -> p c t", t=128)), reads=["PST"], writes=[("GQT", i)])
                P.op("act", lambda e: e.copy(out=GK2[:, :, NKC + i * 128:NKC + (i + 1) * 128], in_=PST[:, 512:768].rearrange("p (c t) -> p c t", t=128)), reads=["PST"], writes=[("GKT", i)])
                yield

            def tl_stream(li):
                for i, tt in enumerate(tiles):
                    if i % 2 == li:
                        yield from tile_job(i, tt, li)
            gens = [tl_stream(0), tl_stream(1)]
            while gens:
                for g in list(gens):
                    try:
                        next(g)
                    except StopIteration:
                        gens.remove(g)
            for base, cnt_ in (("GQT", NTT), ("GKT", NTT), ("VG", NTT), ("VNA", 2 * NTT)):
                evs = list(P.st.get(base, [[], []])[0])
                for i_ in range(cnt_):
                    evs += list(P.st.get((base, i_), [[], []])[0])
                P.st[base] = [evs, []]
            if SUB < 3:
                P.barrier(); return
            PRE["wout"] = (load_w(w_out[l], 0, 8, 0, 512), load_w(w_out[l], 0, 8, 512, 512))
            nvt = NTT + NKC // 128
            VGA = AR[:, tok_off:tok_off + nvt * 128].bitcast(BF16).rearrange("p (t h c) -> p t h c", h=2, c=128)
            tl0 = ["TOK_0", "TOK2_0", "TOK3_0", "TK2_0"]
            P.op("pool", lambda e: e.memset(VGA[:, :, :, 64:128], 1.0), writes=["VGA1"] + tl0)
            P.op("act", lambda e: e.copy(out=VGA[:, :, :, 0:64], in_=VG), reads=["VG"], writes=["VGA"] + tl0)
            def finish_head(acc, acck, acs, acsk, chunk, half, q0, qn):
                P.op("dve", lambda e: e.reciprocal(out=RC[0:64, 0:qn], in_=acs[0:64, 0:qn]), reads=[acsk], writes=["RC"])
                P.op("dve", lambda e: e.tensor_tensor(out=OCT[64 * half:64 * half + 64, chunk, q0:q0 + qn], in0=acc[0:64, 0:qn], in1=RC[0:64, 0:qn], op=ALU.mult), reads=[acck, "RC"], writes=["OCT"])

            def score_block(qT, kT, nk, qn, bias=None):
                ps_, psk = psum()
                P.op("pe", lambda e: e.matmul(ps_[0:nk, 0:qn], lhsT=kT, rhs=qT, start=True, stop=True), reads=["QKT", "GQT", "GKT", "KCT"], writes=[psk])
                pt_, ptk = ptile()
                if bias is None:
                    P.op("act", lambda e: e.activation(out=pt_[0:nk, 0:qn], in_=ps_[0:nk, 0:qn], func=AF.Exp, scale=0.125), reads=[psk], writes=[ptk])
                else:
                    si = sbrr[0] % 2
                    sbrr[0] += 1
                    SB = SBs[si]
                    P.op("dve", lambda e: e.scalar_tensor_tensor(out=SB[0:nk, 0:qn], in0=ps_[0:nk, 0:qn], scalar=0.125, in1=bias, op0=ALU.mult, op1=ALU.add), reads=[psk, "BR"], writes=[("SB", si)])
                    P.op("act", lambda e: e.activation(out=pt_[0:nk, 0:qn], in_=SB[0:nk, 0:qn], func=AF.Exp), reads=[("SB", si)], writes=[ptk])
                return pt_, ptk

            VK = ["VNA", "VG", "VCN", "ONESB"]

            def pv(acc, acck, acs, acsk, vT, nk, pt_, ptk, c0, n, first):
                P.op("pe", lambda e: e.matmul(acc[0:64, c0:c0 + n], lhsT=vT, rhs=pt_[0:nk, 0:n], start=first, stop=False), reads=VK + [ptk], writes=[acck])
                P.op("pe", lambda e: e.matmul(acs[0:64, c0:c0 + n], lhsT=ONESB[0:nk, :], rhs=pt_[0:nk, 0:n], start=first, stop=False), reads=VK + [ptk], writes=[acsk])

            if is_sample:
                qblocks = [(0, 512, 0, 16), (512, 512, 0, 16)]
            else:
                qblocks = [(off, NTs, off // 64, (off + NTs) // 64) for (off, NTs, _s, _i) in pass_seqs]
            def pv_aug(acc, acck, vT, nk, pt_, ptk, c0, n, first):
                P.op("pe", lambda e: e.matmul(acc[:, c0:c0 + n], lhsT=vT, rhs=pt_[0:nk, 0:n], start=first, stop=False), reads=["VGA", "VGA1", ptk], writes=[acck])

            def finish_head_aug(acc, acck, chunk, half, q0, qn):
                P.op("dve", lambda e: e.reciprocal(out=RC[64:128, 0:qn], in_=acc[64:128, 0:qn]), reads=[acck], writes=["RC"])
                P.op("dve", lambda e: e.tensor_tensor(out=OCT[64 * half:64 * half + 64, chunk, q0:q0 + qn], in0=acc[0:64, 0:qn], in1=RC[64:128, 0:qn], op=ALU.mult), reads=[acck, "RC"], writes=["OCT"])

            def run_blocks(blocks, acc, acck, acs, acsk, aug=False):
                sc = {}
                LA = 3
                for i0_ in range(min(LA, len(blocks))):
                    b = blocks[i0_]
                    sc[i0_] = score_block(b[0], b[1], b[2], b[3], bias=b[4])
                for i, b in enumerate(blocks):
                    if i + LA < len(blocks):
                        nb = blocks[i + LA]
                        sc[i + LA] = score_block(nb[0], nb[1], nb[2], nb[3], bias=nb[4])
                    pt_, ptk = sc.pop(i)
                    if aug:
                        pv_aug(acc, acck, b[5], b[2], pt_, ptk, b[6], b[3], i == 0)
                    else:
                        pv(acc, acck, acs, acsk, b[5], b[2], pt_, ptk, b[6], b[3], i == 0)

            for h in range(4):
                hb = 64 * (h % 2)
                qch = h // 2
                kch = 2 + h // 2
                if is_sample:
                    P.dma("sp", BR[0:64, :], nab[l, h], "ctx", writes=["BR"])
                    P.op("dve", lambda e: e.tensor_tensor(out=BR[0:64, :].rearrange("p (a c) -> p a c", c=64), in0=BR[0:64, :].rearrange("p (a c) -> p a c", c=64), in1=COLM.unsqueeze(1).to_broadcast([64, 15, 64]), op=ALU.add), reads=["BR", "CST2"], writes=["BR"])
                for (q0, qn, kr0, kr1) in qblocks:
                    acc, acck = psacc(0)
                    acs, acsk = psacc(1)
                    qT = QKT[hb:hb + 64, qch, q0:q0 + qn]
                    blocks = []
                    if is_sample:
                        for t in range(4):
                            blocks.append((qT, KCT[hb:hb + 64, h // 2, t * 128:(t + 1) * 128], 128, qn, None, VCN[:, t, h, :], 0))
                        rows = range(q0 // 64, (q0 + qn) // 64)
                        for j in range(16):
                            att = [r for r in rows if min(max(r - 4, 0), 8) <= j <= min(max(r - 4, 0), 8) + 7]
                            if not att:
                                continue
                            rlo, rhi = att[0], att[-1]
                            nr = rhi - rlo + 1
                            e0 = rlo - j + 7
                            c0 = (rlo - q0 // 64) * 64
                            blocks.append((QKT[hb:hb + 64, qch, rlo * 64:(rhi + 1) * 64], QKT[hb:hb + 64, kch, j * 64:(j + 1) * 64], 64, nr * 64,
                                           BR[0:64, e0 * 64:(e0 + nr) * 64], VNA[0:64, j, h, :], c0))
                    else:
                        for j in range(kr0, kr1):
                            blocks.append((qT, QKT[hb:hb + 64, kch, j * 64:(j + 1) * 64], 64, qn, None, VNA[0:64, j, h, :], 0))
                    run_blocks(blocks, acc, acck, acs, acsk)
                    finish_head(acc, acck, acs, acsk, 2 + h // 2, h % 2, q0, qn)
            for h in range(8):
                hb = 64 * (h % 2)
                kvh = h // 4
                for (q0, qn, kr0, kr1) in qblocks:
                    acc, acck = psacc(0)
                    acs, acsk = psacc(1)
                    qT = GQT[hb:hb + 64, h // 2, q0:q0 + qn]
                    kts = list(range(NKC // 128)) + [NKC // 128 + t for t in range(kr0 // 2, kr1 // 2)]
                    blocks = [(qT, GK2[hb:hb + 64, kvh, t * 128:(t + 1) * 128], 128, qn, None, VGA[:, t, kvh, :], 0) for t in kts]
                    run_blocks(blocks, acc, acck, acs, acsk, aug=True)
                    finish_head_aug(acc, acck, 4 + h // 2, h % 2, q0, qn)
            P.barrier()

        def dense_out(l, tiles, NT, ci):
            load_mod(l, ci, 2048, 1024, MODB[:, 0:1024])
            load_ln(l, ln1w, ln1b)
            if "wout" in PRE:
                (wv0, wk0), (wv1, wk1) = PRE.pop("wout")
            else:
                wv0, wk0 = load_w(w_out[l], 0, 8, 0, 512)
                wv1, wk1 = load_w(w_out[l], 0, 8, 512, 512)
            for i, tt in enumerate(tiles):
                for (wv, wk, n0) in ((wv0, wk0, 0), (wv1, wk1, 512)):
                    pt, pk = psum()
                    for k in range(8):
                        P.op("pe", lambda e, k=k, i=i, pt=pt, wv=wv: e.matmul(pt[:, 0:512], lhsT=OCT[:, k, i * 128:(i + 1) * 128], rhs=wv[:, k, :], start=(k == 0), stop=(k == 7)), reads=[wk, "OCT"], writes=[pk])
                    P.op("dve", lambda e, pt=pt, n0=n0: e.tensor_tensor(out=TMPF[:, n0:n0 + 512], in0=pt[:, 0:512], in1=MODB[:, n0:n0 + 512], op=ALU.mult), reads=[pk, "MODB"], writes=["TMPF"])
                    P.op("dve", lambda e, tt=tt, n0=n0: e.scalar_tensor_tensor(out=X[:, tt, n0:n0 + 512], in0=X[:, tt, n0:n0 + 512], scalar=ALPHA, in1=TMPF[:, n0:n0 + 512], op0=ALU.mult, op1=ALU.add), reads=["TMPF", ("X", tt)], writes=[("X", tt)])
                layer_norm(tt, l)

        def ffn(l, tiles, NT, ci):
            load_mod(l, ci, 3072, 2048, MODB[:, 0:2048])
            pre_g = load_w(w_fi[l], 0, 8, 0, 512)
            pre_u = load_w(w_fi[l], 0, 8, DFF, 512)
            modulate_transpose(tiles, MODB[:, 0:1024], MODB[:, 1024:2048])
            load_mod(l, ci, 5120, 1024, MODB[:, 0:1024])
            load_ln(l, ln2w, ln2b)
            HT = AR[:, 0:11 * NT // 2].bitcast(BF16).rearrange("p (j t) -> p j t", t=NT)
            GS = AR[:, 6000:6512]
            for half in range(2):
                for jb in range(3):
                    j0 = half * 11 + jb * 4
                    nj = min(4, half * 11 + 11 - j0)
                    if half == 0 and jb == 0:
                        (wg, wgk), (wu, wuk) = pre_g, pre_u
                    else:
                        wg, wgk = load_w(w_fi[l], 0, 8, j0 * 128, nj * 128)
                        wu, wuk = load_w(w_fi[l], 0, 8, DFF + j0 * 128, nj * 128)
                    for jj in range(nj):
                        for g0 in range(0, NT, 512):
                            gn = min(512, NT - g0)
                            pg, pgk = psum()
                            for k in range(8):
                                P.op("pe", lambda e, k=k, pg=pg, jj=jj, g0=g0, gn=gn: e.matmul(pg[:, 0:gn], lhsT=wg[:, k, jj * 128:(jj + 1) * 128], rhs=XMT[:, k, g0:g0 + gn], start=(k == 0), stop=(k == 7)), reads=[wgk, "XMT"], writes=[pgk])
                            pu, puk = psum()
                            for k in range(8):
                                P.op("pe", lambda e, k=k, pu=pu, jj=jj, g0=g0, gn=gn: e.matmul(pu[:, 0:gn], lhsT=wu[:, k, jj * 128:(jj + 1) * 128], rhs=XMT[:, k, g0:g0 + gn], start=(k == 0), stop=(k == 7)), reads=[wuk, "XMT"], writes=[puk])
                            P.op("act", lambda e, pg=pg, gn=gn: e.activation(out=GS[:, 0:gn], in_=pg[:, 0:gn], func=AF.Silu), reads=[pgk], writes=["GS"])
                            P.op("dve", lambda e, pu=pu, gn=gn, g0=g0, jj=jj, jb=jb: e.tensor_tensor(out=HT[:, jb * 4 + jj, g0:g0 + gn], in0=GS[:, 0:gn], in1=pu[:, 0:gn], op=ALU.mult), reads=[puk, "GS"], writes=["HT"])
                for n0 in (0, 512):
                    wa_, wak_ = load_w(w_fo[l], half * 11, 8, n0, 512)
                    wb_, wbk_ = load_w(w_fo[l], half * 11 + 8, 3, n0, 512)
                    for i, tt in enumerate(tiles):
                        pt, pk = psum()
                        for j in range(11):
                            wv, wk, jj = (wa_, wak_, j) if j < 8 else (wb_, wbk_, j - 8)
                            P.op("pe", lambda e, j=j, jj=jj, wv=wv, pt=pt, i=i: e.matmul(pt[:, 0:512], lhsT=HT[:, j, i * 128:(i + 1) * 128], rhs=wv[:, jj, :], start=(j == 0), stop=(j == 10)), reads=[wk, "HT"], writes=[pk])
                        P.op("dve", lambda e, pt=pt, n0=n0: e.tensor_tensor(out=TMPF[:, n0:n0 + 512], in0=pt[:, 0:512], in1=MODB[:, n0:n0 + 512], op=ALU.mult), reads=[pk, "MODB"], writes=["TMPF"])
                        if half == 0:
                            P.op("dve", lambda e, tt=tt, n0=n0: e.scalar_tensor_tensor(out=X[:, tt, n0:n0 + 512], in0=X[:, tt, n0:n0 + 512], scalar=ALPHA, in1=TMPF[:, n0:n0 + 512], op0=ALU.mult, op1=ALU.add), reads=["TMPF", ("X", tt)], writes=[("X", tt)])
                        else:
                            P.op("dve", lambda e, tt=tt, n0=n0: e.tensor_tensor(out=X[:, tt, n0:n0 + 512], in0=X[:, tt, n0:n0 + 512], in1=TMPF[:, n0:n0 + 512], op=ALU.add), reads=["TMPF", ("X", tt)], writes=[("X", tt)])
            for tt in tiles:
                layer_norm(tt, l)

        seqs = [([0, 1, 2, 3], 0, False, [(0, 256, False, 0), (256, 256, False, 1)]), (list(range(4, 12)), 1, True, [(0, 1024, True, 0)])]
        if os.environ.get("SEQS"):
            seqs = [seqs[int(c)] for c in os.environ["SEQS"]]
        for l in range(depth):
            P.barrier()
            P.dma("sp", W2T[:, 0, :], w2[l], "c0", writes=["W2T"])
            P.dma("sp", A2T[:, 0, :], a2[l], "c0", writes=["A2T"])
            P.dma("sp", G2[:, 0, :], g2[l], "c0", writes=["G2"])
            for (tiles, ci, is_sample, pass_seqs) in seqs:
                NT = 128 * len(tiles)
                if stage < 1:
                    continue
                load_mod(l, ci, 0, 2048, MODB[:, 0:2048])
                if stage >= 2:
                    WR_ = WS[:].rearrange("p s c -> p (s c)")[:, 0:8 * 1152].rearrange("p (k c) -> p k c", c=1152)
                    P.dma("pool", WR_, w_in[l].rearrange("(k p) c -> p k c", p=128)[:, :, 0:1152], "ws0", writes=[("ws", 0), ("ws", 1), ("ws", 2)])
                modulate_transpose(tiles, MODB[:, 0:1024], MODB[:, 1024:2048])
                P.barrier()
                if stage >= 2:
                    rwkv_phase(l, NT, pass_seqs, bg=(mods_gen(l + 1) if (is_sample and l + 1 < depth) else None))
                if stage >= 3:
                    attn_phase(l, tiles, NT, is_sample, pass_seqs)
                if stage >= 4:
                    dense_out(l, tiles, NT, ci)
                if stage >= 5:
                    ffn(l, tiles, NT, ci)
        for t in range(12):
            P.dma("sp", y[t * 128:(t + 1) * 128, :], X[:, t, :], "yo", reads=[("X", t)])
        P.barrier()
        P.emit()
    return nc


def _consts():
    cst = np.zeros((128, 128 * 12), np.float32)
    cst[:, 0:128] = np.eye(128, dtype=np.float32)
    bd = np.zeros((128, 128), np.float32)
    bd[0:64, 0:64] = 1.0; bd[64:128, 64:128] = 1.0
    cst[:, 128:256] = bd / 64.0
    i = np.arange(64)[:, None]; j = np.arange(64)[None, :]
    SL = (i > j).astype(np.float32); SU = (i < j).astype(np.float32)
    LI = (i >= j).astype(np.float32); UI = (i <= j).astype(np.float32)
    def blk(m):
        o = np.zeros((128, 128), np.float32); o[0:64, 0:64] = m; o[64:, 64:] = m; return o
    per = {0: [SL, SU, SU, UI, UI], 1: [SU, SL, SL, LI, LI]}
    for d in (0, 1):
        for k, m in enumerate(per[d]):
            cst[:, 256 + (d * 5 + k) * 128: 256 + (d * 5 + k + 1) * 128] = blk(m)
    cst2 = np.zeros((128, 1603), np.float32)
    rm = np.ones(1024, np.float32); rm[::64] = 0.0
    cst2[:, 0:1024] = rm[None, :]
    t = np.arange(1024)
    inv = 10000.0 ** (-np.arange(16, dtype=np.float32) / 16.0)
    angr = (t // 64).astype(np.float32)[:, None] * inv[None, :]
    angc = (t % 64).astype(np.float32)[:, None] * inv[None, :]
    tab = np.concatenate([np.cos(angr), np.cos(angc), np.sin(angr), np.sin(angc)], 1).astype(np.float32)
    cst2[:, 1024:1536] = tab.reshape(8, 128, 64).transpose(1, 0, 2).reshape(128, 512)
    cq = np.arange(64)[None, :]; ck = np.arange(64)[:, None]
    c0 = np.clip(cq - 8, 0, 48)
    inwin = (ck >= c0) & (ck < c0 + 16)
    cst2[0:64, 1536:1600] = np.where(inwin, 0.0, NEG).astype(np.float32)
    cst2[:, 1600] = RMS_EPS; cst2[:, 1601] = GN_EPS; cst2[:, 1602] = LN_EPS
    return cst, cst2


_NC_CACHE = {}


def _prep(inp):
    f = lambda k: np.ascontiguousarray(np.asarray(inp[k], dtype=np.float32))
    cst, cst2 = _consts()
    x_prompt = f("x_prompt"); x_sample = f("x_sample")
    conv = f("rwkv_conv"); w0 = f("rwkv_w0"); a0 = f("rwkv_a0")
    kk_ = f("rwkv_k_k"); ka_ = f("rwkv_k_a"); rk_ = f("rwkv_r_k").reshape(NL, 256)
    lw_ = f("rwkv_lnx_w"); lb_ = f("rwkv_lnx_b")
    pp = np.zeros((128, NL * PPL), np.float32)
    for l in range(NL):
        b = l * PPL
        pp[:, b:b + 27] = conv[l].reshape(3, 9, 128).transpose(2, 0, 1).reshape(128, 27)
        pp[:, b + 27:b + 31] = w0[l].reshape(2, 2, 128).transpose(2, 0, 1).reshape(128, 4)
        pp[:, b + 31:b + 35] = a0[l].reshape(2, 2, 128).transpose(2, 0, 1).reshape(128, 4)
        for j, arr in enumerate((kk_, ka_, rk_, lw_, lb_)):
            pp[:, b + 35 + 2 * j:b + 37 + 2 * j] = arr[l].reshape(2, 128).T
    rpb = f("na_rpb")
    ck = np.arange(64)[:, None]; cq = np.arange(64)[None, :]
    dc = np.clip(ck - cq, -15, 15) + 15
    e = np.arange(15)
    nab = rpb[:, :, 14 - e][:, :, :, dc]
    nab = np.ascontiguousarray(nab.transpose(0, 1, 3, 2, 4)).reshape(NL, 4, 64, 15 * 64)
    qkn = np.concatenate([f("gqa_q_norm"), f("gqa_k_norm")], 1)
    c = f("c"); c_ctx = f("c_ctx")
    st = f("state_rwkv")
    shared = {
        "w_mod": f("w_mod"), "b_mod": f("b_mod"), "w_in": f("w_in"), "w_out": f("w_out"),
        "w_ffn_in": f("w_ffn_in"), "w_ffn_out": f("w_ffn_out"), "pp": pp,
        "rwkv_w2": f("rwkv_w2").reshape(NL, 128, 256), "rwkv_a2": f("rwkv_a2").reshape(NL, 128, 256), "rwkv_g2": f("rwkv_g2"),
        "ln1_w": f("ln1_w"), "ln1_b": f("ln1_b"), "ln2_w": f("ln2_w"), "ln2_b": f("ln2_b"),
        "qkn": qkn, "nab": nab, "cst": cst, "cst2": cst2,
    }
    in_maps = []
    for i in range(8):
        b = i // 4
        m = dict(shared)
        m["xin"] = np.concatenate([x_prompt[2 * i], x_prompt[2 * i + 1], x_sample[b]], 0)
        cc = np.stack([c_ctx, c[b]], 1)
        m["c2"] = np.ascontiguousarray(cc.reshape(8, 128, 2).transpose(1, 0, 2).reshape(128, 16))
        s = st[b].transpose(0, 1, 2, 4, 3)
        m["stT"] = np.ascontiguousarray(s.reshape(NL, 2, 2, 128, 64))
        m["cnak"] = np.ascontiguousarray(f("cache_na_k")[b].reshape(NL, PAST, 256))
        m["cnav"] = np.ascontiguousarray(f("cache_na_v")[b].reshape(NL, PAST, 256))
        m["cgk"] = np.ascontiguousarray(f("cache_gqa_k")[b].reshape(NL, PAST, 128))
        m["cgv"] = np.ascontiguousarray(f("cache_gqa_v")[b].reshape(NL, PAST, 128))
        in_maps.append(m)
    return in_maps


def kernel(**inp):
    depth = NL
    if depth not in _NC_CACHE:
        _NC_CACHE[depth] = build(depth)
    nc = _NC_CACHE[depth]
    in_maps = _prep(inp)
    res = run_bass_kernel_spmd(nc, in_maps, core_ids=list(range(8)))
    R = res.results
    y_prompt = np.stack([R[i // 2]["y"][(i % 2) * 256:(i % 2) * 256 + 256] for i in range(16)], 0)
    y_sample = np.stack([R[0]["y"][512:], R[4]["y"][512:]], 0)
    nst = np.stack([R[i // 2]["nst"][:, i % 2] for i in range(16)], 0)
    def cache(name, hh):
        return np.stack([R[i // 2][name][:, (i % 2) * 256:(i % 2) * 256 + 256].reshape(NL, 256, hh, 64) for i in range(16)], 0)
    return (y_prompt.astype(np.float32), y_sample.astype(np.float32), nst.astype(np.float32),
            cache("nak", 4), cache("nav", 4), cache("ngk", 2), cache("ngv", 2))
```
